# Optimizing a Trainium2 kernel written in Bass

```python
import jax, jax.numpy as jnp
from jax import lax
import numpy as np

D_MODEL = 1024
BATCH = 4
SEQ = 4096
DEPTH = 2

GRID_W = 64
CTX_LEN = 256
POOL_WIDTH = 512
POOL_GROUPS = 4
POOL_GROUP_DIM = POOL_WIDTH // POOL_GROUPS
POOL_WINDOWS = (2, 4, 8, 16)
N_HEADS = 8
N_KV_HEADS = 2
HEAD_DIM = 64
Q_GROUP = N_HEADS // N_KV_HEADS
ATTN_WIDTH = N_HEADS * HEAD_DIM
KV_WIDTH = N_KV_HEADS * HEAD_DIM
MIX_WIDTH = POOL_WIDTH + ATTN_WIDTH
PROJ_WIDTH = POOL_WIDTH + ATTN_WIDTH + 2 * KV_WIDTH
WINDOW = 128
BLOCK = 128
ROPE_BASE = 10000.0
ROPE_AXIS_DIM = HEAD_DIM // 2
D_FF = 2816
N_MOD = 9
EPS = 1e-6
NEG_INF = -1e30

kernel_name = "hybrid_pool_swa_macaron_dit_block"


def rmsnorm(x, g):
    xf = x.astype(jnp.float32)
    y = xf * lax.rsqrt(jnp.mean(xf * xf, axis=-1, keepdims=True) + EPS)
    return (y * g.astype(jnp.float32)).astype(x.dtype)


def norm_modulate(x, g, shift, scale):
    return rmsnorm(x, g) * (1 + scale) + shift


def swiglu(n, w_in, w_out):
    a, b = jnp.split(n @ w_in, 2, axis=-1)
    return (jax.nn.silu(a) * b) @ w_out


def axial_rope_tables(T):
    rows = T // GRID_W
    row = jnp.repeat(jnp.arange(rows), GRID_W).astype(jnp.float32)
    col = jnp.tile(jnp.arange(GRID_W), rows).astype(jnp.float32)
    inv = ROPE_BASE ** (-jnp.arange(0, ROPE_AXIS_DIM, 2, dtype=jnp.float32) / ROPE_AXIS_DIM)
    ang = jnp.concatenate([row[:, None] * inv, col[:, None] * inv], axis=-1)
    return jnp.cos(ang), jnp.sin(ang)


def apply_rope(x, cos, sin):
    xf = x.astype(jnp.float32)
    x1, x2 = xf[..., :HEAD_DIM // 2], xf[..., HEAD_DIM // 2:]
    c, s = cos[None, :, None, :], sin[None, :, None, :]
    return jnp.concatenate([x1 * c - x2 * s, x2 * c + x1 * s], axis=-1).astype(x.dtype)


def pool_mixer(u, w_pool, pool_scale):
    B, T, _ = u.shape
    uf = u.astype(jnp.float32)
    cs = jnp.pad(jnp.cumsum(uf, axis=1), ((0, 0), (1, 0), (0, 0)))
    t = jnp.arange(T)
    outs = []
    for g, w in enumerate(POOL_WINDOWS):
        lo = jnp.clip(t - w // 2, 0, T)
        hi = jnp.clip(t + w - w // 2, 0, T)
        csg = cs[..., g * POOL_GROUP_DIM:(g + 1) * POOL_GROUP_DIM]
        outs.append((csg[:, hi] - csg[:, lo]) / (hi - lo).astype(jnp.float32)[None, :, None])
    pooled = (jnp.concatenate(outs, axis=-1) - uf).astype(u.dtype)
    pooled = pooled.reshape(B, T, POOL_GROUPS, POOL_GROUP_DIM)
    mixed = jnp.einsum("btgc,gcd->btgd", pooled, w_pool).reshape(B, T, POOL_WIDTH)
    return mixed * pool_scale


def latent_attention(q, k, v, kc, vc, sink):
    B, T = q.shape[:2]
    nb = T // BLOCK
    scale = HEAD_DIM ** -0.5
    qb = q.reshape(B, nb, BLOCK, N_KV_HEADS, Q_GROUP, HEAD_DIM)
    pad = ((0, 0), (BLOCK, BLOCK), (0, 0), (0, 0))
    kp = jnp.pad(k, pad).reshape(B, nb + 2, BLOCK, N_KV_HEADS, HEAD_DIM)
    vp = jnp.pad(v, pad).reshape(B, nb + 2, BLOCK, N_KV_HEADS, HEAD_DIM)
    kb = jnp.concatenate([kp[:, :-2], kp[:, 1:-1], kp[:, 2:]], axis=2)
    vb = jnp.concatenate([vp[:, :-2], vp[:, 1:-1], vp[:, 2:]], axis=2)
    s_loc = jnp.einsum("bnqhgd,bnkhd->bnhgqk", qb, kb).astype(jnp.float32) * scale
    n_idx = jnp.arange(nb)[:, None, None]
    qpos = n_idx * BLOCK + jnp.arange(BLOCK)[None, :, None]
    kpos = n_idx * BLOCK + jnp.arange(3 * BLOCK)[None, None, :] - BLOCK
    valid = (kpos >= 0) & (kpos < T) & (jnp.abs(kpos - qpos) <= WINDOW)
    s_loc = jnp.where(valid[None, :, None, None], s_loc, NEG_INF)
    s_ctx = jnp.einsum("bnqhgd,bmhd->bnhgqm", qb, kc).astype(jnp.float32) * scale
    s_sink = jnp.broadcast_to(sink.astype(jnp.float32).reshape(1, 1, N_KV_HEADS, Q_GROUP, 1, 1),
                              s_loc.shape[:-1] + (1,))
    p = jax.nn.softmax(jnp.concatenate([s_loc, s_ctx, s_sink], axis=-1), axis=-1)
    L = kc.shape[1]
    p_loc = p[..., :3 * BLOCK].astype(v.dtype)
    p_ctx = p[..., 3 * BLOCK:3 * BLOCK + L].astype(v.dtype)
    o = jnp.einsum("bnhgqk,bnkhd->bnqhgd", p_loc, vb) + jnp.einsum("bnhgqm,bmhd->bnqhgd", p_ctx, vc)
    return o.reshape(B, T, ATTN_WIDTH)


def context_attention(qc, kc, vc, sink):
    B, L = qc.shape[:2]
    qg = qc.reshape(B, L, N_KV_HEADS, Q_GROUP, HEAD_DIM)
    s = jnp.einsum("blhgd,bmhd->bhglm", qg, kc).astype(jnp.float32) * HEAD_DIM ** -0.5
    s_sink = jnp.broadcast_to(sink.astype(jnp.float32).reshape(1, N_KV_HEADS, Q_GROUP, 1, 1),
                              s.shape[:-1] + (1,))
    p = jax.nn.softmax(jnp.concatenate([s, s_sink], axis=-1), axis=-1)[..., :L].astype(vc.dtype)
    return jnp.einsum("bhglm,bmhd->blhgd", p, vc).reshape(B, L, ATTN_WIDTH)


def context_kv(nc, w_in):
    B, L = nc.shape[:2]
    kc, vc = jnp.split(nc @ w_in[:, MIX_WIDTH:], 2, axis=-1)
    return (kc.reshape(B, L, N_KV_HEADS, HEAD_DIM), vc.reshape(B, L, N_KV_HEADS, HEAD_DIM))


def mix_latent(n, kc, vc, w_in, w_pool, pool_scale, sink, w_out, cos, sin):
    B, T = n.shape[:2]
    u, q, k, v = jnp.split(n @ w_in, [POOL_WIDTH, MIX_WIDTH, MIX_WIDTH + KV_WIDTH], axis=-1)
    pool_out = pool_mixer(u, w_pool, pool_scale)
    q = apply_rope(q.reshape(B, T, N_HEADS, HEAD_DIM), cos, sin)
    k = apply_rope(k.reshape(B, T, N_KV_HEADS, HEAD_DIM), cos, sin)
    v = v.reshape(B, T, N_KV_HEADS, HEAD_DIM)
    attn_out = latent_attention(q, k, v, kc, vc, sink)
    return jnp.concatenate([pool_out, attn_out], axis=-1) @ w_out


def mix_context(nc, kc, vc, w_in, w_pool, pool_scale, sink, w_out):
    B, L = nc.shape[:2]
    u, q = jnp.split(nc @ w_in[:, :MIX_WIDTH], [POOL_WIDTH], axis=-1)
    pool_out = pool_mixer(u, w_pool, pool_scale)
    attn_out = context_attention(q.reshape(B, L, N_HEADS, HEAD_DIM), kc, vc, sink)
    return jnp.concatenate([pool_out, attn_out], axis=-1) @ w_out


def setup_inputs(seed: int = 0) -> dict:
    key = jax.random.key(seed)
    ks = jax.random.split(key, 24)
    f32 = jnp.float32
    nrm = lambda k, shape, s: jax.random.normal(k, shape, f32) * s
    gain = lambda k, shape: 1.0 + 0.1 * jax.random.normal(k, shape, f32)
    return {
        "x": nrm(ks[0], (BATCH, SEQ, D_MODEL), 1.0),
        "c": nrm(ks[1], (BATCH, D_MODEL), 1.0),
        "ctx": nrm(ks[2], (BATCH, CTX_LEN, D_MODEL), 1.0),
        "c_ctx": nrm(ks[3], (D_MODEL,), 1.0),
        "w_mod": nrm(ks[4], (DEPTH, D_MODEL, N_MOD * D_MODEL), D_MODEL ** -0.5),
        "b_mod": nrm(ks[5], (DEPTH, N_MOD * D_MODEL), 0.02),
        "norm_ffn1": gain(ks[6], (DEPTH, D_MODEL)),
        "w_ffn1_in": nrm(ks[7], (DEPTH, D_MODEL, 2 * D_FF), D_MODEL ** -0.5),
        "w_ffn1_out": nrm(ks[8], (DEPTH, D_FF, D_MODEL), D_FF ** -0.5),
        "norm_mix": gain(ks[9], (DEPTH, D_MODEL)),
        "w_in": nrm(ks[10], (DEPTH, D_MODEL, PROJ_WIDTH), D_MODEL ** -0.5),
        "w_pool": nrm(ks[11], (DEPTH, POOL_GROUPS, POOL_GROUP_DIM, POOL_GROUP_DIM), POOL_GROUP_DIM ** -0.5),
        "pool_scale": gain(ks[12], (DEPTH, POOL_WIDTH)),
        "sink": nrm(ks[13], (DEPTH, N_HEADS), 1.0),
        "w_out": nrm(ks[14], (DEPTH, MIX_WIDTH, D_MODEL), MIX_WIDTH ** -0.5),
        "norm_ffn2": gain(ks[15], (DEPTH, D_MODEL)),
        "w_ffn2_in": nrm(ks[16], (DEPTH, D_MODEL, 2 * D_FF), D_MODEL ** -0.5),
        "w_ffn2_out": nrm(ks[17], (DEPTH, D_FF, D_MODEL), D_FF ** -0.5),
        "norm_final": gain(ks[18], (D_MODEL,)),
    }


def reference(x, c, ctx, c_ctx, w_mod, b_mod, norm_ffn1, w_ffn1_in, w_ffn1_out, norm_mix, w_in,
              w_pool, pool_scale, sink, w_out, norm_ffn2, w_ffn2_in, w_ffn2_out, norm_final):
    B = x.shape[0]
    cos, sin = axial_rope_tables(x.shape[1])
    h, hc = x, ctx
    for l in range(DEPTH):
        last = l == DEPTH - 1
        mx = (jax.nn.silu(c) @ w_mod[l] + b_mod[l]).reshape(B, N_MOD, 1, D_MODEL)
        mc = (jax.nn.silu(c_ctx) @ w_mod[l] + b_mod[l]).reshape(N_MOD, D_MODEL)
        h = h + 0.5 * mx[:, 2] * swiglu(norm_modulate(h, norm_ffn1[l], mx[:, 0], mx[:, 1]),
                                         w_ffn1_in[l], w_ffn1_out[l])
        hc = hc + 0.5 * mc[2] * swiglu(norm_modulate(hc, norm_ffn1[l], mc[0], mc[1]),
                                        w_ffn1_in[l], w_ffn1_out[l])
        n = norm_modulate(h, norm_mix[l], mx[:, 3], mx[:, 4])
        nc = norm_modulate(hc, norm_mix[l], mc[3], mc[4])
        kc, vc = context_kv(nc, w_in[l])
        h = h + mx[:, 5] * mix_latent(n, kc, vc, w_in[l], w_pool[l], pool_scale[l], sink[l], w_out[l], cos, sin)
        if not last:
            hc = hc + mc[5] * mix_context(nc, kc, vc, w_in[l], w_pool[l], pool_scale[l], sink[l], w_out[l])
            hc = hc + 0.5 * mc[8] * swiglu(norm_modulate(hc, norm_ffn2[l], mc[6], mc[7]),
                                            w_ffn2_in[l], w_ffn2_out[l])
        h = h + 0.5 * mx[:, 8] * swiglu(norm_modulate(h, norm_ffn2[l], mx[:, 6], mx[:, 7]),
                                         w_ffn2_in[l], w_ffn2_out[l])
    return rmsnorm(h, norm_final)
```

```python
import numpy as np
from contextlib import ExitStack
import concourse.bass as bass
import concourse.mybir as mybir
from concourse.bass_utils import run_bass_kernel_spmd

F32 = mybir.dt.float32
BF16 = mybir.dt.bfloat16
AF = mybir.ActivationFunctionType
ALU = mybir.AluOpType

D = 1024
DFF = 2816
NJ = DFF // 128
SEQ = 4096
OWN = 2048
NL = 2304
NCOL = 2560
CTX0 = 2176
H2C0 = 2432
EPS = 1e-6
DEPTH = 2
NEG = -30000.0


def colset_ranges(c0, n):
    out = []
    for (a, b, s) in ((0, CTX0, 0), (CTX0, H2C0, 1), (H2C0, NCOL, 0)):
        lo, hi = max(a, c0), min(b, c0 + n)
        if hi > lo:
            out.append((lo, hi - lo, s))
    return out


class Sched:
    ENG = ["pe", "act", "dve", "pool", "sp"]

    def __init__(self, nc, es):
        self.nc, self.es = nc, es
        self.ops = {e: [] for e in self.ENG}
        self.sems, self.cnt = {}, {}
        self.seen = {e: {} for e in self.ENG}
        self.res = {}
        self.nins = {e: 0 for e in self.ENG}

    def sem(self, name):
        if name not in self.sems:
            self.sems[name] = self.es.enter_context(self.nc.semaphore(name))
            self.cnt[name] = 0
        return self.sems[name]

    def _deps(self, reads, writes):
        d = {}
        for r in reads:
            st = self.res.get(r)
            if st and st[0] is not None:
                s, v = st[0]
                if d.get(s, 0) < v:
                    d[s] = v
        for w in writes:
            st = self.res.get(w)
            if st:
                if st[0] is not None:
                    s, v = st[0]
                    if d.get(s, 0) < v:
                        d[s] = v
                for s, v in st[1].items():
                    if d.get(s, 0) < v:
                        d[s] = v
        return d

    def _waits(self, eng, d):
        seen = self.seen[eng]
        for s, v in d.items():
            if seen.get(s, 0) < v:
                seen[s] = v
                h = self.sems[s]
                self.ops[eng].append(lambda e, h=h, v=v: e.wait_ge(h, v))

    def _commit(self, ev, reads, writes):
        s, v = ev
        for r in reads:
            st = self.res.setdefault(r, [None, {}])
            if st[1].get(s, 0) < v:
                st[1][s] = v
        for w in writes:
            self.res[w] = [ev, {}]

    def op(self, eng, fn, reads=(), writes=()):
        self._waits(eng, self._deps(reads, writes))
        sname = "s_" + eng
        h = self.sem(sname)
        self.cnt[sname] += 1
        ev = (sname, self.cnt[sname])
        self.ops[eng].append(lambda e, fn=fn, h=h: fn(e).then_inc(h, 1))
        self.nins[eng] += 1
        self._commit(ev, reads, writes)

    def mm(self, out, pairs, reads, writes, start=True, stop=True):
        self._waits("pe", self._deps(reads, writes))
        h = self.sem("s_pe")
        self.cnt["s_pe"] += 1
        ev = ("s_pe", self.cnt["s_pe"])
        n = len(pairs)
        for i, pr in enumerate(pairs):
            l, r = pr[0], pr[1]
            o_i = pr[2] if len(pr) > 2 else out
            st_i = pr[3] if len(pr) > 3 else (start and i == 0)
            sp_i = pr[4] if len(pr) > 4 else (stop and (i == n - 1))
            if i == n - 1:
                self.ops["pe"].append(lambda e, l=l, r=r, o_i=o_i, st_i=st_i, sp_i=sp_i: e.matmul(o_i, l, r, start=st_i, stop=sp_i).then_inc(h, 1))
            else:
                self.ops["pe"].append(lambda e, l=l, r=r, o_i=o_i, st_i=st_i, sp_i=sp_i: e.matmul(o_i, l, r, start=st_i, stop=sp_i))
        self.nins["pe"] += n
        self._commit(ev, reads, writes)

    def barrier(self):
        d = {s: v for s, v in self.cnt.items() if v > 0}
        for eng in self.ENG:
            self._waits(eng, dict(d))
        self.res = {}

    def dma(self, eng, sname, out, in_, reads=(), writes=()):
        if writes:
            sname = "d_" + "_".join(str(x) for x in writes[0])
        assert sname is not None
        self._waits(eng, self._deps(reads, writes))
        h = self.sem(sname)
        self.cnt[sname] += 16
        ev = (sname, self.cnt[sname])
        self.ops[eng].append(lambda e: e.dma_start(out=out, in_=in_).then_inc(h, 16))
        self._commit(ev, reads, writes)
        return ev

    def wait_all(self, eng, evs):
        d = {}
        for s, v in evs:
            d[s] = max(d.get(s, 0), v)
        self._waits(eng, d)


class Arena:
    def __init__(self, nc, es, nbytes, name):
        self.t = es.enter_context(nc.sbuf_tensor(name, [128, nbytes // 4], F32))
        self.nbytes = nbytes

    def view(self, off, dtype, shape):
        esz = 4 if dtype == F32 else 2
        n = int(np.prod(shape))
        assert off % 4 == 0 and off + n * esz <= self.nbytes, (off, n, esz, self.nbytes)
        ap = self.t[:, off // 4: off // 4 + (n * esz + 3) // 4]
        if dtype != F32:
            ap = ap.bitcast(dtype)
        if len(shape) == 2:
            ap = ap.rearrange("p (a b) -> p a b", a=shape[0])
        elif len(shape) == 3:
            ap = ap.rearrange("p (a b c) -> p a b c", a=shape[0], b=shape[1])
        return ap


def subtiles(n, step=512):
    out, off = [], 0
    while off < n:
        m = min(step, n - off)
        out.append((off, m))
        off += m
    return out


WCA = 512 + 256 + 256 + 128
WCOLS = WCA + 1024
ULAT = 8 + 2192 + 8
UCTX = 8 + 256 + 8
STAGES = ["l0ffn1", "l0mix", "l0ffn2", "l1ffn1", "l1mix", "l1ffn2", "final"]
DEBUG_MIX_STOP = [None]


def build(upto="final", dump=False):
    nc = bass.Bass("TRN2", target_bir_lowering=False)
    es = ExitStack()
    nstage = STAGES.index(upto)
    with es:
        S = Sched(nc, es)

        def din(name, shape):
            return nc.dram_tensor(name, list(shape), F32, kind="ExternalInput").ap()

        xT = din("xT", [128, 8, NCOL])
        cT = din("cT", [128, 8, 2])
        wmod = din("wmod", [DEPTH, 72, 128, 8 * 128])
        bmod = din("bmod", [DEPTH, 128, 72])
        gains = din("gains", [DEPTH, 128, 3, 8])
        gfin = din("gfin", [128, 8])
        wab = din("wab", [DEPTH, 2, NJ, 128, 2 * 8 * 128])
        wo = din("wo", [DEPTH, 2, 8, 128, NJ * 128])
        win = din("win", [DEPTH, 128, 8, WCOLS])
        wout = din("wout", [DEPTH, 128, 8, D])
        wpool = din("wpool", [DEPTH, 128, 4, 128])
        pscale = din("pscale", [DEPTH, 128, 4])
        sinkrow = din("sinkrow", [DEPTH, 1, 2, 512])
        cosT = din("cosT", [128, NCOL])
        sinT = din("sinT", [128, NCOL])
        bandT = din("bandT", [128, 32, 128])
        cmask = din("cmask", [128, 128 + 512 + 512 + 128])
        if dump:
            hdump = nc.dram_tensor("hdump", [128, 8, NCOL], F32, kind="ExternalOutput").ap()
        outT = nc.dram_tensor("outT", [128, 8, OWN], F32, kind="ExternalOutput").ap()

        A = Arena(nc, es, 212480, "arena")
        pos = [0]

        def alloc(dtype, shape, dma=False):
            esz = 4 if dtype == F32 else 2
            nb = int(np.prod(shape)) * esz
            al = 512 if dma else 64
            pos[0] = (pos[0] + al - 1) // al * al
            v = A.view(pos[0], dtype, shape)
            pos[0] += (nb + al - 1) // al * al
            return v

        hT = alloc(F32, [8, NCOL], dma=True)
        cTs = alloc(F32, [8, 2], dma=True)
        bm_l = [alloc(F32, [72], dma=True) for _ in range(2)]
        gn_l = [alloc(F32, [3, 8], dma=True) for _ in range(2)]
        gf = alloc(F32, [8], dma=True)
        ones = alloc(BF16, [128])
        sc = alloc(BF16, [8, 2])
        modraw_l = [alloc(F32, [72, 2]) for _ in range(2)]
        Amod_l = [alloc(F32, [3, 8, 2]) for _ in range(2)]
        gate_l = [alloc(F32, [3, 8, 2]) for _ in range(2)]
        epsb = alloc(F32, [2])
        NSQ = 4
        sqr = [alloc(BF16, [512], dma=True) for i in range(NSQ)]
        tmp_s = [alloc(F32, [512], dma=True) for i in range(2)]
        rstd = alloc(F32, [512], dma=True)
        PH2 = pos[0]
        nTb = alloc(BF16, [8, 1280], dma=True)
        gT_off = pos[0]
        gTb = alloc(BF16, [NJ, 1280], dma=True)
        wab_s = [alloc(BF16, [2, 8, 128], dma=True) for i in range(3)]
        wo_s = [alloc(BF16, [NJ, 128], dma=True) for i in range(2)]
        sa_s = [alloc(F32, [512], dma=True) for i in range(2)]
        NWM = 3
        wm_s = [alloc(BF16, [8, 128], dma=True) for i in range(NWM)]
        ofin_s = [alloc(F32, [512], dma=True) for i in range(2)]
        assert pos[0] <= A.nbytes, pos[0]
        pos[0] = PH2
        poolout = alloc(BF16, [4, 2432], dma=True)
        nTt = alloc(BF16, [8, 512], dma=True)
        ident = alloc(BF16, [128], dma=True)
        mbp = alloc(BF16, [512], dma=True)
        mbn = alloc(BF16, [512], dma=True)
        sinkl = alloc(BF16, [128], dma=True)
        esink = alloc(BF16, [2, 512], dma=True)
        MG = pos[0]
        kT = alloc(BF16, [2, NCOL], dma=True)
        Vaug = alloc(BF16, [20, 2, 128], dma=True)
        cst = alloc(F32, [2, 512], dma=True)
        rt = [alloc(F32, [512], dma=True) for i in range(2)]
        MQ = pos[0]
        wq = alloc(BF16, [8, 1024], dma=True)
        wou = alloc(BF16, [8, D], dma=True)
        qz = alloc(BF16, [8, 512], dma=True)
        aot = alloc(BF16, [4, 512], dma=True)
        NPT = 8
        PT = [alloc(BF16, [512], dma=True) for i in range(NPT)]
        rcp = rt[0]
        assert pos[0] <= A.nbytes, pos[0]
        pos[0] = MQ
        wkv = alloc(BF16, [8, 640], dma=True)
        wu = alloc(BF16, [8, 512], dma=True)
        wpl = alloc(BF16, [4, 128], dma=True)
        psc = alloc(F32, [4], dma=True)
        band = alloc(BF16, [32, 128], dma=True)
        utok = alloc(BF16, [20, 512], dma=True)
        pooled = alloc(BF16, [4, 512], dma=True)
        assert pos[0] <= A.nbytes, pos[0]

        ps = [es.enter_context(nc.psum_tensor("ps%d" % i, [128, 512], F32)) for i in range(8)]
        pa, pb, py, pss, pmisc = ps[0:2], ps[2:4], ps[4:6], ps[6], ps[7]

        S.op("dve", lambda e: e.memset(ones, 1.0), writes=[("ones",)])
        S.op("dve", lambda e: e.memset(epsb, EPS), writes=[("epsb",)])
        for c0 in range(0, NCOL, 512):
            S.dma("sp", None, hT[:, :, c0:c0 + 512], xT[:, :, c0:c0 + 512],
                  writes=[("h", k, b) for b in range(c0 // 128, c0 // 128 + 4) for k in range(8)])
        S.dma("sp", "d_c", cTs, cT, writes=[("cT",)])
        S.dma("sp", "d_c", gf, gfin, writes=[("gf",)])
        S.op("act", lambda e: e.activation(sc, cTs, AF.Silu), reads=[("cT",)], writes=[("sc",)])

        wm_cnt = [0]
        curp = {"par": 0}

        def mod_load(l):
            par = l % 2
            S.dma("sp", None, bm_l[par], bmod[l], writes=[("bm", par)])
            S.dma("sp", None, gn_l[par], gains[l], writes=[("gn", par)])

        def mod_chunk(l, idx):
            slot = wm_cnt[0] % NWM
            wm_cnt[0] += 1
            S.dma("pool", None, wm_s[slot], wmod[l, idx].rearrange("p (k c) -> p k c", k=8), writes=[("wm", slot)])
            S.mm(pmisc[:, 2 * idx:2 * idx + 2], [(wm_s[slot][:, k, :], sc[:, k, :]) for k in range(8)],
                 reads=[("wm", slot), ("sc",)], writes=[("pm",)])

        def mod_finish(l, m):
            par = l % 2
            pm = pmisc[:, 16 * m:16 * m + 16].rearrange("p (m t) -> p m t", t=2)
            for t in range(2):
                S.op("dve", lambda e, t=t, pm=pm: e.tensor_tensor(
                    modraw_l[par][:, 8 * m:8 * m + 8, t], pm[:, :, t], bm_l[par][:, 8 * m:8 * m + 8], ALU.add),
                    reads=[("pm",), ("bm", par)], writes=[("modraw", par, m)])

        def mod_derive(l, i, what="both"):
            par = l % 2
            modraw, Amod, gate, gn = modraw_l[par], Amod_l[par], gate_l[par], gn_l[par]
            scale = modraw[:, (3 * i + 1) * 8:(3 * i + 2) * 8, :]
            gt = modraw[:, (3 * i + 2) * 8:(3 * i + 3) * 8, :]
            if what in ("both", "A"):
                S.op("dve", lambda e: e.tensor_scalar(Amod[:, i], scale, 1.0, None, ALU.add),
                     reads=[("modraw", par, 3 * i + 1)], writes=[("Amod", par, i)])
                for t in range(2):
                    S.op("dve", lambda e, t=t: e.tensor_tensor(Amod[:, i, :, t], Amod[:, i, :, t], gn[:, i, :], ALU.mult),
                         reads=[("Amod", par, i), ("gn", par)], writes=[("Amod", par, i)])
            if what in ("both", "gate"):
                S.op("dve", lambda e: e.tensor_scalar(gate[:, i], gt, 0.5 if i != 1 else 1.0, None, ALU.mult),
                     reads=[("modraw", par, 3 * i + 2)], writes=[("gate", par, i)])

        def mod_tasks(l, ms):
            tasks = []
            for m in ms:
                for c in range(8):
                    tasks.append(lambda m=m, c=c: mod_chunk(l, 8 * m + c))
                tasks.append(lambda m=m: mod_finish(l, m))
            return tasks

        def hkeys(k, c0, n):
            return [("h", k, b) for b in range(c0 // 128, (c0 + n + 127) // 128)]

        cnt = {"tmp": 0, "sq": 0, "ab": 0, "o": 0, "p": 0, "pt": 0, "pv": 0, "st": 0}

        def rstd_steps(c0, n):
            steps = []

            def sq_step(k):
                si = cnt["sq"] % NSQ
                cnt["sq"] += 1
                S.op("act", lambda e: e.activation(sqr[si][:, 0:n], hT[:, k, c0:c0 + n], AF.Square),
                     reads=hkeys(k, c0, n), writes=[("sq", si)])
                S.mm(pss[:, 0:n], [(ones, sqr[si][:, 0:n])], reads=[("ones",), ("sq", si)],
                     writes=[("pss",)] if k in (0, 7) else [], start=(k == 0), stop=(k == 7))

            def fin():
                S.op("act", lambda e: e.activation(rstd[:, 0:n], pss[:, 0:n], AF.Ln, bias=epsb[:, 0:1], scale=1.0 / D),
                     reads=[("pss",), ("epsb",)], writes=[("rstd",)])
                S.op("act", lambda e: e.activation(rstd[:, 0:n], rstd[:, 0:n], AF.Exp, scale=-0.5),
                     reads=[("rstd",)], writes=[("rstd",)])

            for k in range(8):
                steps.append(lambda k=k: sq_step(k))
            steps.append(fin)
            return steps

        def rstd_of(c0, n):
            for st in rstd_steps(c0, n):
                st()

        def norm_steps(c0, n, i_norm, dst, dst_keys, eng="dve", split=False):
            rsteps = rstd_steps(c0, n)
            steps = []
            shift_i = 3 * i_norm
            par = curp["par"]
            modraw, Amod = modraw_l[par], Amod_l[par]

            def mod_step(k, a, m, t):
                off = a - c0
                ti = cnt["tmp"] % 2
                cnt["tmp"] += 1
                tmp = tmp_s[ti]
                S.op(eng, lambda e: e.tensor_tensor(tmp[:, 0:m], hT[:, k, a:a + m], rstd[:, off:off + m], ALU.mult),
                     reads=hkeys(k, a, m) + [("rstd",)], writes=[("tmp", ti)])
                S.op("act", lambda e: e.activation(
                    dst(k, off, m), tmp[:, 0:m], AF.Identity, bias=modraw[:, shift_i * 8 + k, t:t + 1],
                    scale=Amod[:, i_norm, k, t:t + 1]),
                    reads=[("tmp", ti), ("modraw", par, shift_i), ("Amod", par, i_norm)], writes=dst_keys(k))

            for k in range(8):
                for (a, m, t) in colset_ranges(c0, n):
                    steps.append(lambda k=k, a=a, m=m, t=t: mod_step(k, a, m, t))
            if split:
                return rsteps, steps
            return rsteps + steps

        def norm_mod(c0, n, i_norm, dst, dst_keys, eng="dve"):
            for st in norm_steps(c0, n, i_norm, dst, dst_keys, eng):
                st()

        def interleave(main, side):
            side = list(side)
            nm = max(len(main), 1)
            per = (len(side) + nm - 1) // nm
            for st in main:
                st()
                for _ in range(per):
                    if side:
                        side.pop(0)()
            while side:
                side.pop(0)()

        def resid(pst, c0, n, i_norm, d):
            par = curp["par"]
            gate = gate_l[par]
            for (a, m, t) in colset_ranges(c0, n):
                o2 = a - c0
                S.op("dve", lambda e, a=a, m=m, t=t, o2=o2: e.scalar_tensor_tensor(
                    hT[:, d, a:a + m], pst[1][:, o2:o2 + m], gate[:, i_norm, d, t:t + 1], hT[:, d, a:a + m],
                    ALU.mult, ALU.add),
                    reads=[pst[0], ("gate", par, i_norm)], writes=hkeys(d, a, m))

        def ffn_seg_norm(c0, sn, i_norm):
            steps = []
            for sti, (off, n) in enumerate(subtiles(sn)):
                steps += norm_steps(
                    c0 + off, n, i_norm,
                    lambda k, o2, m, off=off: nTb[:, k, off + o2: off + o2 + m],
                    lambda k, sti=sti: [("nT", k, sti)])
            return steps

        def ffn(l, f, i_norm, segs, extra=(), pre=(), last_seg_extra=(), skip_first_norm=False, tail_side=()):
            curp["par"] = l % 2
            for t in pre:
                t()
            extra = list(extra)
            nslots = NJ * len(segs) - (0 if skip_first_norm else 3)
            per = (len(extra) + nslots - 1) // nslots if extra else 0
            lse = list(last_seg_extra)
            per_last = (len(lse) + NJ - 1) // NJ if lse else 0

            def seg_norm(c0, sn):
                return ffn_seg_norm(c0, sn, i_norm)

            first_norms = None
            if not skip_first_norm:
                c00, sn0 = segs[0]
                first_norms = [norm_steps(c00 + off, n, i_norm,
                                          lambda k, o2, m, off=off: nTb[:, k, off + o2: off + o2 + m],
                                          lambda k, sti=sti: [("nT", k, sti)])
                               for sti, (off, n) in enumerate(subtiles(sn0))]
                for t in first_norms[0]:
                    t()
            def wab_load(j):
                slot = cnt["ab"] % 3
                cnt["ab"] += 1
                S.dma("pool", None, wab_s[slot], wab[l, f, j].rearrange("p (a k c) -> p a k c", a=2, k=8),
                      writes=[("wab", slot)])
                return slot

            def in_step(j, slot, sti, off, n):
                pi = cnt["p"] % 2
                cnt["p"] += 1
                nkeys = [("nT", k, sti) for k in range(8)]
                S.mm(pa[pi][:, 0:n], [(wab_s[slot][:, 0, k, :], nTb[:, k, off:off + n]) for k in range(8)],
                     reads=[("wab", slot)] + nkeys, writes=[("pa", pi)])
                S.mm(pb[pi][:, 0:n], [(wab_s[slot][:, 1, k, :], nTb[:, k, off:off + n]) for k in range(8)],
                     reads=[("wab", slot)] + nkeys, writes=[("pb", pi)])
                S.op("act", lambda e: e.activation(sa_s[pi][:, 0:n], pa[pi][:, 0:n], AF.Silu),
                     reads=[("pa", pi)], writes=[("sa", pi)])
                S.op("dve", lambda e: e.tensor_tensor(gTb[:, j, off:off + n], sa_s[pi][:, 0:n], pb[pi][:, 0:n], ALU.mult),
                     reads=[("sa", pi), ("pb", pi)], writes=[("g", j, sti)])

            for si, (c0, sn) in enumerate(segs):
                sts = subtiles(sn)
                j_start = 0
                if si == 0 and first_norms is not None and len(sts) > 1:
                    slots = [wab_load(j) for j in range(3)]
                    for sti, (off, n) in enumerate(sts):
                        side = first_norms[sti + 1] if sti + 1 < len(sts) else []
                        interleave([lambda j=j, sti=sti, off=off, n=n: in_step(j, slots[j], sti, off, n) for j in range(3)], side)
                    j_start = 3
                for j in range(j_start, NJ):
                    slot = wab_load(j)
                    for sti, (off, n) in enumerate(sts):
                        in_step(j, slot, sti, off, n)
                    for _ in range(per):
                        if extra:
                            extra.pop(0)()
                    if si == len(segs) - 1:
                        for _ in range(per_last):
                            if lse:
                                lse.pop(0)()
                nxt = seg_norm(*segs[si + 1]) if si + 1 < len(segs) else list(tail_side)
                main = []

                def out_step(d, sti, off, n, slot):
                    pi = cnt["p"] % 2
                    cnt["p"] += 1
                    S.mm(py[pi][:, 0:n], [(wo_s[slot][:, j, :], gTb[:, j, off:off + n]) for j in range(NJ)],
                         reads=[("wo", slot)] + [("g", j, sti) for j in range(NJ)], writes=[("py", pi)])
                    resid((("py", pi), py[pi]), c0 + off, n, i_norm, d)

                def wo_load(d, slot):
                    S.dma("pool", None, wo_s[slot], wo[l, f, d].rearrange("p (j c) -> p j c", j=NJ),
                          writes=[("wo", slot)])

                for d in range(8):
                    slot = cnt["o"] % 2
                    cnt["o"] += 1
                    for sti, (off, n) in enumerate(sts):
                        if sti == 0:
                            main.append(lambda d=d, sti=sti, off=off, n=n, slot=slot: (wo_load(d, slot), out_step(d, sti, off, n, slot)))
                        else:
                            main.append(lambda d=d, sti=sti, off=off, n=n, slot=slot: out_step(d, sti, off, n, slot))
                interleave(main, nxt)
            while extra:
                extra.pop(0)()
            while lse:
                lse.pop(0)()

        def col2blk(c):
            return c // 128

        def mix(l):
            last = (l == DEPTH - 1)
            curp["par"] = l % 2
            S.dma("pool", "d_cst", ident, cmask[:, 0:128], writes=[("ident",)])
            S.dma("pool", "d_cst", mbp, cmask[:, 128:640], writes=[("mbp",)])
            S.dma("pool", "d_cst", mbn, cmask[:, 640:1152], writes=[("mbn",)])
            S.dma("pool", "d_cst", sinkl[0:1, :], cmask[0:1, 1152:1280], writes=[("sinkl",)])
            S.dma("pool", "d_cst", wpl, wpool[l], writes=[("wpl",)])
            S.dma("sp", "d_c", psc, pscale[l], writes=[("psc",)])
            for g in range(2):
                S.dma("sp", "d_c", tmp_s[g][0:1, :], sinkrow[l, :, g, :], writes=[("tmp", g)])
                S.op("act", lambda e, g=g: e.activation(esink[0:1, g, :], tmp_s[g][0:1, :], AF.Exp),
                     reads=[("tmp", g)], writes=[("esink",)])

            S.dma("pool", None, wkv, win[l, :, :, 512:WCA], writes=[("wkv",)])
            S.dma("pool", None, wu, win[l, :, :, 0:512], writes=[("wu",)])
            S.op("dve", lambda e: e.memset(Vaug[:, :, :, 64:128], 1.0), writes=[("Vaug",)])
            acols = [(0, 512), (512, 512), (1024, 512), (1536, 512), (2048, 384) if last else (2048, 512)]
            ev_i = [0]

            def evac(dst, src, reads, writes):
                ev_i[0] += 1
                if ev_i[0] % 2:
                    S.op("act", lambda e: e.copy(dst, src), reads=reads, writes=writes)
                else:
                    S.op("dve", lambda e: e.tensor_copy(dst, src), reads=reads, writes=writes)

            nTt2 = band.rearrange("p a b -> p (a b)").rearrange("p (k c) -> p k c", k=8)
            nbufs = [nTt, nTt2]

            def nkeys(ti):
                return [("nTt", k) for k in range(8)] if ti % 2 == 0 else [("band",)]

            def a_norm(ti, c0, n):
                nb = nbufs[ti % 2]
                return norm_steps(c0, n, 1, lambda k, o2, m: nb[:, k, o2:o2 + m],
                                  (lambda k: [("nTt", k)]) if ti % 2 == 0 else (lambda k: [("band",)]), eng="pool")

            for st in a_norm(0, *acols[0]):
                st()
            for ti, (c0, n) in enumerate(acols):
                nTt_ = nbufs[ti % 2]
                nk = nkeys(ti)
                def cs_load(n=n, c0=c0):
                    S.dma("sp", None, cst[:, 0, 0:n], cosT[:, c0:c0 + n], writes=[("cs", 0)])
                    S.dma("sp", None, cst[:, 1, 0:n], sinT[:, c0:c0 + n], writes=[("cs", 1)])

                def k_step(g, n=n, c0=c0, nTt=nTt_, nk=nk):
                    S.mm(pa[g][:, 0:n], [(wkv[:, k, g * 128:(g + 1) * 128], nTt[:, k, 0:n]) for k in range(8)],
                         reads=[("wkv",)] + nk, writes=[("pa", g)])
                    S.mm(pb[g][:, 0:n], [(wkv[:, k, 256 + g * 128:256 + (g + 1) * 128], nTt[:, k, 0:n]) for k in range(8)],
                         reads=[("wkv",)] + nk, writes=[("pb", g)])
                    S.op("dve", lambda e: e.tensor_tensor(rt[0][:, 0:n], pa[g][:, 0:n], cst[:, 0, 0:n], ALU.mult),
                         reads=[("pa", g), ("cs", 0)], writes=[("rt", 0)])
                    S.op("dve", lambda e: e.tensor_tensor(rt[1][:, 0:n], pb[g][:, 0:n], cst[:, 1, 0:n], ALU.mult),
                         reads=[("pb", g), ("cs", 1)], writes=[("rt", 1)])
                    S.op("dve", lambda e: e.tensor_tensor(kT[:, g, c0:c0 + n], rt[0][:, 0:n], rt[1][:, 0:n], ALU.add),
                         reads=[("rt", 0), ("rt", 1)], writes=[("kT", g, c0)])

                def v_step(tb, c0=c0, nTt=nTt_, nk=nk):
                    blk = (c0 + tb * 128) // 128
                    pi = cnt["p"] % 2
                    cnt["p"] += 1
                    S.mm(py[pi][:, 0:128], [(nTt[:, k, tb * 128:(tb + 1) * 128], wkv[:, k, 512:640]) for k in range(8)],
                         reads=[("wkv",)] + nk, writes=[("py", pi)])
                    S.op("act", lambda e: e.copy(
                        Vaug[:, blk, :, 0:64], py[pi][:, 0:128].rearrange("p (g d) -> p g d", g=2)),
                        reads=[("py", pi)], writes=[("Vaug",)])

                def utok_step(tb, c0=c0, nTt=nTt_, nk=nk):
                    blk = c0 // 128 + tb
                    pi = cnt["p"] % 2
                    cnt["p"] += 1
                    S.mm(py[pi][:, :], [(nTt[:, k, tb * 128:(tb + 1) * 128], wu[:, k, :]) for k in range(8)],
                         reads=[("wu",)] + nk, writes=[("py", pi)])
                    S.op("dve", lambda e: e.tensor_copy(utok[:, blk, :], py[pi][:, :]), reads=[("py", pi)], writes=[("utok", blk)])

                cs_load()
                nb_ = n // 128
                main = [lambda tb=tb: utok_step(tb) for tb in range(nb_)]
                main += [lambda g=g: k_step(g) for g in range(2)]
                main += [lambda tb=tb: v_step(tb) for tb in range(nb_)]
                side = a_norm(ti + 1, *acols[ti + 1]) if ti + 1 < len(acols) else []
                interleave(main, side)
            S.dma("pool", None, band, bandT, writes=[("band",)])
            S.dma("pool", None, wq, win[l, :, :, WCA:WCOLS], writes=[("wq",), ("wkv",), ("wu",)])
            oblocks = list(range(16)) if last else list(range(19))
            for t0 in range(0, len(oblocks), 4):
                obs = oblocks[t0:t0 + 4]
                n = 128 * len(obs)
                for g in range(4):
                    pbank, pkey = [(pa[0], ("pa", 0)), (pa[1], ("pa", 1)), (pb[0], ("pb", 0)), (pb[1], ("pb", 1))][g]
                    pairs, rd = [], [("band",)]
                    for oi, ob in enumerate(obs):
                        if ob <= 16:
                            srcs = ([(ob - 1, 0)] if ob >= 1 else []) + [(ob, 3 if ob == 0 else 1), (ob + 1 if ob < 16 else 19, 2)]
                        elif ob == 17:
                            srcs = [(17, 6), (18, 5)]
                        else:
                            srcs = [(17, 4), (18, 7)]
                        for si, (sb, kind) in enumerate(srcs):
                            pairs.append((utok[:, sb, g * 128:(g + 1) * 128], band[:, g * 8 + kind, :],
                                          pbank[:, oi * 128:(oi + 1) * 128], si == 0, si == len(srcs) - 1))
                            rd.append(("utok", sb))
                    S.mm(pbank[:, 0:n], pairs, reads=rd, writes=[pkey])
                    S.op("dve", lambda e, g=g, n=n, pbank=pbank: e.tensor_copy(pooled[:, g, 0:n], pbank[:, 0:n]),
                         reads=[pkey], writes=[("pooled", g)])
                for g in range(4):
                    pj = cnt["p"] % 2
                    cnt["p"] += 1
                    S.mm(py[pj][:, 0:n], [(wpl[:, g, :], pooled[:, g, 0:n])], reads=[("wpl",), ("pooled", g)], writes=[("py", pj)])
                    oc = obs[0] * 128
                    S.op("act", lambda e, pj=pj, g=g, n=n, oc=oc: e.activation(
                        poolout[:, g, oc:oc + n], py[pj][:, 0:n], AF.Identity, scale=psc[:, g:g + 1]),
                        reads=[("py", pj), ("psc",)], writes=[("poolout", g, oc)])
            qtiles = [(0, 512), (512, 512), (1024, 512), (1536, 512)]
            if not last:
                qtiles.append((2048, 384))
            for st in norm_steps(qtiles[0][0], qtiles[0][1], 1, lambda k, o2, m: nTt[:, k, o2:o2 + m], lambda k: [("nTt", k)], eng="pool"):
                st()
            S.barrier()

            if DEBUG_MIX_STOP[0] in ("P", "KV"):
                return
            S.dma("pool", None, wou, wout[l], writes=[("wou",)])
            S.op("dve", lambda e: e.memset(qz, 0.0), writes=[("qz", h) for h in range(8)])
            pst = [pb[0], pb[1], py[0]]
            pstk = [("pb", 0), ("pb", 1), ("py", 0)]
            pvb = [py[1], pmisc]
            pvk = [("py", 1), ("pmisc",)]

            def q_norm(c0, n):
                return norm_steps(c0, n, 1, lambda k, o2, m: nTt[:, k, o2:o2 + m], lambda k: [("nTt", k)], eng="pool")

            def emit_scores(item, sel):
                (qs, g, kbs, pts) = item
                for (kb, mb, mbkey) in kbs[sel]:
                    si = cnt["st"] % 3
                    cnt["st"] += 1
                    stp, stkey = pst[si], pstk[si]
                    pairs = []
                    rd = [("qz", 4 * g + hh) for hh in range(4)]
                    for hh in range(4):
                        pairs.append((kT[:, g, kb * 128:(kb + 1) * 128], qz[:, 4 * g + hh, qs],
                                      stp[:, hh * 128:(hh + 1) * 128], True, True))
                    S.mm(stp[:, :], pairs, reads=rd, writes=[stkey])
                    pti = cnt["pt"] % NPT
                    cnt["pt"] += 1
                    S.op("act", lambda e, stp=stp, pti=pti: e.activation(PT[pti], stp[:, :], AF.Exp, scale=0.125),
                         reads=[stkey], writes=[("PT", pti)])
                    if mb is not None:
                        S.op("dve", lambda e, pti=pti, mb=mb: e.tensor_tensor(PT[pti], PT[pti], mb, ALU.mult),
                             reads=[("PT", pti), mbkey], writes=[("PT", pti)])
                    pts.append((kb, pti))

            def emit_pv(item):
                (qs, g, kbs, pts) = item
                pvi = cnt["pv"] % 2
                cnt["pv"] += 1
                pvp, pvkey = pvb[pvi], pvk[pvi]
                pairs = [(Vaug[:, kb, g, :], PT[pti]) for (kb, pti) in pts]
                pairs.append((sinkl[0:1, :], esink[0:1, g, :]))
                S.mm(pvp[:, :], pairs, reads=[("Vaug",), ("sinkl",), ("esink",)] + [("PT", pti) for (_, pti) in pts],
                     writes=[pvkey])
                return (pvp, pvkey, qs, g)

            def emit_normalise(pvinfo):
                (pvp, pvkey, qs, g) = pvinfo
                S.op("act", lambda e: e.activation(rcp[64:128, :], pvp[64:128, :], AF.Ln), reads=[pvkey], writes=[("rt", 0)])
                S.op("act", lambda e: e.activation(rcp[64:128, :], rcp[64:128, :], AF.Exp, scale=-1.0),
                     reads=[("rt", 0)], writes=[("rt", 0)])
                for hf in range(2):
                    i0v = pvp[0:64, :].rearrange("p (c t q) -> p c t q", c=2, t=2)[:, :, hf, :]
                    i1v = rcp[64:128, :].rearrange("p (c t q) -> p c t q", c=2, t=2)[:, :, hf, :]
                    ov = aot[64 * hf:64 * hf + 64, 2 * g:2 * g + 2, qs]
                    S.op("dve", lambda e, i0v=i0v, i1v=i1v, ov=ov: e.tensor_tensor(ov, i0v, i1v, ALU.mult),
                         reads=[pvkey, ("rt", 0)], writes=[("aot", 2 * g), ("aot", 2 * g + 1)])

            def wout_step(c0, n, d):
                pairs = [(wou[:, c, d * 128:(d + 1) * 128], poolout[:, c, c0:c0 + n]) for c in range(4)]
                pairs += [(wou[:, 4 + c, d * 128:(d + 1) * 128], aot[:, c, 0:n]) for c in range(4)]
                pi = d % 2
                rd = [("wou",)] + [("aot", c) for c in range(4)]
                S.mm(pa[pi][:, 0:n], pairs, reads=rd, writes=[("pa", pi)])
                resid((("pa", pi), pa[pi]), c0, n, 1, d)

            for ti, (c0, n) in enumerate(qtiles):
                S.dma("sp", None, cst[:, 0, 0:n], cosT[:, c0:c0 + n], writes=[("cs", 0)])
                S.dma("sp", None, cst[:, 1, 0:n], sinT[:, c0:c0 + n], writes=[("cs", 1)])
                nk = [("nTt", k) for k in range(8)]
                for c in range(4):
                    S.mm(pa[0][:, 0:n], [(wq[:, k, c * 128:(c + 1) * 128], nTt[:, k, 0:n]) for k in range(8)],
                         reads=[("wq",)] + nk, writes=[("pa", 0)])
                    S.mm(pa[1][:, 0:n], [(wq[:, k, 512 + c * 128:512 + (c + 1) * 128], nTt[:, k, 0:n]) for k in range(8)],
                         reads=[("wq",)] + nk, writes=[("pa", 1)])
                    S.op("dve", lambda e, n=n: e.tensor_tensor(rt[0][:, 0:n], pa[0][:, 0:n], cst[:, 0, 0:n], ALU.mult),
                         reads=[("pa", 0), ("cs", 0)], writes=[("rt", 0)])
                    S.op("dve", lambda e, n=n: e.tensor_tensor(rt[1][:, 0:n], pa[1][:, 0:n], cst[:, 1, 0:n], ALU.mult),
                         reads=[("pa", 1), ("cs", 1)], writes=[("rt", 1)])
                    for hf in range(2):
                        rows = slice(64 * hf, 64 * hf + 64)
                        S.op("pool", lambda e, c=c, n=n, hf=hf, rows=rows: e.tensor_tensor(
                            qz[rows, 2 * c + hf, 0:n], rt[0][rows, 0:n], rt[1][rows, 0:n], ALU.add),
                            reads=[("rt", 0), ("rt", 1)], writes=[("qz", 2 * c + hf)])
                items = []
                for qb in range(n // 128):
                    qblk = (c0 + qb * 128) // 128
                    qs = slice(qb * 128, (qb + 1) * 128)
                    if qblk <= 16:
                        kbs = []
                        if qblk >= 1:
                            kbs.append((qblk - 1, mbp, ("mbp",)))
                        kbs.append((qblk, None, None))
                        nxt = qblk + 1 if qblk < 16 else 19
                        kbs.append((nxt, mbn, ("mbn",)))
                        kbs += [(17, None, None), (18, None, None)]
                    else:
                        kbs = [(17, None, None), (18, None, None)]
                    for g in range(2):
                        items.append((qs, g, kbs, []))
                emit_scores(items[0], slice(0, None))
                for ii, item in enumerate(items):
                    if ii + 1 < len(items):
                        emit_scores(items[ii + 1], slice(0, 2))
                    pvinfo = emit_pv(item)
                    if ii + 1 < len(items):
                        emit_scores(items[ii + 1], slice(2, None))
                    emit_normalise(pvinfo)
                side = q_norm(*qtiles[ti + 1]) if ti + 1 < len(qtiles) else []
                for st in side[:9]:
                    st()
                interleave([lambda d=d: wout_step(c0, n, d) for d in range(8)], side[9:])
            S.barrier()

        fin_cnt = [0]

        def final_tasks(tiles):
            tasks = []
            for (c0, n) in tiles:
                tasks += rstd_steps(c0, n)

                def out_step(k, c0=c0, n=n):
                    slot = fin_cnt[0] % 2
                    fin_cnt[0] += 1
                    S.op("dve", lambda e: e.scalar_tensor_tensor(
                        ofin_s[slot][:, 0:n], hT[:, k, c0:c0 + n], gf[:, k:k + 1], rstd[:, 0:n], ALU.mult, ALU.mult),
                        reads=hkeys(k, c0, n) + [("gf",), ("rstd",)], writes=[("ofin", slot)])
                    evs.append(S.dma("sp", "d_ofin%d" % slot, outT[:, k, c0:c0 + n], ofin_s[slot][:, 0:n], reads=[("ofin", slot)]))

                tasks += [lambda k=k, f=out_step: f(k) for k in range(8)]
            return tasks

        def final():
            for t in final_tasks([(1024, 512), (1536, 512)]):
                t()

        evs = []
        def l0_first():
            mod_load(0)
            pre = mod_tasks(0, [1, 0]) + [lambda: mod_derive(0, 0, "A")]
            extra = (mod_tasks(0, [2]) + [lambda: mod_derive(0, 0, "gate")]
                     + mod_tasks(0, [4, 3, 5, 7, 6, 8]) + [lambda: mod_derive(0, 1), lambda: mod_derive(0, 2)])
            ffn(0, 0, 0, [(0, 1280), (1280, 1280)], extra=extra, pre=pre)
            S.barrier()

        def l0_ffn2():
            mod_load(1)
            extra = mod_tasks(1, list(range(9))) + [lambda i=i: mod_derive(1, i) for i in range(3)]
            curp["par"] = 1
            nxt_norm = ffn_seg_norm(0, 1280, 0)
            curp["par"] = 0
            ffn(0, 1, 2, [(0, 1280), (1280, 1152)], extra=extra, tail_side=nxt_norm)

        prog = [
            l0_first,
            lambda: mix(0),
            l0_ffn2,
            lambda: (ffn(1, 0, 0, [(0, 1280), (1280, 1152)], skip_first_norm=True), S.barrier()),
            lambda: mix(1),
            lambda: ffn(1, 1, 2, [(0, 1024), (1024, 1024)], last_seg_extra=final_tasks([(0, 512), (512, 512)])),
            lambda: final(),
        ]
        for i in range(nstage + 1):
            prog[i]()

        if dump:
            for k in range(8):
                evs.append(S.dma("sp", "d_out", hdump[:, k, :], hT[:, k, :], reads=[("h", k, b) for b in range(20)]))
        if upto != "final":
            for k in range(8):
                evs.append(S.dma("sp", "d_out", outT[:, k, :], hT[:, k, 0:OWN], reads=[("h", k, b) for b in range(16)]))
        S.wait_all("sp", evs)

        with nc.Block() as block:
            @block.tensor
            def _(e):
                for f in S.ops["pe"]:
                    f(e)

            @block.scalar
            def _(e):
                for f in S.ops["act"]:
                    f(e)

            @block.vector
            def _(e):
                for f in S.ops["dve"]:
                    f(e)

            @block.gpsimd
            def _(e):
                for f in S.ops["pool"]:
                    f(e)

            @block.sync
            def _(e):
                for f in S.ops["sp"]:
                    f(e)
        print("instr counts", S.nins, {k: v for k, v in S.cnt.items()})
    return nc


def core_columns(half):
    p = np.arange(NL)
    return p if half == 0 else (SEQ - 1 - p)


def _kmajor(w):
    return w.reshape(8, 128, w.shape[-1]).transpose(1, 0, 2)


def prep_shared(inp):
    sh = {}
    sh["wmod"] = np.ascontiguousarray(
        inp["w_mod"].reshape(DEPTH, 8, 128, 72, 128).transpose(0, 3, 2, 1, 4).reshape(DEPTH, 72, 128, 8 * 128))
    sh["bmod"] = np.ascontiguousarray(inp["b_mod"].reshape(DEPTH, 72, 128).transpose(0, 2, 1))
    g = np.stack([inp["norm_ffn1"], inp["norm_mix"], inp["norm_ffn2"]], axis=1)
    sh["gains"] = np.ascontiguousarray(g.reshape(DEPTH, 3, 8, 128).transpose(0, 3, 1, 2))
    sh["gfin"] = np.ascontiguousarray(inp["norm_final"].reshape(8, 128).T)
    wab, wo = [], []
    for name_in, name_out in (("w_ffn1_in", "w_ffn1_out"), ("w_ffn2_in", "w_ffn2_out")):
        w = inp[name_in].reshape(DEPTH, 8, 128, 2, NJ, 128)
        wab.append(w.transpose(0, 4, 2, 3, 1, 5).reshape(DEPTH, NJ, 128, 2 * 8 * 128))
        w2 = inp[name_out].reshape(DEPTH, NJ, 128, 8, 128)
        wo.append(w2.transpose(0, 3, 2, 1, 4).reshape(DEPTH, 8, 128, NJ * 128))
    sh["wab"] = np.ascontiguousarray(np.stack(wab, axis=1))
    sh["wo"] = np.ascontiguousarray(np.stack(wo, axis=1))
    wins, wouts = [], []
    for l in range(DEPTH):
        w = inp["w_in"][l]
        u, q, k, v = w[:, 0:512], w[:, 512:1024], w[:, 1024:1152], w[:, 1152:1280]
        qsw = q.reshape(D, 8, 2, 32)[:, :, ::-1, :].reshape(D, 512)
        k3 = k.reshape(D, 2, 64)
        ksw3 = k.reshape(D, 2, 2, 32)[:, :, ::-1, :].reshape(D, 2, 64)
        kdup = np.concatenate([k3, k3], axis=-1).reshape(D, 256)
        kswdup = np.concatenate([ksw3, ksw3], axis=-1).reshape(D, 256)
        cols = np.concatenate([u, kdup, kswdup, v, q, qsw], axis=1)
        wins.append(_kmajor(cols))
        wouts.append(_kmajor(inp["w_out"][l]))
    sh["win"] = np.ascontiguousarray(np.stack(wins))
    sh["wout"] = np.ascontiguousarray(np.stack(wouts))
    sh["wpool"] = np.ascontiguousarray(inp["w_pool"].transpose(0, 2, 1, 3))
    sh["pscale"] = np.ascontiguousarray(inp["pool_scale"].reshape(DEPTH, 4, 128).transpose(0, 2, 1))
    sh["sinkrow"] = np.ascontiguousarray(
        np.repeat(inp["sink"].reshape(DEPTH, 2, 4), 128, axis=2).reshape(DEPTH, 1, 2, 512))
    cm = np.zeros((128, 1280), np.float32)
    cm[:, 0:128] = np.eye(128, dtype=np.float32)
    ki = np.arange(128)[:, None]
    qi = np.arange(128)[None, :]
    cm[:, 128:640] = np.tile(np.where(qi <= ki, 1.0, 0.0), (1, 4))
    cm[:, 640:1152] = np.tile(np.where(ki <= qi, 1.0, 0.0), (1, 4))
    cm[0, 1152 + 64:1280] = 1.0
    sh["cmask"] = cm
    return sh


def _band(out_pos, src_pos, offs, lo, hi):
    P = np.asarray(out_pos)[None, :]
    Q = np.asarray(src_pos)[:, None]
    offs = np.asarray(offs)
    win = P[None, :, :] + offs[:, None, None]
    ok = (win >= lo) & (win < hi)
    cntv = ok.sum(axis=0)
    hit = ((win == Q[None, :, :]) & ok).any(axis=0)
    return hit / cntv - (Q == P)


_BAND_CACHE = {}


def _band_tables(half):
    if half in _BAND_CACHE:
        return _BAND_CACHE[half]
    _BAND_CACHE[half] = _band_tables_build(half)
    return _BAND_CACHE[half]


def _band_tables_build(half):
    T = np.zeros((128, 32, 128), np.float32)
    blk = lambda b: list(range(128 * b, 128 * b + 128))
    for g, w in enumerate((2, 4, 8, 16)):
        fwd = list(range(-(w // 2), w - w // 2))
        offs = fwd if half == 0 else [-o for o in fwd]
        big = 1 << 30
        T[:, g * 8 + 0] = _band(blk(2), blk(1), offs, 0, big)
        T[:, g * 8 + 1] = _band(blk(2), blk(2), offs, 0, big)
        T[:, g * 8 + 2] = _band(blk(2), blk(3), offs, 0, big)
        T[:, g * 8 + 3] = _band(blk(0), blk(0), offs, 0, big)
        T[:, g * 8 + 4] = _band(blk(1), blk(0), fwd, 0, 256)
        T[:, g * 8 + 5] = _band(blk(0), blk(1), fwd, 0, 256)
        T[:, g * 8 + 6] = _band(blk(0), blk(0), fwd, 0, 256)
        T[:, g * 8 + 7] = _band(blk(1), blk(1), fwd, 0, 256)
    return T


def prep_core(inp, core):
    b, half = core // 2, core % 2
    tok = core_columns(half)
    xl = inp["x"][b][tok]
    cols = np.concatenate([xl[:CTX0], inp["ctx"][b], xl[CTX0:]], axis=0)
    m = {}
    m["xT"] = np.ascontiguousarray(cols.T.reshape(8, 128, NCOL).transpose(1, 0, 2))
    cc = np.stack([inp["c"][b], inp["c_ctx"]], axis=-1)
    m["cT"] = np.ascontiguousarray(cc.reshape(8, 128, 2).transpose(1, 0, 2))
    inv = 10000.0 ** (-np.arange(0, 32, 2, dtype=np.float64) / 32.0)
    ang = np.concatenate([(tok // 64)[:, None] * inv, (tok % 64)[:, None] * inv], axis=-1)
    cos_l, sin_l = np.cos(ang).T, np.sin(ang).T
    cosc = np.ones((32, NCOL)); sinc = np.zeros((32, NCOL))
    cosc[:, :CTX0] = cos_l[:, :CTX0]; cosc[:, H2C0:] = cos_l[:, CTX0:]
    sinc[:, :CTX0] = sin_l[:, :CTX0]; sinc[:, H2C0:] = sin_l[:, CTX0:]
    m["cosT"] = np.ascontiguousarray(np.tile(cosc, (4, 1)).astype(np.float32))
    m["sinT"] = np.ascontiguousarray(np.concatenate([-sinc, sinc, -sinc, sinc], axis=0).astype(np.float32))
    m["bandT"] = _band_tables(half)
    return m


_NC_CACHE = {}


def kernel(**inputs):
    inp = {k: np.asarray(v) for k, v in inputs.items()}
    if "nc" not in _NC_CACHE:
        _NC_CACHE["nc"] = build()
    nc = _NC_CACHE["nc"]
    sh = prep_shared(inp)
    in_maps = []
    for core in range(8):
        m = dict(sh)
        m.update(prep_core(inp, core))
        in_maps.append(m)
    res = run_bass_kernel_spmd(nc, in_maps, core_ids=list(range(8)))
    out = np.empty((4, SEQ, D), np.float32)
    for core in range(8):
        b, half = core // 2, core % 2
        o = res.results[core]["outT"]
        rows = o.transpose(2, 1, 0).reshape(OWN, D)
        out[b, core_columns(half)[:OWN]] = rows
    return out
```

```python
import numpy as np
from contextlib import ExitStack
import concourse.bass as bass
import concourse.mybir as mybir
from concourse.bass_utils import run_bass_kernel_spmd

F32 = mybir.dt.float32
BF16 = mybir.dt.bfloat16
AF = mybir.ActivationFunctionType
ALU = mybir.AluOpType

D = 1024
DFF = 2816
NJ = DFF // 128
SEQ = 4096
OWN = 2048
NL = 2304
NCOL = 2560
CTX0 = 2176
H2C0 = 2432
EPS = 1e-6
DEPTH = 2
NEG = -30000.0


def colset_ranges(c0, n):
    out = []
    for (a, b, s) in ((0, CTX0, 0), (CTX0, H2C0, 1), (H2C0, NCOL, 0)):
        lo, hi = max(a, c0), min(b, c0 + n)
        if hi > lo:
            out.append((lo, hi - lo, s))
    return out


class Sched:
    ENG = ["pe", "act", "dve", "pool", "sp"]

    def __init__(self, nc, es):
        self.nc, self.es = nc, es
        self.ops = {e: [] for e in self.ENG}
        self.sems, self.cnt = {}, {}
        self.seen = {e: {} for e in self.ENG}
        self.res = {}
        self.nins = {e: 0 for e in self.ENG}

    def sem(self, name):
        if name not in self.sems:
            self.sems[name] = self.es.enter_context(self.nc.semaphore(name))
            self.cnt[name] = 0
        return self.sems[name]

    def _deps(self, reads, writes):
        d = {}
        for r in reads:
            st = self.res.get(r)
            if st and st[0] is not None:
                s, v = st[0]
                if d.get(s, 0) < v:
                    d[s] = v
        for w in writes:
            st = self.res.get(w)
            if st:
                if st[0] is not None:
                    s, v = st[0]
                    if d.get(s, 0) < v:
                        d[s] = v
                for s, v in st[1].items():
                    if d.get(s, 0) < v:
                        d[s] = v
        return d

    def _waits(self, eng, d):
        seen = self.seen[eng]
        for s, v in d.items():
            if seen.get(s, 0) < v:
                seen[s] = v
                h = self.sems[s]
                self.ops[eng].append(lambda e, h=h, v=v: e.wait_ge(h, v))

    def _commit(self, ev, reads, writes):
        s, v = ev
        for r in reads:
            st = self.res.setdefault(r, [None, {}])
            if st[1].get(s, 0) < v:
                st[1][s] = v
        for w in writes:
            self.res[w] = [ev, {}]

    def op(self, eng, fn, reads=(), writes=()):
        self._waits(eng, self._deps(reads, writes))
        sname = "s_" + eng
        h = self.sem(sname)
        self.cnt[sname] += 1
        ev = (sname, self.cnt[sname])
        self.ops[eng].append(lambda e, fn=fn, h=h: fn(e).then_inc(h, 1))
        self.nins[eng] += 1
        self._commit(ev, reads, writes)

    def mm(self, out, pairs, reads, writes, start=True, stop=True):
        self._waits("pe", self._deps(reads, writes))
        h = self.sem("s_pe")
        self.cnt["s_pe"] += 1
        ev = ("s_pe", self.cnt["s_pe"])
        n = len(pairs)
        for i, pr in enumerate(pairs):
            l, r = pr[0], pr[1]
            o_i = pr[2] if len(pr) > 2 else out
            st_i = pr[3] if len(pr) > 3 else (start and i == 0)
            sp_i = pr[4] if len(pr) > 4 else (stop and (i == n - 1))
            if i == n - 1:
                self.ops["pe"].append(lambda e, l=l, r=r, o_i=o_i, st_i=st_i, sp_i=sp_i: e.matmul(o_i, l, r, start=st_i, stop=sp_i).then_inc(h, 1))
            else:
                self.ops["pe"].append(lambda e, l=l, r=r, o_i=o_i, st_i=st_i, sp_i=sp_i: e.matmul(o_i, l, r, start=st_i, stop=sp_i))
        self.nins["pe"] += n
        self._commit(ev, reads, writes)

    def barrier(self):
        d = {s: v for s, v in self.cnt.items() if v > 0}
        for eng in self.ENG:
            self._waits(eng, dict(d))
        self.res = {}

    def dma(self, eng, sname, out, in_, reads=(), writes=()):
        if writes:
            sname = "d_" + "_".join(str(x) for x in writes[0])
        assert sname is not None
        self._waits(eng, self._deps(reads, writes))
        h = self.sem(sname)
        self.cnt[sname] += 16
        ev = (sname, self.cnt[sname])
        self.ops[eng].append(lambda e: e.dma_start(out=out, in_=in_).then_inc(h, 16))
        self._commit(ev, reads, writes)
        return ev

    def wait_all(self, eng, evs):
        d = {}
        for s, v in evs:
            d[s] = max(d.get(s, 0), v)
        self._waits(eng, d)


class Arena:
    def __init__(self, nc, es, nbytes, name):
        self.t = es.enter_context(nc.sbuf_tensor(name, [128, nbytes // 4], F32))
        self.nbytes = nbytes

    def view(self, off, dtype, shape):
        esz = 4 if dtype == F32 else 2
        n = int(np.prod(shape))
        assert off % 4 == 0 and off + n * esz <= self.nbytes, (off, n, esz, self.nbytes)
        ap = self.t[:, off // 4: off // 4 + (n * esz + 3) // 4]
        if dtype != F32:
            ap = ap.bitcast(dtype)
        if len(shape) == 2:
            ap = ap.rearrange("p (a b) -> p a b", a=shape[0])
        elif len(shape) == 3:
            ap = ap.rearrange("p (a b c) -> p a b c", a=shape[0], b=shape[1])
        return ap


def subtiles(n, step=512):
    out, off = [], 0
    while off < n:
        m = min(step, n - off)
        out.append((off, m))
        off += m
    return out


WCA = 512 + 256 + 256 + 128
WCOLS = WCA + 1024
ULAT = 8 + 2192 + 8
UCTX = 8 + 256 + 8
STAGES = ["l0ffn1", "l0mix", "l0ffn2", "l1ffn1", "l1mix", "l1ffn2", "final"]
DEBUG_MIX_STOP = [None]


def build(upto="final", dump=False):
    nc = bass.Bass("TRN2", target_bir_lowering=False)
    es = ExitStack()
    nstage = STAGES.index(upto)
    with es:
        S = Sched(nc, es)

        def din(name, shape):
            return nc.dram_tensor(name, list(shape), F32, kind="ExternalInput").ap()

        xT = din("xT", [128, 8, NCOL])
        cT = din("cT", [128, 8, 2])
        wmod = din("wmod", [DEPTH, 72, 128, 8 * 128])
        bmod = din("bmod", [DEPTH, 128, 72])
        gains = din("gains", [DEPTH, 128, 3, 8])
        gfin = din("gfin", [128, 8])
        wab = din("wab", [DEPTH, 2, NJ, 128, 2 * 8 * 128])
        wo = din("wo", [DEPTH, 2, 8, 128, NJ * 128])
        win = din("win", [DEPTH, 128, 8, WCOLS])
        wout = din("wout", [DEPTH, 128, 8, D])
        wpool = din("wpool", [DEPTH, 128, 4, 128])
        pscale = din("pscale", [DEPTH, 128, 4])
        sinkrow = din("sinkrow", [DEPTH, 1, 2, 512])
        cosT = din("cosT", [128, NCOL])
        sinT = din("sinT", [128, NCOL])
        bandT = din("bandT", [128, 32, 128])
        cmask = din("cmask", [128, 128 + 512 + 512 + 128])
        if dump:
            hdump = nc.dram_tensor("hdump", [128, 8, NCOL], F32, kind="ExternalOutput").ap()
        outT = nc.dram_tensor("outT", [128, 8, OWN], F32, kind="ExternalOutput").ap()

        A = Arena(nc, es, 212480, "arena")
        pos = [0]

        def alloc(dtype, shape, dma=False):
            esz = 4 if dtype == F32 else 2
            nb = int(np.prod(shape)) * esz
            al = 512 if dma else 64
            pos[0] = (pos[0] + al - 1) // al * al
            v = A.view(pos[0], dtype, shape)
            pos[0] += (nb + al - 1) // al * al
            return v

        hT = alloc(F32, [8, NCOL], dma=True)
        cTs = alloc(F32, [8, 2], dma=True)
        bm_l = [alloc(F32, [72], dma=True) for _ in range(2)]
        gn_l = [alloc(F32, [3, 8], dma=True) for _ in range(2)]
        gf = alloc(F32, [8], dma=True)
        ones = alloc(BF16, [128])
        sc = alloc(BF16, [8, 2])
        modraw_l = [alloc(F32, [72, 2]) for _ in range(2)]
        Amod_l = [alloc(F32, [3, 8, 2]) for _ in range(2)]
        gate_l = [alloc(F32, [3, 8, 2]) for _ in range(2)]
        epsb = alloc(F32, [2])
        NSQ = 4
        sqr = [alloc(BF16, [512], dma=True) for i in range(NSQ)]
        tmp_s = [alloc(F32, [512], dma=True) for i in range(2)]
        rstd = alloc(F32, [512], dma=True)
        PH2 = pos[0]
        nTb = alloc(BF16, [8, 1280], dma=True)
        gT_off = pos[0]
        gTb = alloc(BF16, [NJ, 1280], dma=True)
        wab_s = [alloc(BF16, [2, 8, 128], dma=True) for i in range(3)]
        wo_s = [alloc(BF16, [NJ, 128], dma=True) for i in range(2)]
        sa_s = [alloc(F32, [512], dma=True) for i in range(2)]
        NWM = 3
        wm_s = [alloc(BF16, [8, 128], dma=True) for i in range(NWM)]
        ofin_s = [alloc(F32, [512], dma=True) for i in range(2)]
        assert pos[0] <= A.nbytes, pos[0]
        pos[0] = PH2
        poolout = alloc(BF16, [4, 2432], dma=True)
        nTt = alloc(BF16, [8, 512], dma=True)
        ident = alloc(BF16, [128], dma=True)
        mbp = alloc(BF16, [512], dma=True)
        mbn = alloc(BF16, [512], dma=True)
        sinkl = alloc(BF16, [128], dma=True)
        esink = alloc(BF16, [2, 512], dma=True)
        MG = pos[0]
        kT = alloc(BF16, [2, NCOL], dma=True)
        Vaug = alloc(BF16, [20, 2, 128], dma=True)
        cst = alloc(F32, [2, 512], dma=True)
        rt = [alloc(F32, [512], dma=True) for i in range(2)]
        MQ = pos[0]
        wq = alloc(BF16, [8, 1024], dma=True)
        wou = alloc(BF16, [8, D], dma=True)
        qz = alloc(BF16, [8, 512], dma=True)
        aot = alloc(BF16, [4, 512], dma=True)
        NPT = 8
        PT = [alloc(BF16, [512], dma=True) for i in range(NPT)]
        rcp = rt[0]
        assert pos[0] <= A.nbytes, pos[0]
        pos[0] = MQ
        wkv = alloc(BF16, [8, 640], dma=True)
        wu = alloc(BF16, [8, 512], dma=True)
        wpl = alloc(BF16, [4, 128], dma=True)
        psc = alloc(F32, [4], dma=True)
        band = alloc(BF16, [32, 128], dma=True)
        utok = alloc(BF16, [20, 512], dma=True)
        pooled = alloc(BF16, [4, 512], dma=True)
        assert pos[0] <= A.nbytes, pos[0]

        ps = [es.enter_context(nc.psum_tensor("ps%d" % i, [128, 512], F32)) for i in range(8)]
        pa, pb, py, pss, pmisc = ps[0:2], ps[2:4], ps[4:6], ps[6], ps[7]

        S.op("dve", lambda e: e.memset(ones, 1.0), writes=[("ones",)])
        S.op("dve", lambda e: e.memset(epsb, EPS), writes=[("epsb",)])
        def x_load(eng, c0):
            S.dma(eng, None, hT[:, :, c0:c0 + 512], xT[:, :, c0:c0 + 512],
                  writes=[("h", k, b) for b in range(c0 // 128, c0 // 128 + 4) for k in range(8)])

        x_load("sp", 0)
        S.dma("sp", "d_c", cTs, cT, writes=[("cT",)])
        S.dma("sp", "d_c", gf, gfin, writes=[("gf",)])
        S.op("act", lambda e: e.activation(sc, cTs, AF.Silu), reads=[("cT",)], writes=[("sc",)])

        wm_cnt = [0]
        curp = {"par": 0}

        def mod_load(l):
            par = l % 2
            S.dma("sp", None, bm_l[par], bmod[l], writes=[("bm", par)])
            S.dma("sp", None, gn_l[par], gains[l], writes=[("gn", par)])

        def mod_chunk(l, idx):
            slot = wm_cnt[0] % NWM
            wm_cnt[0] += 1
            S.dma("pool", None, wm_s[slot], wmod[l, idx].rearrange("p (k c) -> p k c", k=8), writes=[("wm", slot)])
            S.mm(pmisc[:, 2 * idx:2 * idx + 2], [(wm_s[slot][:, k, :], sc[:, k, :]) for k in range(8)],
                 reads=[("wm", slot), ("sc",)], writes=[("pm",)])

        def mod_finish(l, m):
            par = l % 2
            pm = pmisc[:, 16 * m:16 * m + 16].rearrange("p (m t) -> p m t", t=2)
            for t in range(2):
                S.op("dve", lambda e, t=t, pm=pm: e.tensor_tensor(
                    modraw_l[par][:, 8 * m:8 * m + 8, t], pm[:, :, t], bm_l[par][:, 8 * m:8 * m + 8], ALU.add),
                    reads=[("pm",), ("bm", par)], writes=[("modraw", par, m)])

        def mod_derive(l, i, what="both"):
            par = l % 2
            modraw, Amod, gate, gn = modraw_l[par], Amod_l[par], gate_l[par], gn_l[par]
            scale = modraw[:, (3 * i + 1) * 8:(3 * i + 2) * 8, :]
            gt = modraw[:, (3 * i + 2) * 8:(3 * i + 3) * 8, :]
            if what in ("both", "A"):
                S.op("dve", lambda e: e.tensor_scalar(Amod[:, i], scale, 1.0, None, ALU.add),
                     reads=[("modraw", par, 3 * i + 1)], writes=[("Amod", par, i)])
                for t in range(2):
                    S.op("dve", lambda e, t=t: e.tensor_tensor(Amod[:, i, :, t], Amod[:, i, :, t], gn[:, i, :], ALU.mult),
                         reads=[("Amod", par, i), ("gn", par)], writes=[("Amod", par, i)])
            if what in ("both", "gate"):
                S.op("dve", lambda e: e.tensor_scalar(gate[:, i], gt, 0.5 if i != 1 else 1.0, None, ALU.mult),
                     reads=[("modraw", par, 3 * i + 2)], writes=[("gate", par, i)])

        def mod_tasks(l, ms):
            tasks = []
            for m in ms:
                for c in range(8):
                    tasks.append(lambda m=m, c=c: mod_chunk(l, 8 * m + c))
                tasks.append(lambda m=m: mod_finish(l, m))
            return tasks

        def hkeys(k, c0, n):
            return [("h", k, b) for b in range(c0 // 128, (c0 + n + 127) // 128)]

        cnt = {"tmp": 0, "sq": 0, "ab": 0, "o": 0, "p": 0, "pt": 0, "pv": 0, "st": 0}

        def rstd_steps(c0, n):
            steps = []

            def sq_step(k):
                si = cnt["sq"] % NSQ
                cnt["sq"] += 1
                S.op("act", lambda e: e.activation(sqr[si][:, 0:n], hT[:, k, c0:c0 + n], AF.Square),
                     reads=hkeys(k, c0, n), writes=[("sq", si)])
                S.mm(pss[:, 0:n], [(ones, sqr[si][:, 0:n])], reads=[("ones",), ("sq", si)],
                     writes=[("pss",)] if k in (0, 7) else [], start=(k == 0), stop=(k == 7))

            def fin():
                S.op("act", lambda e: e.activation(rstd[:, 0:n], pss[:, 0:n], AF.Ln, bias=epsb[:, 0:1], scale=1.0 / D),
                     reads=[("pss",), ("epsb",)], writes=[("rstd",)])
                S.op("act", lambda e: e.activation(rstd[:, 0:n], rstd[:, 0:n], AF.Exp, scale=-0.5),
                     reads=[("rstd",)], writes=[("rstd",)])

            for k in range(8):
                steps.append(lambda k=k: sq_step(k))
            steps.append(fin)
            return steps

        def rstd_of(c0, n):
            for st in rstd_steps(c0, n):
                st()

        def norm_steps(c0, n, i_norm, dst, dst_keys, eng="dve", split=False):
            rsteps = rstd_steps(c0, n)
            steps = []
            shift_i = 3 * i_norm
            par = curp["par"]
            modraw, Amod = modraw_l[par], Amod_l[par]

            def mod_step(k, a, m, t):
                off = a - c0
                ti = cnt["tmp"] % 2
                cnt["tmp"] += 1
                tmp = tmp_s[ti]
                S.op(eng, lambda e: e.tensor_tensor(tmp[:, 0:m], hT[:, k, a:a + m], rstd[:, off:off + m], ALU.mult),
                     reads=hkeys(k, a, m) + [("rstd",)], writes=[("tmp", ti)])
                S.op("act", lambda e: e.activation(
                    dst(k, off, m), tmp[:, 0:m], AF.Identity, bias=modraw[:, shift_i * 8 + k, t:t + 1],
                    scale=Amod[:, i_norm, k, t:t + 1]),
                    reads=[("tmp", ti), ("modraw", par, shift_i), ("Amod", par, i_norm)], writes=dst_keys(k))

            for k in range(8):
                for (a, m, t) in colset_ranges(c0, n):
                    steps.append(lambda k=k, a=a, m=m, t=t: mod_step(k, a, m, t))
            if split:
                return rsteps, steps
            return rsteps + steps

        def norm_mod(c0, n, i_norm, dst, dst_keys, eng="dve"):
            for st in norm_steps(c0, n, i_norm, dst, dst_keys, eng):
                st()

        def interleave(main, side):
            side = list(side)
            nm = max(len(main), 1)
            per = (len(side) + nm - 1) // nm
            for st in main:
                st()
                for _ in range(per):
                    if side:
                        side.pop(0)()
            while side:
                side.pop(0)()

        def resid(pst, c0, n, i_norm, d):
            par = curp["par"]
            gate = gate_l[par]
            for (a, m, t) in colset_ranges(c0, n):
                o2 = a - c0
                S.op("dve", lambda e, a=a, m=m, t=t, o2=o2: e.scalar_tensor_tensor(
                    hT[:, d, a:a + m], pst[1][:, o2:o2 + m], gate[:, i_norm, d, t:t + 1], hT[:, d, a:a + m],
                    ALU.mult, ALU.add),
                    reads=[pst[0], ("gate", par, i_norm)], writes=hkeys(d, a, m))

        def ffn_seg_norm(c0, sn, i_norm):
            steps = []
            for sti, (off, n) in enumerate(subtiles(sn)):
                steps += norm_steps(
                    c0 + off, n, i_norm,
                    lambda k, o2, m, off=off: nTb[:, k, off + o2: off + o2 + m],
                    lambda k, sti=sti: [("nT", k, sti)])
            return steps

        def ffn(l, f, i_norm, segs, extra=(), pre=(), last_seg_extra=(), skip_first_norm=False, tail_side=(),
                after_first_wab=None):
            curp["par"] = l % 2
            for t in pre:
                t()
            extra = list(extra)
            nslots = NJ * len(segs) - (0 if skip_first_norm else 3)
            per = (len(extra) + nslots - 1) // nslots if extra else 0
            lse = list(last_seg_extra)
            per_last = (len(lse) + NJ - 1) // NJ if lse else 0

            def seg_norm(c0, sn):
                return ffn_seg_norm(c0, sn, i_norm)

            first_norms = None
            if not skip_first_norm:
                c00, sn0 = segs[0]
                first_norms = [norm_steps(c00 + off, n, i_norm,
                                          lambda k, o2, m, off=off: nTb[:, k, off + o2: off + o2 + m],
                                          lambda k, sti=sti: [("nT", k, sti)])
                               for sti, (off, n) in enumerate(subtiles(sn0))]
                for t in first_norms[0]:
                    t()
            def wab_load(j):
                slot = cnt["ab"] % 3
                cnt["ab"] += 1
                S.dma("pool", None, wab_s[slot], wab[l, f, j].rearrange("p (a k c) -> p a k c", a=2, k=8),
                      writes=[("wab", slot)])
                return slot

            def in_step(j, slot, sti, off, n):
                pi = cnt["p"] % 2
                cnt["p"] += 1
                nkeys = [("nT", k, sti) for k in range(8)]
                S.mm(pa[pi][:, 0:n], [(wab_s[slot][:, 0, k, :], nTb[:, k, off:off + n]) for k in range(8)],
                     reads=[("wab", slot)] + nkeys, writes=[("pa", pi)])
                S.mm(pb[pi][:, 0:n], [(wab_s[slot][:, 1, k, :], nTb[:, k, off:off + n]) for k in range(8)],
                     reads=[("wab", slot)] + nkeys, writes=[("pb", pi)])
                S.op("act", lambda e: e.activation(sa_s[pi][:, 0:n], pa[pi][:, 0:n], AF.Silu),
                     reads=[("pa", pi)], writes=[("sa", pi)])
                S.op("dve", lambda e: e.tensor_tensor(gTb[:, j, off:off + n], sa_s[pi][:, 0:n], pb[pi][:, 0:n], ALU.mult),
                     reads=[("sa", pi), ("pb", pi)], writes=[("g", j, sti)])

            for si, (c0, sn) in enumerate(segs):
                sts = subtiles(sn)
                j_start = 0
                if si == 0 and first_norms is not None and len(sts) > 1:
                    slots = [wab_load(j) for j in range(3)]
                    if after_first_wab is not None:
                        after_first_wab()
                    for sti, (off, n) in enumerate(sts):
                        side = first_norms[sti + 1] if sti + 1 < len(sts) else []
                        interleave([lambda j=j, sti=sti, off=off, n=n: in_step(j, slots[j], sti, off, n) for j in range(3)], side)
                    j_start = 3
                for j in range(j_start, NJ):
                    slot = wab_load(j)
                    for sti, (off, n) in enumerate(sts):
                        in_step(j, slot, sti, off, n)
                    for _ in range(per):
                        if extra:
                            extra.pop(0)()
                    if si == len(segs) - 1:
                        for _ in range(per_last):
                            if lse:
                                lse.pop(0)()
                nxt = seg_norm(*segs[si + 1]) if si + 1 < len(segs) else list(tail_side)
                main = []

                def out_step(d, sti, off, n, slot):
                    pi = cnt["p"] % 2
                    cnt["p"] += 1
                    S.mm(py[pi][:, 0:n], [(wo_s[slot][:, j, :], gTb[:, j, off:off + n]) for j in range(NJ)],
                         reads=[("wo", slot)] + [("g", j, sti) for j in range(NJ)], writes=[("py", pi)])
                    resid((("py", pi), py[pi]), c0 + off, n, i_norm, d)

                def wo_load(d, slot):
                    S.dma("pool", None, wo_s[slot], wo[l, f, d].rearrange("p (j c) -> p j c", j=NJ),
                          writes=[("wo", slot)])

                for d in range(8):
                    slot = cnt["o"] % 2
                    cnt["o"] += 1
                    for sti, (off, n) in enumerate(sts):
                        if sti == 0:
                            main.append(lambda d=d, sti=sti, off=off, n=n, slot=slot: (wo_load(d, slot), out_step(d, sti, off, n, slot)))
                        else:
                            main.append(lambda d=d, sti=sti, off=off, n=n, slot=slot: out_step(d, sti, off, n, slot))
                interleave(main, nxt)
            while extra:
                extra.pop(0)()
            while lse:
                lse.pop(0)()

        def col2blk(c):
            return c // 128

        def mix(l):
            last = (l == DEPTH - 1)
            curp["par"] = l % 2
            S.dma("pool", "d_cst", ident, cmask[:, 0:128], writes=[("ident",)])
            S.dma("pool", "d_cst", mbp, cmask[:, 128:640], writes=[("mbp",)])
            S.dma("pool", "d_cst", mbn, cmask[:, 640:1152], writes=[("mbn",)])
            S.dma("pool", "d_cst", sinkl[0:1, :], cmask[0:1, 1152:1280], writes=[("sinkl",)])
            S.dma("pool", "d_cst", wpl, wpool[l], writes=[("wpl",)])
            S.dma("sp", "d_c", psc, pscale[l], writes=[("psc",)])
            for g in range(2):
                S.dma("sp", "d_c", tmp_s[g][0:1, :], sinkrow[l, :, g, :], writes=[("tmp", g)])
                S.op("act", lambda e, g=g: e.activation(esink[0:1, g, :], tmp_s[g][0:1, :], AF.Exp),
                     reads=[("tmp", g)], writes=[("esink",)])

            S.dma("pool", None, wkv, win[l, :, :, 512:WCA], writes=[("wkv",)])
            S.dma("pool", None, wu, win[l, :, :, 0:512], writes=[("wu",)])
            S.op("dve", lambda e: e.memset(Vaug[:, :, :, 64:128], 1.0), writes=[("Vaug",)])
            acols = [(0, 512), (512, 512), (1024, 512), (1536, 512), (2048, 384) if last else (2048, 512)]
            ev_i = [0]

            def evac(dst, src, reads, writes):
                ev_i[0] += 1
                if ev_i[0] % 2:
                    S.op("act", lambda e: e.copy(dst, src), reads=reads, writes=writes)
                else:
                    S.op("dve", lambda e: e.tensor_copy(dst, src), reads=reads, writes=writes)

            nTt2 = band.rearrange("p a b -> p (a b)").rearrange("p (k c) -> p k c", k=8)
            nbufs = [nTt, nTt2]

            def nkeys(ti):
                return [("nTt", k) for k in range(8)] if ti % 2 == 0 else [("band",)]

            def a_norm(ti, c0, n):
                nb = nbufs[ti % 2]
                return norm_steps(c0, n, 1, lambda k, o2, m: nb[:, k, o2:o2 + m],
                                  (lambda k: [("nTt", k)]) if ti % 2 == 0 else (lambda k: [("band",)]), eng="pool")

            for st in a_norm(0, *acols[0]):
                st()
            for ti, (c0, n) in enumerate(acols):
                nTt_ = nbufs[ti % 2]
                nk = nkeys(ti)
                def cs_load(n=n, c0=c0):
                    S.dma("sp", None, cst[:, 0, 0:n], cosT[:, c0:c0 + n], writes=[("cs", 0)])
                    S.dma("sp", None, cst[:, 1, 0:n], sinT[:, c0:c0 + n], writes=[("cs", 1)])

                def k_step(g, n=n, c0=c0, nTt=nTt_, nk=nk):
                    S.mm(pa[g][:, 0:n], [(wkv[:, k, g * 128:(g + 1) * 128], nTt[:, k, 0:n]) for k in range(8)],
                         reads=[("wkv",)] + nk, writes=[("pa", g)])
                    S.mm(pb[g][:, 0:n], [(wkv[:, k, 256 + g * 128:256 + (g + 1) * 128], nTt[:, k, 0:n]) for k in range(8)],
                         reads=[("wkv",)] + nk, writes=[("pb", g)])
                    S.op("dve", lambda e: e.tensor_tensor(rt[0][:, 0:n], pa[g][:, 0:n], cst[:, 0, 0:n], ALU.mult),
                         reads=[("pa", g), ("cs", 0)], writes=[("rt", 0)])
                    S.op("dve", lambda e: e.tensor_tensor(rt[1][:, 0:n], pb[g][:, 0:n], cst[:, 1, 0:n], ALU.mult),
                         reads=[("pb", g), ("cs", 1)], writes=[("rt", 1)])
                    S.op("dve", lambda e: e.tensor_tensor(kT[:, g, c0:c0 + n], rt[0][:, 0:n], rt[1][:, 0:n], ALU.add),
                         reads=[("rt", 0), ("rt", 1)], writes=[("kT", g, c0)])

                def v_step(tb, c0=c0, nTt=nTt_, nk=nk):
                    blk = (c0 + tb * 128) // 128
                    pi = cnt["p"] % 2
                    cnt["p"] += 1
                    S.mm(py[pi][:, 0:128], [(nTt[:, k, tb * 128:(tb + 1) * 128], wkv[:, k, 512:640]) for k in range(8)],
                         reads=[("wkv",)] + nk, writes=[("py", pi)])
                    S.op("act", lambda e: e.copy(
                        Vaug[:, blk, :, 0:64], py[pi][:, 0:128].rearrange("p (g d) -> p g d", g=2)),
                        reads=[("py", pi)], writes=[("Vaug",)])

                def utok_step(tb, c0=c0, nTt=nTt_, nk=nk):
                    blk = c0 // 128 + tb
                    pi = cnt["p"] % 2
                    cnt["p"] += 1
                    S.mm(py[pi][:, :], [(nTt[:, k, tb * 128:(tb + 1) * 128], wu[:, k, :]) for k in range(8)],
                         reads=[("wu",)] + nk, writes=[("py", pi)])
                    S.op("dve", lambda e: e.tensor_copy(utok[:, blk, :], py[pi][:, :]), reads=[("py", pi)], writes=[("utok", blk)])

                cs_load()
                nb_ = n // 128
                main = [lambda tb=tb: utok_step(tb) for tb in range(nb_)]
                main += [lambda g=g: k_step(g) for g in range(2)]
                main += [lambda tb=tb: v_step(tb) for tb in range(nb_)]
                side = a_norm(ti + 1, *acols[ti + 1]) if ti + 1 < len(acols) else []
                interleave(main, side)
            S.dma("pool", None, band, bandT, writes=[("band",)])
            S.dma("pool", None, wq, win[l, :, :, WCA:WCOLS], writes=[("wq",), ("wkv",), ("wu",)])
            oblocks = list(range(16)) if last else list(range(19))
            for t0 in range(0, len(oblocks), 4):
                obs = oblocks[t0:t0 + 4]
                n = 128 * len(obs)
                for g in range(4):
                    pbank, pkey = [(pa[0], ("pa", 0)), (pa[1], ("pa", 1)), (pb[0], ("pb", 0)), (pb[1], ("pb", 1))][g]
                    pairs, rd = [], [("band",)]
                    for oi, ob in enumerate(obs):
                        if ob <= 16:
                            srcs = ([(ob - 1, 0)] if ob >= 1 else []) + [(ob, 3 if ob == 0 else 1), (ob + 1 if ob < 16 else 19, 2)]
                        elif ob == 17:
                            srcs = [(17, 6), (18, 5)]
                        else:
                            srcs = [(17, 4), (18, 7)]
                        for si, (sb, kind) in enumerate(srcs):
                            pairs.append((utok[:, sb, g * 128:(g + 1) * 128], band[:, g * 8 + kind, :],
                                          pbank[:, oi * 128:(oi + 1) * 128], si == 0, si == len(srcs) - 1))
                            rd.append(("utok", sb))
                    S.mm(pbank[:, 0:n], pairs, reads=rd, writes=[pkey])
                    S.op("dve", lambda e, g=g, n=n, pbank=pbank: e.tensor_copy(pooled[:, g, 0:n], pbank[:, 0:n]),
                         reads=[pkey], writes=[("pooled", g)])
                for g in range(4):
                    pj = cnt["p"] % 2
                    cnt["p"] += 1
                    S.mm(py[pj][:, 0:n], [(wpl[:, g, :], pooled[:, g, 0:n])], reads=[("wpl",), ("pooled", g)], writes=[("py", pj)])
                    oc = obs[0] * 128
                    S.op("act", lambda e, pj=pj, g=g, n=n, oc=oc: e.activation(
                        poolout[:, g, oc:oc + n], py[pj][:, 0:n], AF.Identity, scale=psc[:, g:g + 1]),
                        reads=[("py", pj), ("psc",)], writes=[("poolout", g, oc)])
            qtiles = [(0, 512), (512, 512), (1024, 512), (1536, 512)]
            if not last:
                qtiles.append((2048, 384))
            for st in norm_steps(qtiles[0][0], qtiles[0][1], 1, lambda k, o2, m: nTt[:, k, o2:o2 + m], lambda k: [("nTt", k)], eng="pool"):
                st()
            S.barrier()

            if DEBUG_MIX_STOP[0] in ("P", "KV"):
                return
            S.dma("pool", None, wou, wout[l], writes=[("wou",)])
            S.op("dve", lambda e: e.memset(qz, 0.0), writes=[("qz", h) for h in range(8)])
            pst = [pb[0], pb[1], py[0]]
            pstk = [("pb", 0), ("pb", 1), ("py", 0)]
            pvb = [py[1], pmisc]
            pvk = [("py", 1), ("pmisc",)]

            def q_norm(c0, n):
                return norm_steps(c0, n, 1, lambda k, o2, m: nTt[:, k, o2:o2 + m], lambda k: [("nTt", k)], eng="pool")

            def emit_scores(item, sel):
                (qs, g, kbs, pts) = item
                for (kb, mb, mbkey) in kbs[sel]:
                    si = cnt["st"] % 3
                    cnt["st"] += 1
                    stp, stkey = pst[si], pstk[si]
                    pairs = []
                    rd = [("qz", 4 * g + hh) for hh in range(4)]
                    for hh in range(4):
                        pairs.append((kT[:, g, kb * 128:(kb + 1) * 128], qz[:, 4 * g + hh, qs],
                                      stp[:, hh * 128:(hh + 1) * 128], True, True))
                    S.mm(stp[:, :], pairs, reads=rd, writes=[stkey])
                    pti = cnt["pt"] % NPT
                    cnt["pt"] += 1
                    S.op("act", lambda e, stp=stp, pti=pti: e.activation(PT[pti], stp[:, :], AF.Exp, scale=0.125),
                         reads=[stkey], writes=[("PT", pti)])
                    if mb is not None:
                        S.op("dve", lambda e, pti=pti, mb=mb: e.tensor_tensor(PT[pti], PT[pti], mb, ALU.mult),
                             reads=[("PT", pti), mbkey], writes=[("PT", pti)])
                    pts.append((kb, pti))

            def emit_pv(item):
                (qs, g, kbs, pts) = item
                pvi = cnt["pv"] % 2
                cnt["pv"] += 1
                pvp, pvkey = pvb[pvi], pvk[pvi]
                pairs = [(Vaug[:, kb, g, :], PT[pti]) for (kb, pti) in pts]
                pairs.append((sinkl[0:1, :], esink[0:1, g, :]))
                S.mm(pvp[:, :], pairs, reads=[("Vaug",), ("sinkl",), ("esink",)] + [("PT", pti) for (_, pti) in pts],
                     writes=[pvkey])
                return (pvp, pvkey, qs, g)

            def emit_normalise(pvinfo):
                (pvp, pvkey, qs, g) = pvinfo
                S.op("act", lambda e: e.activation(rcp[64:128, :], pvp[64:128, :], AF.Ln), reads=[pvkey], writes=[("rt", 0)])
                S.op("act", lambda e: e.activation(rcp[64:128, :], rcp[64:128, :], AF.Exp, scale=-1.0),
                     reads=[("rt", 0)], writes=[("rt", 0)])
                for hf in range(2):
                    i0v = pvp[0:64, :].rearrange("p (c t q) -> p c t q", c=2, t=2)[:, :, hf, :]
                    i1v = rcp[64:128, :].rearrange("p (c t q) -> p c t q", c=2, t=2)[:, :, hf, :]
                    ov = aot[64 * hf:64 * hf + 64, 2 * g:2 * g + 2, qs]
                    S.op("dve", lambda e, i0v=i0v, i1v=i1v, ov=ov: e.tensor_tensor(ov, i0v, i1v, ALU.mult),
                         reads=[pvkey, ("rt", 0)], writes=[("aot", 2 * g), ("aot", 2 * g + 1)])

            def wout_step(c0, n, d):
                pairs = [(wou[:, c, d * 128:(d + 1) * 128], poolout[:, c, c0:c0 + n]) for c in range(4)]
                pairs += [(wou[:, 4 + c, d * 128:(d + 1) * 128], aot[:, c, 0:n]) for c in range(4)]
                pi = d % 2
                rd = [("wou",)] + [("aot", c) for c in range(4)]
                S.mm(pa[pi][:, 0:n], pairs, reads=rd, writes=[("pa", pi)])
                resid((("pa", pi), pa[pi]), c0, n, 1, d)

            for ti, (c0, n) in enumerate(qtiles):
                S.dma("sp", None, cst[:, 0, 0:n], cosT[:, c0:c0 + n], writes=[("cs", 0)])
                S.dma("sp", None, cst[:, 1, 0:n], sinT[:, c0:c0 + n], writes=[("cs", 1)])
                nk = [("nTt", k) for k in range(8)]
                for c in range(4):
                    S.mm(pa[0][:, 0:n], [(wq[:, k, c * 128:(c + 1) * 128], nTt[:, k, 0:n]) for k in range(8)],
                         reads=[("wq",)] + nk, writes=[("pa", 0)])
                    S.mm(pa[1][:, 0:n], [(wq[:, k, 512 + c * 128:512 + (c + 1) * 128], nTt[:, k, 0:n]) for k in range(8)],
                         reads=[("wq",)] + nk, writes=[("pa", 1)])
                    S.op("dve", lambda e, n=n: e.tensor_tensor(rt[0][:, 0:n], pa[0][:, 0:n], cst[:, 0, 0:n], ALU.mult),
                         reads=[("pa", 0), ("cs", 0)], writes=[("rt", 0)])
                    S.op("dve", lambda e, n=n: e.tensor_tensor(rt[1][:, 0:n], pa[1][:, 0:n], cst[:, 1, 0:n], ALU.mult),
                         reads=[("pa", 1), ("cs", 1)], writes=[("rt", 1)])
                    for hf in range(2):
                        rows = slice(64 * hf, 64 * hf + 64)
                        S.op("pool", lambda e, c=c, n=n, hf=hf, rows=rows: e.tensor_tensor(
                            qz[rows, 2 * c + hf, 0:n], rt[0][rows, 0:n], rt[1][rows, 0:n], ALU.add),
                            reads=[("rt", 0), ("rt", 1)], writes=[("qz", 2 * c + hf)])
                items = []
                for qb in range(n // 128):
                    qblk = (c0 + qb * 128) // 128
                    qs = slice(qb * 128, (qb + 1) * 128)
                    if qblk <= 16:
                        kbs = []
                        if qblk >= 1:
                            kbs.append((qblk - 1, mbp, ("mbp",)))
                        kbs.append((qblk, None, None))
                        nxt = qblk + 1 if qblk < 16 else 19
                        kbs.append((nxt, mbn, ("mbn",)))
                        kbs += [(17, None, None), (18, None, None)]
                    else:
                        kbs = [(17, None, None), (18, None, None)]
                    for g in range(2):
                        items.append((qs, g, kbs, []))
                emit_scores(items[0], slice(0, None))
                for ii, item in enumerate(items):
                    if ii + 1 < len(items):
                        emit_scores(items[ii + 1], slice(0, 2))
                    pvinfo = emit_pv(item)
                    if ii + 1 < len(items):
                        emit_scores(items[ii + 1], slice(2, None))
                    emit_normalise(pvinfo)
                side = q_norm(*qtiles[ti + 1]) if ti + 1 < len(qtiles) else []
                for st in side[:9]:
                    st()
                interleave([lambda d=d: wout_step(c0, n, d) for d in range(8)], side[9:])
            S.barrier()

        fin_cnt = [0]

        def final_tasks(tiles):
            tasks = []
            for (c0, n) in tiles:
                tasks += rstd_steps(c0, n)

                def out_step(k, c0=c0, n=n):
                    slot = fin_cnt[0] % 2
                    fin_cnt[0] += 1
                    S.op("dve", lambda e: e.scalar_tensor_tensor(
                        ofin_s[slot][:, 0:n], hT[:, k, c0:c0 + n], gf[:, k:k + 1], rstd[:, 0:n], ALU.mult, ALU.mult),
                        reads=hkeys(k, c0, n) + [("gf",), ("rstd",)], writes=[("ofin", slot)])
                    evs.append(S.dma("sp", "d_ofin%d" % slot, outT[:, k, c0:c0 + n], ofin_s[slot][:, 0:n], reads=[("ofin", slot)]))

                tasks += [lambda k=k, f=out_step: f(k) for k in range(8)]
            return tasks

        def final():
            for t in final_tasks([(1024, 512), (1536, 512)]):
                t()

        evs = []
        def l0_first():
            mod_load(0)
            pre = mod_tasks(0, [1, 0]) + [lambda: mod_derive(0, 0, "A")]
            extra = (mod_tasks(0, [2]) + [lambda: mod_derive(0, 0, "gate")]
                     + mod_tasks(0, [4, 3, 5, 7, 6, 8]) + [lambda: mod_derive(0, 1), lambda: mod_derive(0, 2)])
            ffn(0, 0, 0, [(0, 1280), (1280, 1280)], extra=extra, pre=pre,
                after_first_wab=lambda: [x_load("pool", c0) for c0 in range(512, NCOL, 512)])
            S.barrier()

        def l0_ffn2():
            mod_load(1)
            extra = mod_tasks(1, list(range(9))) + [lambda i=i: mod_derive(1, i) for i in range(3)]
            curp["par"] = 1
            nxt_norm = ffn_seg_norm(0, 1280, 0)
            curp["par"] = 0
            ffn(0, 1, 2, [(0, 1280), (1280, 1152)], extra=extra, tail_side=nxt_norm)

        prog = [
            l0_first,
            lambda: mix(0),
            l0_ffn2,
            lambda: (ffn(1, 0, 0, [(0, 1280), (1280, 1152)], skip_first_norm=True), S.barrier()),
            lambda: mix(1),
            lambda: ffn(1, 1, 2, [(0, 1024), (1024, 1024)], last_seg_extra=final_tasks([(0, 512), (512, 512)])),
            lambda: final(),
        ]
        for i in range(nstage + 1):
            prog[i]()

        if dump:
            for k in range(8):
                evs.append(S.dma("sp", "d_out", hdump[:, k, :], hT[:, k, :], reads=[("h", k, b) for b in range(20)]))
        if upto != "final":
            for k in range(8):
                evs.append(S.dma("sp", "d_out", outT[:, k, :], hT[:, k, 0:OWN], reads=[("h", k, b) for b in range(16)]))
        S.wait_all("sp", evs)

        with nc.Block() as block:
            @block.tensor
            def _(e):
                for f in S.ops["pe"]:
                    f(e)

            @block.scalar
            def _(e):
                for f in S.ops["act"]:
                    f(e)

            @block.vector
            def _(e):
                for f in S.ops["dve"]:
                    f(e)

            @block.gpsimd
            def _(e):
                for f in S.ops["pool"]:
                    f(e)

            @block.sync
            def _(e):
                for f in S.ops["sp"]:
                    f(e)
        print("instr counts", S.nins, {k: v for k, v in S.cnt.items()})
    return nc


def core_columns(half):
    p = np.arange(NL)
    return p if half == 0 else (SEQ - 1 - p)


def _kmajor(w):
    return w.reshape(8, 128, w.shape[-1]).transpose(1, 0, 2)


def prep_shared(inp):
    sh = {}
    sh["wmod"] = np.ascontiguousarray(
        inp["w_mod"].reshape(DEPTH, 8, 128, 72, 128).transpose(0, 3, 2, 1, 4).reshape(DEPTH, 72, 128, 8 * 128))
    sh["bmod"] = np.ascontiguousarray(inp["b_mod"].reshape(DEPTH, 72, 128).transpose(0, 2, 1))
    g = np.stack([inp["norm_ffn1"], inp["norm_mix"], inp["norm_ffn2"]], axis=1)
    sh["gains"] = np.ascontiguousarray(g.reshape(DEPTH, 3, 8, 128).transpose(0, 3, 1, 2))
    sh["gfin"] = np.ascontiguousarray(inp["norm_final"].reshape(8, 128).T)
    wab, wo = [], []
    for name_in, name_out in (("w_ffn1_in", "w_ffn1_out"), ("w_ffn2_in", "w_ffn2_out")):
        w = inp[name_in].reshape(DEPTH, 8, 128, 2, NJ, 128)
        wab.append(w.transpose(0, 4, 2, 3, 1, 5).reshape(DEPTH, NJ, 128, 2 * 8 * 128))
        w2 = inp[name_out].reshape(DEPTH, NJ, 128, 8, 128)
        wo.append(w2.transpose(0, 3, 2, 1, 4).reshape(DEPTH, 8, 128, NJ * 128))
    sh["wab"] = np.ascontiguousarray(np.stack(wab, axis=1))
    sh["wo"] = np.ascontiguousarray(np.stack(wo, axis=1))
    wins, wouts = [], []
    for l in range(DEPTH):
        w = inp["w_in"][l]
        u, q, k, v = w[:, 0:512], w[:, 512:1024], w[:, 1024:1152], w[:, 1152:1280]
        qsw = q.reshape(D, 8, 2, 32)[:, :, ::-1, :].reshape(D, 512)
        k3 = k.reshape(D, 2, 64)
        ksw3 = k.reshape(D, 2, 2, 32)[:, :, ::-1, :].reshape(D, 2, 64)
        kdup = np.concatenate([k3, k3], axis=-1).reshape(D, 256)
        kswdup = np.concatenate([ksw3, ksw3], axis=-1).reshape(D, 256)
        cols = np.concatenate([u, kdup, kswdup, v, q, qsw], axis=1)
        wins.append(_kmajor(cols))
        wouts.append(_kmajor(inp["w_out"][l]))
    sh["win"] = np.ascontiguousarray(np.stack(wins))
    sh["wout"] = np.ascontiguousarray(np.stack(wouts))
    sh["wpool"] = np.ascontiguousarray(inp["w_pool"].transpose(0, 2, 1, 3))
    sh["pscale"] = np.ascontiguousarray(inp["pool_scale"].reshape(DEPTH, 4, 128).transpose(0, 2, 1))
    sh["sinkrow"] = np.ascontiguousarray(
        np.repeat(inp["sink"].reshape(DEPTH, 2, 4), 128, axis=2).reshape(DEPTH, 1, 2, 512))
    cm = np.zeros((128, 1280), np.float32)
    cm[:, 0:128] = np.eye(128, dtype=np.float32)
    ki = np.arange(128)[:, None]
    qi = np.arange(128)[None, :]
    cm[:, 128:640] = np.tile(np.where(qi <= ki, 1.0, 0.0), (1, 4))
    cm[:, 640:1152] = np.tile(np.where(ki <= qi, 1.0, 0.0), (1, 4))
    cm[0, 1152 + 64:1280] = 1.0
    sh["cmask"] = cm
    return sh


def _band(out_pos, src_pos, offs, lo, hi):
    P = np.asarray(out_pos)[None, :]
    Q = np.asarray(src_pos)[:, None]
    offs = np.asarray(offs)
    win = P[None, :, :] + offs[:, None, None]
    ok = (win >= lo) & (win < hi)
    cntv = ok.sum(axis=0)
    hit = ((win == Q[None, :, :]) & ok).any(axis=0)
    return hit / cntv - (Q == P)


_BAND_CACHE = {}


def _band_tables(half):
    if half in _BAND_CACHE:
        return _BAND_CACHE[half]
    _BAND_CACHE[half] = _band_tables_build(half)
    return _BAND_CACHE[half]


def _band_tables_build(half):
    T = np.zeros((128, 32, 128), np.float32)
    blk = lambda b: list(range(128 * b, 128 * b + 128))
    for g, w in enumerate((2, 4, 8, 16)):
        fwd = list(range(-(w // 2), w - w // 2))
        offs = fwd if half == 0 else [-o for o in fwd]
        big = 1 << 30
        T[:, g * 8 + 0] = _band(blk(2), blk(1), offs, 0, big)
        T[:, g * 8 + 1] = _band(blk(2), blk(2), offs, 0, big)
        T[:, g * 8 + 2] = _band(blk(2), blk(3), offs, 0, big)
        T[:, g * 8 + 3] = _band(blk(0), blk(0), offs, 0, big)
        T[:, g * 8 + 4] = _band(blk(1), blk(0), fwd, 0, 256)
        T[:, g * 8 + 5] = _band(blk(0), blk(1), fwd, 0, 256)
        T[:, g * 8 + 6] = _band(blk(0), blk(0), fwd, 0, 256)
        T[:, g * 8 + 7] = _band(blk(1), blk(1), fwd, 0, 256)
    return T


def prep_core(inp, core):
    b, half = core // 2, core % 2
    tok = core_columns(half)
    xl = inp["x"][b][tok]
    cols = np.concatenate([xl[:CTX0], inp["ctx"][b], xl[CTX0:]], axis=0)
    m = {}
    m["xT"] = np.ascontiguousarray(cols.T.reshape(8, 128, NCOL).transpose(1, 0, 2))
    cc = np.stack([inp["c"][b], inp["c_ctx"]], axis=-1)
    m["cT"] = np.ascontiguousarray(cc.reshape(8, 128, 2).transpose(1, 0, 2))
    inv = 10000.0 ** (-np.arange(0, 32, 2, dtype=np.float64) / 32.0)
    ang = np.concatenate([(tok // 64)[:, None] * inv, (tok % 64)[:, None] * inv], axis=-1)
    cos_l, sin_l = np.cos(ang).T, np.sin(ang).T
    cosc = np.ones((32, NCOL)); sinc = np.zeros((32, NCOL))
    cosc[:, :CTX0] = cos_l[:, :CTX0]; cosc[:, H2C0:] = cos_l[:, CTX0:]
    sinc[:, :CTX0] = sin_l[:, :CTX0]; sinc[:, H2C0:] = sin_l[:, CTX0:]
    m["cosT"] = np.ascontiguousarray(np.tile(cosc, (4, 1)).astype(np.float32))
    m["sinT"] = np.ascontiguousarray(np.concatenate([-sinc, sinc, -sinc, sinc], axis=0).astype(np.float32))
    m["bandT"] = _band_tables(half)
    return m


_NC_CACHE = {}


def kernel(**inputs):
    inp = {k: np.asarray(v) for k, v in inputs.items()}
    if "nc" not in _NC_CACHE:
        _NC_CACHE["nc"] = build()
    nc = _NC_CACHE["nc"]
    sh = prep_shared(inp)
    in_maps = []
    for core in range(8):
        m = dict(sh)
        m.update(prep_core(inp, core))
        in_maps.append(m)
    res = run_bass_kernel_spmd(nc, in_maps, core_ids=list(range(8)))
    out = np.empty((4, SEQ, D), np.float32)
    for core in range(8):
        b, half = core // 2, core % 2
        o = res.results[core]["outT"]
        rows = o.transpose(2, 1, 0).reshape(OWN, D)
        out[b, core_columns(half)[:OWN]] = rows
    return out
```

```python
import numpy as np
from contextlib import ExitStack
import concourse.bass as bass
import concourse.mybir as mybir
from concourse.bass_utils import run_bass_kernel_spmd

F32 = mybir.dt.float32
BF16 = mybir.dt.bfloat16
AF = mybir.ActivationFunctionType
ALU = mybir.AluOpType

D = 1024
DFF = 2816
NJ = DFF // 128
SEQ = 4096
OWN = 2048
NL = 2304
NCOL = 2560
CTX0 = 2176
H2C0 = 2432
EPS = 1e-6
DEPTH = 2
NEG = -30000.0


def colset_ranges(c0, n):
    out = []
    for (a, b, s) in ((0, CTX0, 0), (CTX0, H2C0, 1), (H2C0, NCOL, 0)):
        lo, hi = max(a, c0), min(b, c0 + n)
        if hi > lo:
            out.append((lo, hi - lo, s))
    return out


class Sched:
    ENG = ["pe", "act", "dve", "pool", "sp"]

    def __init__(self, nc, es):
        self.nc, self.es = nc, es
        self.ops = {e: [] for e in self.ENG}
        self.sems, self.cnt = {}, {}
        self.seen = {e: {} for e in self.ENG}
        self.res = {}
        self.nins = {e: 0 for e in self.ENG}

    def sem(self, name):
        if name not in self.sems:
            self.sems[name] = self.es.enter_context(self.nc.semaphore(name))
            self.cnt[name] = 0
        return self.sems[name]

    def _deps(self, reads, writes):
        d = {}
        for r in reads:
            st = self.res.get(r)
            if st and st[0] is not None:
                s, v = st[0]
                if d.get(s, 0) < v:
                    d[s] = v
        for w in writes:
            st = self.res.get(w)
            if st:
                if st[0] is not None:
                    s, v = st[0]
                    if d.get(s, 0) < v:
                        d[s] = v
                for s, v in st[1].items():
                    if d.get(s, 0) < v:
                        d[s] = v
        return d

    def _waits(self, eng, d):
        seen = self.seen[eng]
        for s, v in d.items():
            if seen.get(s, 0) < v:
                seen[s] = v
                h = self.sems[s]
                self.ops[eng].append(lambda e, h=h, v=v: e.wait_ge(h, v))

    def _commit(self, ev, reads, writes):
        s, v = ev
        for r in reads:
            st = self.res.setdefault(r, [None, {}])
            if st[1].get(s, 0) < v:
                st[1][s] = v
        for w in writes:
            self.res[w] = [ev, {}]

    def op(self, eng, fn, reads=(), writes=()):
        self._waits(eng, self._deps(reads, writes))
        sname = "s_" + eng
        h = self.sem(sname)
        self.cnt[sname] += 1
        ev = (sname, self.cnt[sname])
        self.ops[eng].append(lambda e, fn=fn, h=h: fn(e).then_inc(h, 1))
        self.nins[eng] += 1
        self._commit(ev, reads, writes)

    def mm(self, out, pairs, reads, writes, start=True, stop=True):
        self._waits("pe", self._deps(reads, writes))
        h = self.sem("s_pe")
        self.cnt["s_pe"] += 1
        ev = ("s_pe", self.cnt["s_pe"])
        n = len(pairs)
        for i, pr in enumerate(pairs):
            l, r = pr[0], pr[1]
            o_i = pr[2] if len(pr) > 2 else out
            st_i = pr[3] if len(pr) > 3 else (start and i == 0)
            sp_i = pr[4] if len(pr) > 4 else (stop and (i == n - 1))
            if i == n - 1:
                self.ops["pe"].append(lambda e, l=l, r=r, o_i=o_i, st_i=st_i, sp_i=sp_i: e.matmul(o_i, l, r, start=st_i, stop=sp_i).then_inc(h, 1))
            else:
                self.ops["pe"].append(lambda e, l=l, r=r, o_i=o_i, st_i=st_i, sp_i=sp_i: e.matmul(o_i, l, r, start=st_i, stop=sp_i))
        self.nins["pe"] += n
        self._commit(ev, reads, writes)

    def barrier(self):
        d = {s: v for s, v in self.cnt.items() if v > 0}
        for eng in self.ENG:
            self._waits(eng, dict(d))
        self.res = {}

    def dma(self, eng, sname, out, in_, reads=(), writes=()):
        if writes:
            sname = "d_" + "_".join(str(x) for x in writes[0])
        assert sname is not None
        self._waits(eng, self._deps(reads, writes))
        h = self.sem(sname)
        self.cnt[sname] += 16
        ev = (sname, self.cnt[sname])
        self.ops[eng].append(lambda e: e.dma_start(out=out, in_=in_).then_inc(h, 16))
        self._commit(ev, reads, writes)
        return ev

    def wait_all(self, eng, evs):
        d = {}
        for s, v in evs:
            d[s] = max(d.get(s, 0), v)
        self._waits(eng, d)


class Arena:
    def __init__(self, nc, es, nbytes, name):
        self.t = es.enter_context(nc.sbuf_tensor(name, [128, nbytes // 4], F32))
        self.nbytes = nbytes

    def view(self, off, dtype, shape):
        esz = 4 if dtype == F32 else 2
        n = int(np.prod(shape))
        assert off % 4 == 0 and off + n * esz <= self.nbytes, (off, n, esz, self.nbytes)
        ap = self.t[:, off // 4: off // 4 + (n * esz + 3) // 4]
        if dtype != F32:
            ap = ap.bitcast(dtype)
        if len(shape) == 2:
            ap = ap.rearrange("p (a b) -> p a b", a=shape[0])
        elif len(shape) == 3:
            ap = ap.rearrange("p (a b c) -> p a b c", a=shape[0], b=shape[1])
        return ap


def subtiles(n, step=512):
    out, off = [], 0
    while off < n:
        m = min(step, n - off)
        out.append((off, m))
        off += m
    return out


WCA = 512 + 256 + 256 + 128
WCOLS = WCA + 1024
ULAT = 8 + 2192 + 8
UCTX = 8 + 256 + 8
STAGES = ["l0ffn1", "l0mix", "l0ffn2", "l1ffn1", "l1mix", "l1ffn2", "final"]
DEBUG_MIX_STOP = [None]


def build(upto="final", dump=False):
    nc = bass.Bass("TRN2", target_bir_lowering=False)
    es = ExitStack()
    nstage = STAGES.index(upto)
    with es:
        S = Sched(nc, es)

        def din(name, shape):
            return nc.dram_tensor(name, list(shape), F32, kind="ExternalInput").ap()

        xT = din("xT", [128, 8, NCOL])
        cT = din("cT", [128, 8, 2])
        wmod = din("wmod", [DEPTH, 72, 128, 8 * 128])
        bmod = din("bmod", [DEPTH, 128, 72])
        gains = din("gains", [DEPTH, 128, 3, 8])
        gfin = din("gfin", [128, 8])
        wab = din("wab", [DEPTH, 2, NJ, 128, 2 * 8 * 128])
        wo = din("wo", [DEPTH, 2, 8, 128, NJ * 128])
        win = din("win", [DEPTH, 128, 8, WCOLS])
        wout = din("wout", [DEPTH, 128, 8, D])
        wpool = din("wpool", [DEPTH, 128, 4, 128])
        pscale = din("pscale", [DEPTH, 128, 4])
        sinkrow = din("sinkrow", [DEPTH, 1, 2, 512])
        cosT = din("cosT", [128, NCOL])
        sinT = din("sinT", [128, NCOL])
        bandT = din("bandT", [128, 32, 128])
        cmask = din("cmask", [128, 128 + 512 + 512 + 128])
        if dump:
            hdump = nc.dram_tensor("hdump", [128, 8, NCOL], F32, kind="ExternalOutput").ap()
        outT = nc.dram_tensor("outT", [128, 8, OWN], F32, kind="ExternalOutput").ap()

        A = Arena(nc, es, 212480, "arena")
        pos = [0]

        def alloc(dtype, shape, dma=False):
            esz = 4 if dtype == F32 else 2
            nb = int(np.prod(shape)) * esz
            al = 512 if dma else 64
            pos[0] = (pos[0] + al - 1) // al * al
            v = A.view(pos[0], dtype, shape)
            pos[0] += (nb + al - 1) // al * al
            return v

        hT = alloc(F32, [8, NCOL], dma=True)
        cTs = alloc(F32, [8, 2], dma=True)
        bm_l = [alloc(F32, [72], dma=True) for _ in range(2)]
        gn_l = [alloc(F32, [3, 8], dma=True) for _ in range(2)]
        gf = alloc(F32, [8], dma=True)
        ones = alloc(BF16, [128])
        sc = alloc(BF16, [8, 2])
        modraw_l = [alloc(F32, [72, 2]) for _ in range(2)]
        Amod_l = [alloc(F32, [3, 8, 2]) for _ in range(2)]
        gate_l = [alloc(F32, [3, 8, 2]) for _ in range(2)]
        epsb = alloc(F32, [2])
        NSQ = 4
        sqr = [alloc(BF16, [512], dma=True) for i in range(NSQ)]
        tmp_s = [alloc(F32, [512], dma=True) for i in range(2)]
        rstd = alloc(F32, [512], dma=True)
        PH2 = pos[0]
        nTb = alloc(BF16, [8, 1280], dma=True)
        gT_off = pos[0]
        gTb = alloc(BF16, [NJ, 1280], dma=True)
        wab_s = [alloc(BF16, [2, 8, 128], dma=True) for i in range(3)]
        wo_s = [alloc(BF16, [NJ, 128], dma=True) for i in range(2)]
        sa_s = [alloc(F32, [512], dma=True) for i in range(2)]
        NWM = 3
        wm_s = [alloc(BF16, [8, 128], dma=True) for i in range(NWM)]
        ofin_s = [alloc(F32, [512], dma=True) for i in range(2)]
        assert pos[0] <= A.nbytes, pos[0]
        pos[0] = PH2
        poolout = alloc(BF16, [4, 2432], dma=True)
        nTt = alloc(BF16, [8, 512], dma=True)
        ident = alloc(BF16, [128], dma=True)
        mbp = alloc(BF16, [512], dma=True)
        mbn = alloc(BF16, [512], dma=True)
        sinkl = alloc(BF16, [128], dma=True)
        esink = alloc(BF16, [2, 512], dma=True)
        MG = pos[0]
        kT = alloc(BF16, [2, NCOL], dma=True)
        Vaug = alloc(BF16, [20, 2, 128], dma=True)
        cst = alloc(F32, [2, 512], dma=True)
        rt = [alloc(F32, [512], dma=True) for i in range(2)]
        MQ = pos[0]
        wq = alloc(BF16, [8, 1024], dma=True)
        wou = alloc(BF16, [8, D], dma=True)
        qz = alloc(BF16, [8, 512], dma=True)
        aot = alloc(BF16, [4, 512], dma=True)
        NPT = 8
        PT = [alloc(BF16, [512], dma=True) for i in range(NPT)]
        rcp = rt[0]
        assert pos[0] <= A.nbytes, pos[0]
        pos[0] = MQ
        wkv = alloc(BF16, [8, 640], dma=True)
        wu = alloc(BF16, [8, 512], dma=True)
        wpl = alloc(BF16, [4, 128], dma=True)
        psc = alloc(F32, [4], dma=True)
        band = alloc(BF16, [32, 128], dma=True)
        utok = alloc(BF16, [20, 512], dma=True)
        pooled = alloc(BF16, [4, 512], dma=True)
        assert pos[0] <= A.nbytes, pos[0]

        ps = [es.enter_context(nc.psum_tensor("ps%d" % i, [128, 512], F32)) for i in range(8)]
        pa, pb, py, pss, pmisc = ps[0:2], ps[2:4], ps[4:6], ps[6], ps[7]

        S.op("dve", lambda e: e.memset(ones, 1.0), writes=[("ones",)])
        S.op("dve", lambda e: e.memset(epsb, EPS), writes=[("epsb",)])
        for c0 in range(0, NCOL, 512):
            S.dma("sp", None, hT[:, :, c0:c0 + 512], xT[:, :, c0:c0 + 512],
                  writes=[("h", k, b) for b in range(c0 // 128, c0 // 128 + 4) for k in range(8)])
        S.dma("sp", "d_c", cTs, cT, writes=[("cT",)])
        S.dma("sp", "d_c", gf, gfin, writes=[("gf",)])
        S.op("act", lambda e: e.activation(sc, cTs, AF.Silu), reads=[("cT",)], writes=[("sc",)])

        wm_cnt = [0]
        curp = {"par": 0}

        def mod_load(l):
            par = l % 2
            S.dma("sp", None, bm_l[par], bmod[l], writes=[("bm", par)])
            S.dma("sp", None, gn_l[par], gains[l], writes=[("gn", par)])

        def mod_chunk(l, idx):
            slot = wm_cnt[0] % NWM
            wm_cnt[0] += 1
            S.dma("pool", None, wm_s[slot], wmod[l, idx].rearrange("p (k c) -> p k c", k=8), writes=[("wm", slot)])
            S.mm(pmisc[:, 2 * idx:2 * idx + 2], [(wm_s[slot][:, k, :], sc[:, k, :]) for k in range(8)],
                 reads=[("wm", slot), ("sc",)], writes=[("pm",)])

        def mod_finish(l, m):
            par = l % 2
            pm = pmisc[:, 16 * m:16 * m + 16].rearrange("p (m t) -> p m t", t=2)
            for t in range(2):
                S.op("dve", lambda e, t=t, pm=pm: e.tensor_tensor(
                    modraw_l[par][:, 8 * m:8 * m + 8, t], pm[:, :, t], bm_l[par][:, 8 * m:8 * m + 8], ALU.add),
                    reads=[("pm",), ("bm", par)], writes=[("modraw", par, m)])

        def mod_derive(l, i, what="both"):
            par = l % 2
            modraw, Amod, gate, gn = modraw_l[par], Amod_l[par], gate_l[par], gn_l[par]
            scale = modraw[:, (3 * i + 1) * 8:(3 * i + 2) * 8, :]
            gt = modraw[:, (3 * i + 2) * 8:(3 * i + 3) * 8, :]
            if what in ("both", "A"):
                S.op("dve", lambda e: e.tensor_scalar(Amod[:, i], scale, 1.0, None, ALU.add),
                     reads=[("modraw", par, 3 * i + 1)], writes=[("Amod", par, i)])
                for t in range(2):
                    S.op("dve", lambda e, t=t: e.tensor_tensor(Amod[:, i, :, t], Amod[:, i, :, t], gn[:, i, :], ALU.mult),
                         reads=[("Amod", par, i), ("gn", par)], writes=[("Amod", par, i)])
            if what in ("both", "gate"):
                S.op("dve", lambda e: e.tensor_scalar(gate[:, i], gt, 0.5 if i != 1 else 1.0, None, ALU.mult),
                     reads=[("modraw", par, 3 * i + 2)], writes=[("gate", par, i)])

        def mod_tasks(l, ms):
            tasks = []
            for m in ms:
                for c in range(8):
                    tasks.append(lambda m=m, c=c: mod_chunk(l, 8 * m + c))
                tasks.append(lambda m=m: mod_finish(l, m))
            return tasks

        def hkeys(k, c0, n):
            return [("h", k, b) for b in range(c0 // 128, (c0 + n + 127) // 128)]

        cnt = {"tmp": 0, "sq": 0, "ab": 0, "o": 0, "p": 0, "pt": 0, "pv": 0, "st": 0}

        def rstd_steps(c0, n):
            steps = []

            def sq_step(k):
                si = cnt["sq"] % NSQ
                cnt["sq"] += 1
                S.op("act", lambda e: e.activation(sqr[si][:, 0:n], hT[:, k, c0:c0 + n], AF.Square),
                     reads=hkeys(k, c0, n), writes=[("sq", si)])
                S.mm(pss[:, 0:n], [(ones, sqr[si][:, 0:n])], reads=[("ones",), ("sq", si)],
                     writes=[("pss",)] if k in (0, 7) else [], start=(k == 0), stop=(k == 7))

            def fin():
                S.op("act", lambda e: e.activation(rstd[:, 0:n], pss[:, 0:n], AF.Ln, bias=epsb[:, 0:1], scale=1.0 / D),
                     reads=[("pss",), ("epsb",)], writes=[("rstd",)])
                S.op("act", lambda e: e.activation(rstd[:, 0:n], rstd[:, 0:n], AF.Exp, scale=-0.5),
                     reads=[("rstd",)], writes=[("rstd",)])

            for k in range(8):
                steps.append(lambda k=k: sq_step(k))
            steps.append(fin)
            return steps

        def rstd_of(c0, n):
            for st in rstd_steps(c0, n):
                st()

        def norm_steps(c0, n, i_norm, dst, dst_keys, eng="dve", split=False):
            rsteps = rstd_steps(c0, n)
            steps = []
            shift_i = 3 * i_norm
            par = curp["par"]
            modraw, Amod = modraw_l[par], Amod_l[par]

            def mod_step(k, a, m, t):
                off = a - c0
                ti = cnt["tmp"] % 2
                cnt["tmp"] += 1
                tmp = tmp_s[ti]
                S.op(eng, lambda e: e.tensor_tensor(tmp[:, 0:m], hT[:, k, a:a + m], rstd[:, off:off + m], ALU.mult),
                     reads=hkeys(k, a, m) + [("rstd",)], writes=[("tmp", ti)])
                S.op("act", lambda e: e.activation(
                    dst(k, off, m), tmp[:, 0:m], AF.Identity, bias=modraw[:, shift_i * 8 + k, t:t + 1],
                    scale=Amod[:, i_norm, k, t:t + 1]),
                    reads=[("tmp", ti), ("modraw", par, shift_i), ("Amod", par, i_norm)], writes=dst_keys(k))

            for k in range(8):
                for (a, m, t) in colset_ranges(c0, n):
                    steps.append(lambda k=k, a=a, m=m, t=t: mod_step(k, a, m, t))
            if split:
                return rsteps, steps
            return rsteps + steps

        def norm_mod(c0, n, i_norm, dst, dst_keys, eng="dve"):
            for st in norm_steps(c0, n, i_norm, dst, dst_keys, eng):
                st()

        def interleave(main, side):
            side = list(side)
            nm = max(len(main), 1)
            per = (len(side) + nm - 1) // nm
            for st in main:
                st()
                for _ in range(per):
                    if side:
                        side.pop(0)()
            while side:
                side.pop(0)()

        def resid(pst, c0, n, i_norm, d):
            par = curp["par"]
            gate = gate_l[par]
            for (a, m, t) in colset_ranges(c0, n):
                o2 = a - c0
                S.op("dve", lambda e, a=a, m=m, t=t, o2=o2: e.scalar_tensor_tensor(
                    hT[:, d, a:a + m], pst[1][:, o2:o2 + m], gate[:, i_norm, d, t:t + 1], hT[:, d, a:a + m],
                    ALU.mult, ALU.add),
                    reads=[pst[0], ("gate", par, i_norm)], writes=hkeys(d, a, m))

        def ffn_seg_norm(c0, sn, i_norm):
            steps = []
            for sti, (off, n) in enumerate(subtiles(sn)):
                steps += norm_steps(
                    c0 + off, n, i_norm,
                    lambda k, o2, m, off=off: nTb[:, k, off + o2: off + o2 + m],
                    lambda k, sti=sti: [("nT", k, sti)])
            return steps

        def ffn(l, f, i_norm, segs, extra=(), pre=(), last_seg_extra=(), skip_first_norm=False, tail_side=()):
            curp["par"] = l % 2
            for t in pre:
                t()
            extra = list(extra)
            nslots = NJ * len(segs)
            per = (len(extra) + nslots - 1) // nslots if extra else 0
            lse = list(last_seg_extra)
            per_last = (len(lse) + NJ - 1) // NJ if lse else 0

            def seg_norm(c0, sn):
                return ffn_seg_norm(c0, sn, i_norm)

            if not skip_first_norm:
                for t in seg_norm(*segs[0]):
                    t()
            for si, (c0, sn) in enumerate(segs):
                sts = subtiles(sn)
                for j in range(NJ):
                    slot = cnt["ab"] % 3
                    cnt["ab"] += 1
                    S.dma("pool", None, wab_s[slot], wab[l, f, j].rearrange("p (a k c) -> p a k c", a=2, k=8),
                          writes=[("wab", slot)])
                    for sti, (off, n) in enumerate(sts):
                        pi = cnt["p"] % 2
                        cnt["p"] += 1
                        nkeys = [("nT", k, sti) for k in range(8)]
                        S.mm(pa[pi][:, 0:n], [(wab_s[slot][:, 0, k, :], nTb[:, k, off:off + n]) for k in range(8)],
                             reads=[("wab", slot)] + nkeys, writes=[("pa", pi)])
                        S.mm(pb[pi][:, 0:n], [(wab_s[slot][:, 1, k, :], nTb[:, k, off:off + n]) for k in range(8)],
                             reads=[("wab", slot)] + nkeys, writes=[("pb", pi)])
                        S.op("act", lambda e, pi=pi, n=n: e.activation(sa_s[pi][:, 0:n], pa[pi][:, 0:n], AF.Silu),
                             reads=[("pa", pi)], writes=[("sa", pi)])
                        S.op("dve", lambda e, pi=pi, n=n, j=j, off=off: e.tensor_tensor(
                            gTb[:, j, off:off + n], sa_s[pi][:, 0:n], pb[pi][:, 0:n], ALU.mult),
                            reads=[("sa", pi), ("pb", pi)], writes=[("g", j, sti)])
                    for _ in range(per):
                        if extra:
                            extra.pop(0)()
                    if si == len(segs) - 1:
                        for _ in range(per_last):
                            if lse:
                                lse.pop(0)()
                nxt = seg_norm(*segs[si + 1]) if si + 1 < len(segs) else list(tail_side)
                main = []

                def out_step(d, sti, off, n, slot):
                    pi = cnt["p"] % 2
                    cnt["p"] += 1
                    S.mm(py[pi][:, 0:n], [(wo_s[slot][:, j, :], gTb[:, j, off:off + n]) for j in range(NJ)],
                         reads=[("wo", slot)] + [("g", j, sti) for j in range(NJ)], writes=[("py", pi)])
                    resid((("py", pi), py[pi]), c0 + off, n, i_norm, d)

                def wo_load(d, slot):
                    S.dma("pool", None, wo_s[slot], wo[l, f, d].rearrange("p (j c) -> p j c", j=NJ),
                          writes=[("wo", slot)])

                for d in range(8):
                    slot = cnt["o"] % 2
                    cnt["o"] += 1
                    for sti, (off, n) in enumerate(sts):
                        if sti == 0:
                            main.append(lambda d=d, sti=sti, off=off, n=n, slot=slot: (wo_load(d, slot), out_step(d, sti, off, n, slot)))
                        else:
                            main.append(lambda d=d, sti=sti, off=off, n=n, slot=slot: out_step(d, sti, off, n, slot))
                interleave(main, nxt)
            while extra:
                extra.pop(0)()
            while lse:
                lse.pop(0)()

        def col2blk(c):
            return c // 128

        def mix(l):
            last = (l == DEPTH - 1)
            curp["par"] = l % 2
            S.dma("pool", "d_cst", ident, cmask[:, 0:128], writes=[("ident",)])
            S.dma("pool", "d_cst", mbp, cmask[:, 128:640], writes=[("mbp",)])
            S.dma("pool", "d_cst", mbn, cmask[:, 640:1152], writes=[("mbn",)])
            S.dma("pool", "d_cst", sinkl[0:1, :], cmask[0:1, 1152:1280], writes=[("sinkl",)])
            S.dma("pool", "d_cst", wpl, wpool[l], writes=[("wpl",)])
            S.dma("sp", "d_c", psc, pscale[l], writes=[("psc",)])
            for g in range(2):
                S.dma("sp", "d_c", tmp_s[g][0:1, :], sinkrow[l, :, g, :], writes=[("tmp", g)])
                S.op("act", lambda e, g=g: e.activation(esink[0:1, g, :], tmp_s[g][0:1, :], AF.Exp),
                     reads=[("tmp", g)], writes=[("esink",)])

            S.dma("pool", None, wkv, win[l, :, :, 512:WCA], writes=[("wkv",)])
            S.dma("pool", None, wu, win[l, :, :, 0:512], writes=[("wu",)])
            S.op("dve", lambda e: e.memset(Vaug[:, :, :, 64:128], 1.0), writes=[("Vaug",)])
            acols = [(0, 512), (512, 512), (1024, 512), (1536, 512), (2048, 384) if last else (2048, 512)]
            ev_i = [0]

            def evac(dst, src, reads, writes):
                ev_i[0] += 1
                if ev_i[0] % 2:
                    S.op("act", lambda e: e.copy(dst, src), reads=reads, writes=writes)
                else:
                    S.op("dve", lambda e: e.tensor_copy(dst, src), reads=reads, writes=writes)

            nTt2 = band.rearrange("p a b -> p (a b)").rearrange("p (k c) -> p k c", k=8)
            nbufs = [nTt, nTt2]

            def nkeys(ti):
                return [("nTt", k) for k in range(8)] if ti % 2 == 0 else [("band",)]

            def a_norm(ti, c0, n):
                nb = nbufs[ti % 2]
                return norm_steps(c0, n, 1, lambda k, o2, m: nb[:, k, o2:o2 + m],
                                  (lambda k: [("nTt", k)]) if ti % 2 == 0 else (lambda k: [("band",)]), eng="pool")

            for st in a_norm(0, *acols[0]):
                st()
            for ti, (c0, n) in enumerate(acols):
                nTt_ = nbufs[ti % 2]
                nk = nkeys(ti)
                def cs_load(n=n, c0=c0):
                    S.dma("sp", None, cst[:, 0, 0:n], cosT[:, c0:c0 + n], writes=[("cs", 0)])
                    S.dma("sp", None, cst[:, 1, 0:n], sinT[:, c0:c0 + n], writes=[("cs", 1)])

                def k_step(g, n=n, c0=c0, nTt=nTt_, nk=nk):
                    S.mm(pa[g][:, 0:n], [(wkv[:, k, g * 128:(g + 1) * 128], nTt[:, k, 0:n]) for k in range(8)],
                         reads=[("wkv",)] + nk, writes=[("pa", g)])
                    S.mm(pb[g][:, 0:n], [(wkv[:, k, 256 + g * 128:256 + (g + 1) * 128], nTt[:, k, 0:n]) for k in range(8)],
                         reads=[("wkv",)] + nk, writes=[("pb", g)])
                    S.op("dve", lambda e: e.tensor_tensor(rt[0][:, 0:n], pa[g][:, 0:n], cst[:, 0, 0:n], ALU.mult),
                         reads=[("pa", g), ("cs", 0)], writes=[("rt", 0)])
                    S.op("dve", lambda e: e.tensor_tensor(rt[1][:, 0:n], pb[g][:, 0:n], cst[:, 1, 0:n], ALU.mult),
                         reads=[("pb", g), ("cs", 1)], writes=[("rt", 1)])
                    S.op("dve", lambda e: e.tensor_tensor(kT[:, g, c0:c0 + n], rt[0][:, 0:n], rt[1][:, 0:n], ALU.add),
                         reads=[("rt", 0), ("rt", 1)], writes=[("kT", g, c0)])

                def v_step(tb, c0=c0, nTt=nTt_, nk=nk):
                    blk = (c0 + tb * 128) // 128
                    pi = cnt["p"] % 2
                    cnt["p"] += 1
                    S.mm(py[pi][:, 0:128], [(nTt[:, k, tb * 128:(tb + 1) * 128], wkv[:, k, 512:640]) for k in range(8)],
                         reads=[("wkv",)] + nk, writes=[("py", pi)])
                    S.op("act", lambda e: e.copy(
                        Vaug[:, blk, :, 0:64], py[pi][:, 0:128].rearrange("p (g d) -> p g d", g=2)),
                        reads=[("py", pi)], writes=[("Vaug",)])

                def utok_step(tb, c0=c0, nTt=nTt_, nk=nk):
                    blk = c0 // 128 + tb
                    pi = cnt["p"] % 2
                    cnt["p"] += 1
                    S.mm(py[pi][:, :], [(nTt[:, k, tb * 128:(tb + 1) * 128], wu[:, k, :]) for k in range(8)],
                         reads=[("wu",)] + nk, writes=[("py", pi)])
                    S.op("dve", lambda e: e.tensor_copy(utok[:, blk, :], py[pi][:, :]), reads=[("py", pi)], writes=[("utok", blk)])

                cs_load()
                nb_ = n // 128
                main = [lambda tb=tb: utok_step(tb) for tb in range(nb_)]
                main += [lambda g=g: k_step(g) for g in range(2)]
                main += [lambda tb=tb: v_step(tb) for tb in range(nb_)]
                side = a_norm(ti + 1, *acols[ti + 1]) if ti + 1 < len(acols) else []
                interleave(main, side)
            S.dma("pool", None, band, bandT, writes=[("band",)])
            S.dma("pool", None, wq, win[l, :, :, WCA:WCOLS], writes=[("wq",), ("wkv",), ("wu",)])
            oblocks = list(range(16)) if last else list(range(19))
            for t0 in range(0, len(oblocks), 4):
                obs = oblocks[t0:t0 + 4]
                n = 128 * len(obs)
                for g in range(4):
                    pbank, pkey = [(pa[0], ("pa", 0)), (pa[1], ("pa", 1)), (pb[0], ("pb", 0)), (pb[1], ("pb", 1))][g]
                    pairs, rd = [], [("band",)]
                    for oi, ob in enumerate(obs):
                        if ob <= 16:
                            srcs = ([(ob - 1, 0)] if ob >= 1 else []) + [(ob, 3 if ob == 0 else 1), (ob + 1 if ob < 16 else 19, 2)]
                        elif ob == 17:
                            srcs = [(17, 6), (18, 5)]
                        else:
                            srcs = [(17, 4), (18, 7)]
                        for si, (sb, kind) in enumerate(srcs):
                            pairs.append((utok[:, sb, g * 128:(g + 1) * 128], band[:, g * 8 + kind, :],
                                          pbank[:, oi * 128:(oi + 1) * 128], si == 0, si == len(srcs) - 1))
                            rd.append(("utok", sb))
                    S.mm(pbank[:, 0:n], pairs, reads=rd, writes=[pkey])
                    S.op("dve", lambda e, g=g, n=n, pbank=pbank: e.tensor_copy(pooled[:, g, 0:n], pbank[:, 0:n]),
                         reads=[pkey], writes=[("pooled", g)])
                for g in range(4):
                    pj = cnt["p"] % 2
                    cnt["p"] += 1
                    S.mm(py[pj][:, 0:n], [(wpl[:, g, :], pooled[:, g, 0:n])], reads=[("wpl",), ("pooled", g)], writes=[("py", pj)])
                    oc = obs[0] * 128
                    S.op("act", lambda e, pj=pj, g=g, n=n, oc=oc: e.activation(
                        poolout[:, g, oc:oc + n], py[pj][:, 0:n], AF.Identity, scale=psc[:, g:g + 1]),
                        reads=[("py", pj), ("psc",)], writes=[("poolout", g, oc)])
            qtiles = [(0, 512), (512, 512), (1024, 512), (1536, 512)]
            if not last:
                qtiles.append((2048, 384))
            for st in norm_steps(qtiles[0][0], qtiles[0][1], 1, lambda k, o2, m: nTt[:, k, o2:o2 + m], lambda k: [("nTt", k)], eng="pool"):
                st()
            S.barrier()

            if DEBUG_MIX_STOP[0] in ("P", "KV"):
                return
            S.dma("pool", None, wou, wout[l], writes=[("wou",)])
            S.op("dve", lambda e: e.memset(qz, 0.0), writes=[("qz", h) for h in range(8)])
            pst = [pb[0], pb[1], py[0]]
            pstk = [("pb", 0), ("pb", 1), ("py", 0)]
            pvb = [py[1], pmisc]
            pvk = [("py", 1), ("pmisc",)]

            def q_norm(c0, n):
                return norm_steps(c0, n, 1, lambda k, o2, m: nTt[:, k, o2:o2 + m], lambda k: [("nTt", k)], eng="pool")

            def emit_scores(item, sel):
                (qs, g, kbs, pts) = item
                for (kb, mb, mbkey) in kbs[sel]:
                    si = cnt["st"] % 3
                    cnt["st"] += 1
                    stp, stkey = pst[si], pstk[si]
                    pairs = []
                    rd = [("qz", 4 * g + hh) for hh in range(4)]
                    for hh in range(4):
                        pairs.append((kT[:, g, kb * 128:(kb + 1) * 128], qz[:, 4 * g + hh, qs],
                                      stp[:, hh * 128:(hh + 1) * 128], True, True))
                    S.mm(stp[:, :], pairs, reads=rd, writes=[stkey])
                    pti = cnt["pt"] % NPT
                    cnt["pt"] += 1
                    S.op("act", lambda e, stp=stp, pti=pti: e.activation(PT[pti], stp[:, :], AF.Exp, scale=0.125),
                         reads=[stkey], writes=[("PT", pti)])
                    if mb is not None:
                        S.op("dve", lambda e, pti=pti, mb=mb: e.tensor_tensor(PT[pti], PT[pti], mb, ALU.mult),
                             reads=[("PT", pti), mbkey], writes=[("PT", pti)])
                    pts.append((kb, pti))

            def emit_pv(item):
                (qs, g, kbs, pts) = item
                pvi = cnt["pv"] % 2
                cnt["pv"] += 1
                pvp, pvkey = pvb[pvi], pvk[pvi]
                pairs = [(Vaug[:, kb, g, :], PT[pti]) for (kb, pti) in pts]
                pairs.append((sinkl[0:1, :], esink[0:1, g, :]))
                S.mm(pvp[:, :], pairs, reads=[("Vaug",), ("sinkl",), ("esink",)] + [("PT", pti) for (_, pti) in pts],
                     writes=[pvkey])
                return (pvp, pvkey, qs, g)

            def emit_normalise(pvinfo):
                (pvp, pvkey, qs, g) = pvinfo
                S.op("act", lambda e: e.activation(rcp[64:128, :], pvp[64:128, :], AF.Ln), reads=[pvkey], writes=[("rt", 0)])
                S.op("act", lambda e: e.activation(rcp[64:128, :], rcp[64:128, :], AF.Exp, scale=-1.0),
                     reads=[("rt", 0)], writes=[("rt", 0)])
                for hf in range(2):
                    i0v = pvp[0:64, :].rearrange("p (c t q) -> p c t q", c=2, t=2)[:, :, hf, :]
                    i1v = rcp[64:128, :].rearrange("p (c t q) -> p c t q", c=2, t=2)[:, :, hf, :]
                    ov = aot[64 * hf:64 * hf + 64, 2 * g:2 * g + 2, qs]
                    S.op("dve", lambda e, i0v=i0v, i1v=i1v, ov=ov: e.tensor_tensor(ov, i0v, i1v, ALU.mult),
                         reads=[pvkey, ("rt", 0)], writes=[("aot", 2 * g), ("aot", 2 * g + 1)])

            def wout_step(c0, n, d):
                pairs = [(wou[:, c, d * 128:(d + 1) * 128], poolout[:, c, c0:c0 + n]) for c in range(4)]
                pairs += [(wou[:, 4 + c, d * 128:(d + 1) * 128], aot[:, c, 0:n]) for c in range(4)]
                pi = d % 2
                rd = [("wou",)] + [("aot", c) for c in range(4)]
                S.mm(pa[pi][:, 0:n], pairs, reads=rd, writes=[("pa", pi)])
                resid((("pa", pi), pa[pi]), c0, n, 1, d)

            for ti, (c0, n) in enumerate(qtiles):
                S.dma("sp", None, cst[:, 0, 0:n], cosT[:, c0:c0 + n], writes=[("cs", 0)])
                S.dma("sp", None, cst[:, 1, 0:n], sinT[:, c0:c0 + n], writes=[("cs", 1)])
                nk = [("nTt", k) for k in range(8)]
                for c in range(4):
                    S.mm(pa[0][:, 0:n], [(wq[:, k, c * 128:(c + 1) * 128], nTt[:, k, 0:n]) for k in range(8)],
                         reads=[("wq",)] + nk, writes=[("pa", 0)])
                    S.mm(pa[1][:, 0:n], [(wq[:, k, 512 + c * 128:512 + (c + 1) * 128], nTt[:, k, 0:n]) for k in range(8)],
                         reads=[("wq",)] + nk, writes=[("pa", 1)])
                    S.op("dve", lambda e, n=n: e.tensor_tensor(rt[0][:, 0:n], pa[0][:, 0:n], cst[:, 0, 0:n], ALU.mult),
                         reads=[("pa", 0), ("cs", 0)], writes=[("rt", 0)])
                    S.op("dve", lambda e, n=n: e.tensor_tensor(rt[1][:, 0:n], pa[1][:, 0:n], cst[:, 1, 0:n], ALU.mult),
                         reads=[("pa", 1), ("cs", 1)], writes=[("rt", 1)])
                    for hf in range(2):
                        rows = slice(64 * hf, 64 * hf + 64)
                        S.op("pool", lambda e, c=c, n=n, hf=hf, rows=rows: e.tensor_tensor(
                            qz[rows, 2 * c + hf, 0:n], rt[0][rows, 0:n], rt[1][rows, 0:n], ALU.add),
                            reads=[("rt", 0), ("rt", 1)], writes=[("qz", 2 * c + hf)])
                items = []
                for qb in range(n // 128):
                    qblk = (c0 + qb * 128) // 128
                    qs = slice(qb * 128, (qb + 1) * 128)
                    if qblk <= 16:
                        kbs = []
                        if qblk >= 1:
                            kbs.append((qblk - 1, mbp, ("mbp",)))
                        kbs.append((qblk, None, None))
                        nxt = qblk + 1 if qblk < 16 else 19
                        kbs.append((nxt, mbn, ("mbn",)))
                        kbs += [(17, None, None), (18, None, None)]
                    else:
                        kbs = [(17, None, None), (18, None, None)]
                    for g in range(2):
                        items.append((qs, g, kbs, []))
                emit_scores(items[0], slice(0, None))
                for ii, item in enumerate(items):
                    if ii + 1 < len(items):
                        emit_scores(items[ii + 1], slice(0, 3))
                    pvinfo = emit_pv(item)
                    if ii + 1 < len(items):
                        emit_scores(items[ii + 1], slice(3, None))
                    emit_normalise(pvinfo)
                side = q_norm(*qtiles[ti + 1]) if ti + 1 < len(qtiles) else []
                for st in side[:9]:
                    st()
                interleave([lambda d=d: wout_step(c0, n, d) for d in range(8)], side[9:])
            S.barrier()

        fin_cnt = [0]

        def final_tasks(tiles):
            tasks = []
            for (c0, n) in tiles:
                tasks += rstd_steps(c0, n)

                def out_step(k, c0=c0, n=n):
                    slot = fin_cnt[0] % 2
                    fin_cnt[0] += 1
                    S.op("dve", lambda e: e.scalar_tensor_tensor(
                        ofin_s[slot][:, 0:n], hT[:, k, c0:c0 + n], gf[:, k:k + 1], rstd[:, 0:n], ALU.mult, ALU.mult),
                        reads=hkeys(k, c0, n) + [("gf",), ("rstd",)], writes=[("ofin", slot)])
                    evs.append(S.dma("sp", "d_ofin%d" % slot, outT[:, k, c0:c0 + n], ofin_s[slot][:, 0:n], reads=[("ofin", slot)]))

                tasks += [lambda k=k, f=out_step: f(k) for k in range(8)]
            return tasks

        def final():
            for t in final_tasks([(1024, 512), (1536, 512)]):
                t()

        evs = []
        def l0_first():
            mod_load(0)
            pre = mod_tasks(0, [1, 0]) + [lambda: mod_derive(0, 0, "A")]
            extra = (mod_tasks(0, [2]) + [lambda: mod_derive(0, 0, "gate")]
                     + mod_tasks(0, [4, 3, 5, 7, 6, 8]) + [lambda: mod_derive(0, 1), lambda: mod_derive(0, 2)])
            ffn(0, 0, 0, [(0, 1280), (1280, 1280)], extra=extra, pre=pre)
            S.barrier()

        def l0_ffn2():
            mod_load(1)
            extra = mod_tasks(1, list(range(9))) + [lambda i=i: mod_derive(1, i) for i in range(3)]
            curp["par"] = 1
            nxt_norm = ffn_seg_norm(0, 1280, 0)
            curp["par"] = 0
            ffn(0, 1, 2, [(0, 1280), (1280, 1152)], extra=extra, tail_side=nxt_norm)

        prog = [
            l0_first,
            lambda: mix(0),
            l0_ffn2,
            lambda: (ffn(1, 0, 0, [(0, 1280), (1280, 1152)], skip_first_norm=True), S.barrier()),
            lambda: mix(1),
            lambda: ffn(1, 1, 2, [(0, 1024), (1024, 1024)], last_seg_extra=final_tasks([(0, 512), (512, 512)])),
            lambda: final(),
        ]
        for i in range(nstage + 1):
            prog[i]()

        if dump:
            for k in range(8):
                evs.append(S.dma("sp", "d_out", hdump[:, k, :], hT[:, k, :], reads=[("h", k, b) for b in range(20)]))
        if upto != "final":
            for k in range(8):
                evs.append(S.dma("sp", "d_out", outT[:, k, :], hT[:, k, 0:OWN], reads=[("h", k, b) for b in range(16)]))
        S.wait_all("sp", evs)

        with nc.Block() as block:
            @block.tensor
            def _(e):
                for f in S.ops["pe"]:
                    f(e)

            @block.scalar
            def _(e):
                for f in S.ops["act"]:
                    f(e)

            @block.vector
            def _(e):
                for f in S.ops["dve"]:
                    f(e)

            @block.gpsimd
            def _(e):
                for f in S.ops["pool"]:
                    f(e)

            @block.sync
            def _(e):
                for f in S.ops["sp"]:
                    f(e)
        print("instr counts", S.nins, {k: v for k, v in S.cnt.items()})
    return nc


def core_columns(half):
    p = np.arange(NL)
    return p if half == 0 else (SEQ - 1 - p)


def _kmajor(w):
    return w.reshape(8, 128, w.shape[-1]).transpose(1, 0, 2)


def prep_shared(inp):
    sh = {}
    sh["wmod"] = np.ascontiguousarray(
        inp["w_mod"].reshape(DEPTH, 8, 128, 72, 128).transpose(0, 3, 2, 1, 4).reshape(DEPTH, 72, 128, 8 * 128))
    sh["bmod"] = np.ascontiguousarray(inp["b_mod"].reshape(DEPTH, 72, 128).transpose(0, 2, 1))
    g = np.stack([inp["norm_ffn1"], inp["norm_mix"], inp["norm_ffn2"]], axis=1)
    sh["gains"] = np.ascontiguousarray(g.reshape(DEPTH, 3, 8, 128).transpose(0, 3, 1, 2))
    sh["gfin"] = np.ascontiguousarray(inp["norm_final"].reshape(8, 128).T)
    wab, wo = [], []
    for name_in, name_out in (("w_ffn1_in", "w_ffn1_out"), ("w_ffn2_in", "w_ffn2_out")):
        w = inp[name_in].reshape(DEPTH, 8, 128, 2, NJ, 128)
        wab.append(w.transpose(0, 4, 2, 3, 1, 5).reshape(DEPTH, NJ, 128, 2 * 8 * 128))
        w2 = inp[name_out].reshape(DEPTH, NJ, 128, 8, 128)
        wo.append(w2.transpose(0, 3, 2, 1, 4).reshape(DEPTH, 8, 128, NJ * 128))
    sh["wab"] = np.ascontiguousarray(np.stack(wab, axis=1))
    sh["wo"] = np.ascontiguousarray(np.stack(wo, axis=1))
    wins, wouts = [], []
    for l in range(DEPTH):
        w = inp["w_in"][l]
        u, q, k, v = w[:, 0:512], w[:, 512:1024], w[:, 1024:1152], w[:, 1152:1280]
        qsw = q.reshape(D, 8, 2, 32)[:, :, ::-1, :].reshape(D, 512)
        k3 = k.reshape(D, 2, 64)
        ksw3 = k.reshape(D, 2, 2, 32)[:, :, ::-1, :].reshape(D, 2, 64)
        kdup = np.concatenate([k3, k3], axis=-1).reshape(D, 256)
        kswdup = np.concatenate([ksw3, ksw3], axis=-1).reshape(D, 256)
        cols = np.concatenate([u, kdup, kswdup, v, q, qsw], axis=1)
        wins.append(_kmajor(cols))
        wouts.append(_kmajor(inp["w_out"][l]))
    sh["win"] = np.ascontiguousarray(np.stack(wins))
    sh["wout"] = np.ascontiguousarray(np.stack(wouts))
    sh["wpool"] = np.ascontiguousarray(inp["w_pool"].transpose(0, 2, 1, 3))
    sh["pscale"] = np.ascontiguousarray(inp["pool_scale"].reshape(DEPTH, 4, 128).transpose(0, 2, 1))
    sh["sinkrow"] = np.ascontiguousarray(
        np.repeat(inp["sink"].reshape(DEPTH, 2, 4), 128, axis=2).reshape(DEPTH, 1, 2, 512))
    cm = np.zeros((128, 1280), np.float32)
    cm[:, 0:128] = np.eye(128, dtype=np.float32)
    ki = np.arange(128)[:, None]
    qi = np.arange(128)[None, :]
    cm[:, 128:640] = np.tile(np.where(qi <= ki, 1.0, 0.0), (1, 4))
    cm[:, 640:1152] = np.tile(np.where(ki <= qi, 1.0, 0.0), (1, 4))
    cm[0, 1152 + 64:1280] = 1.0
    sh["cmask"] = cm
    return sh


def _band(out_pos, src_pos, offs, lo, hi):
    P = np.asarray(out_pos)[None, :]
    Q = np.asarray(src_pos)[:, None]
    offs = np.asarray(offs)
    win = P[None, :, :] + offs[:, None, None]
    ok = (win >= lo) & (win < hi)
    cntv = ok.sum(axis=0)
    hit = ((win == Q[None, :, :]) & ok).any(axis=0)
    return hit / cntv - (Q == P)


_BAND_CACHE = {}


def _band_tables(half):
    if half in _BAND_CACHE:
        return _BAND_CACHE[half]
    _BAND_CACHE[half] = _band_tables_build(half)
    return _BAND_CACHE[half]


def _band_tables_build(half):
    T = np.zeros((128, 32, 128), np.float32)
    blk = lambda b: list(range(128 * b, 128 * b + 128))
    for g, w in enumerate((2, 4, 8, 16)):
        fwd = list(range(-(w // 2), w - w // 2))
        offs = fwd if half == 0 else [-o for o in fwd]
        big = 1 << 30
        T[:, g * 8 + 0] = _band(blk(2), blk(1), offs, 0, big)
        T[:, g * 8 + 1] = _band(blk(2), blk(2), offs, 0, big)
        T[:, g * 8 + 2] = _band(blk(2), blk(3), offs, 0, big)
        T[:, g * 8 + 3] = _band(blk(0), blk(0), offs, 0, big)
        T[:, g * 8 + 4] = _band(blk(1), blk(0), fwd, 0, 256)
        T[:, g * 8 + 5] = _band(blk(0), blk(1), fwd, 0, 256)
        T[:, g * 8 + 6] = _band(blk(0), blk(0), fwd, 0, 256)
        T[:, g * 8 + 7] = _band(blk(1), blk(1), fwd, 0, 256)
    return T


def prep_core(inp, core):
    b, half = core // 2, core % 2
    tok = core_columns(half)
    xl = inp["x"][b][tok]
    cols = np.concatenate([xl[:CTX0], inp["ctx"][b], xl[CTX0:]], axis=0)
    m = {}
    m["xT"] = np.ascontiguousarray(cols.T.reshape(8, 128, NCOL).transpose(1, 0, 2))
    cc = np.stack([inp["c"][b], inp["c_ctx"]], axis=-1)
    m["cT"] = np.ascontiguousarray(cc.reshape(8, 128, 2).transpose(1, 0, 2))
    inv = 10000.0 ** (-np.arange(0, 32, 2, dtype=np.float64) / 32.0)
    ang = np.concatenate([(tok // 64)[:, None] * inv, (tok % 64)[:, None] * inv], axis=-1)
    cos_l, sin_l = np.cos(ang).T, np.sin(ang).T
    cosc = np.ones((32, NCOL)); sinc = np.zeros((32, NCOL))
    cosc[:, :CTX0] = cos_l[:, :CTX0]; cosc[:, H2C0:] = cos_l[:, CTX0:]
    sinc[:, :CTX0] = sin_l[:, :CTX0]; sinc[:, H2C0:] = sin_l[:, CTX0:]
    m["cosT"] = np.ascontiguousarray(np.tile(cosc, (4, 1)).astype(np.float32))
    m["sinT"] = np.ascontiguousarray(np.concatenate([-sinc, sinc, -sinc, sinc], axis=0).astype(np.float32))
    m["bandT"] = _band_tables(half)
    return m


_NC_CACHE = {}


def kernel(**inputs):
    inp = {k: np.asarray(v) for k, v in inputs.items()}
    if "nc" not in _NC_CACHE:
        _NC_CACHE["nc"] = build()
    nc = _NC_CACHE["nc"]
    sh = prep_shared(inp)
    in_maps = []
    for core in range(8):
        m = dict(sh)
        m.update(prep_core(inp, core))
        in_maps.append(m)
    res = run_bass_kernel_spmd(nc, in_maps, core_ids=list(range(8)))
    out = np.empty((4, SEQ, D), np.float32)
    for core in range(8):
        b, half = core // 2, core % 2
        o = res.results[core]["outT"]
        rows = o.transpose(2, 1, 0).reshape(OWN, D)
        out[b, core_columns(half)[:OWN]] = rows
    return out
```

```python
import numpy as np
from contextlib import ExitStack
import concourse.bass as bass
import concourse.mybir as mybir
from concourse.bass_utils import run_bass_kernel_spmd

F32 = mybir.dt.float32
BF16 = mybir.dt.bfloat16
AF = mybir.ActivationFunctionType
ALU = mybir.AluOpType

D = 1024
DFF = 2816
NJ = DFF // 128
SEQ = 4096
OWN = 2048
NL = 2304
NCOL = 2560
CTX0 = 2176
H2C0 = 2432
EPS = 1e-6
DEPTH = 2
NEG = -30000.0


def colset_ranges(c0, n):
    out = []
    for (a, b, s) in ((0, CTX0, 0), (CTX0, H2C0, 1), (H2C0, NCOL, 0)):
        lo, hi = max(a, c0), min(b, c0 + n)
        if hi > lo:
            out.append((lo, hi - lo, s))
    return out


class Sched:
    ENG = ["pe", "act", "dve", "pool", "sp"]

    def __init__(self, nc, es):
        self.nc, self.es = nc, es
        self.ops = {e: [] for e in self.ENG}
        self.sems, self.cnt = {}, {}
        self.seen = {e: {} for e in self.ENG}
        self.res = {}
        self.nins = {e: 0 for e in self.ENG}

    def sem(self, name):
        if name not in self.sems:
            self.sems[name] = self.es.enter_context(self.nc.semaphore(name))
            self.cnt[name] = 0
        return self.sems[name]

    def _deps(self, reads, writes):
        d = {}
        for r in reads:
            st = self.res.get(r)
            if st and st[0] is not None:
                s, v = st[0]
                if d.get(s, 0) < v:
                    d[s] = v
        for w in writes:
            st = self.res.get(w)
            if st:
                if st[0] is not None:
                    s, v = st[0]
                    if d.get(s, 0) < v:
                        d[s] = v
                for s, v in st[1].items():
                    if d.get(s, 0) < v:
                        d[s] = v
        return d

    def _waits(self, eng, d):
        seen = self.seen[eng]
        for s, v in d.items():
            if seen.get(s, 0) < v:
                seen[s] = v
                h = self.sems[s]
                self.ops[eng].append(lambda e, h=h, v=v: e.wait_ge(h, v))

    def _commit(self, ev, reads, writes):
        s, v = ev
        for r in reads:
            st = self.res.setdefault(r, [None, {}])
            if st[1].get(s, 0) < v:
                st[1][s] = v
        for w in writes:
            self.res[w] = [ev, {}]

    def op(self, eng, fn, reads=(), writes=()):
        self._waits(eng, self._deps(reads, writes))
        sname = "s_" + eng
        h = self.sem(sname)
        self.cnt[sname] += 1
        ev = (sname, self.cnt[sname])
        self.ops[eng].append(lambda e, fn=fn, h=h: fn(e).then_inc(h, 1))
        self.nins[eng] += 1
        self._commit(ev, reads, writes)

    def mm(self, out, pairs, reads, writes, start=True, stop=True):
        self._waits("pe", self._deps(reads, writes))
        h = self.sem("s_pe")
        self.cnt["s_pe"] += 1
        ev = ("s_pe", self.cnt["s_pe"])
        n = len(pairs)
        for i, pr in enumerate(pairs):
            l, r = pr[0], pr[1]
            o_i = pr[2] if len(pr) > 2 else out
            st_i = pr[3] if len(pr) > 3 else (start and i == 0)
            sp_i = pr[4] if len(pr) > 4 else (stop and (i == n - 1))
            if i == n - 1:
                self.ops["pe"].append(lambda e, l=l, r=r, o_i=o_i, st_i=st_i, sp_i=sp_i: e.matmul(o_i, l, r, start=st_i, stop=sp_i).then_inc(h, 1))
            else:
                self.ops["pe"].append(lambda e, l=l, r=r, o_i=o_i, st_i=st_i, sp_i=sp_i: e.matmul(o_i, l, r, start=st_i, stop=sp_i))
        self.nins["pe"] += n
        self._commit(ev, reads, writes)

    def barrier(self):
        d = {s: v for s, v in self.cnt.items() if v > 0}
        for eng in self.ENG:
            self._waits(eng, dict(d))
        self.res = {}

    def dma(self, eng, sname, out, in_, reads=(), writes=()):
        if writes:
            sname = "d_" + "_".join(str(x) for x in writes[0])
        assert sname is not None
        self._waits(eng, self._deps(reads, writes))
        h = self.sem(sname)
        self.cnt[sname] += 16
        ev = (sname, self.cnt[sname])
        self.ops[eng].append(lambda e: e.dma_start(out=out, in_=in_).then_inc(h, 16))
        self._commit(ev, reads, writes)
        return ev

    def wait_all(self, eng, evs):
        d = {}
        for s, v in evs:
            d[s] = max(d.get(s, 0), v)
        self._waits(eng, d)


class Arena:
    def __init__(self, nc, es, nbytes, name):
        self.t = es.enter_context(nc.sbuf_tensor(name, [128, nbytes // 4], F32))
        self.nbytes = nbytes

    def view(self, off, dtype, shape):
        esz = 4 if dtype == F32 else 2
        n = int(np.prod(shape))
        assert off % 4 == 0 and off + n * esz <= self.nbytes, (off, n, esz, self.nbytes)
        ap = self.t[:, off // 4: off // 4 + (n * esz + 3) // 4]
        if dtype != F32:
            ap = ap.bitcast(dtype)
        if len(shape) == 2:
            ap = ap.rearrange("p (a b) -> p a b", a=shape[0])
        elif len(shape) == 3:
            ap = ap.rearrange("p (a b c) -> p a b c", a=shape[0], b=shape[1])
        return ap


def subtiles(n, step=512):
    out, off = [], 0
    while off < n:
        m = min(step, n - off)
        out.append((off, m))
        off += m
    return out


WCA = 512 + 256 + 256 + 128
WCOLS = WCA + 1024
ULAT = 8 + 2192 + 8
UCTX = 8 + 256 + 8
STAGES = ["l0ffn1", "l0mix", "l0ffn2", "l1ffn1", "l1mix", "l1ffn2", "final"]
DEBUG_MIX_STOP = [None]


def build(upto="final", dump=False):
    nc = bass.Bass("TRN2", target_bir_lowering=False)
    es = ExitStack()
    nstage = STAGES.index(upto)
    with es:
        S = Sched(nc, es)

        def din(name, shape):
            return nc.dram_tensor(name, list(shape), F32, kind="ExternalInput").ap()

        xT = din("xT", [128, 8, NCOL])
        cT = din("cT", [128, 8, 2])
        wmod = din("wmod", [DEPTH, 72, 128, 8 * 128])
        bmod = din("bmod", [DEPTH, 128, 72])
        gains = din("gains", [DEPTH, 128, 3, 8])
        gfin = din("gfin", [128, 8])
        wab = din("wab", [DEPTH, 2, NJ, 128, 2 * 8 * 128])
        wo = din("wo", [DEPTH, 2, 8, 128, NJ * 128])
        win = din("win", [DEPTH, 128, 8, WCOLS])
        wout = din("wout", [DEPTH, 128, 8, D])
        wpool = din("wpool", [DEPTH, 128, 4, 128])
        pscale = din("pscale", [DEPTH, 128, 4])
        sinkrow = din("sinkrow", [DEPTH, 1, 2, 512])
        cosT = din("cosT", [128, NCOL])
        sinT = din("sinT", [128, NCOL])
        bandT = din("bandT", [128, 32, 128])
        cmask = din("cmask", [128, 128 + 512 + 512 + 128])
        if dump:
            hdump = nc.dram_tensor("hdump", [128, 8, NCOL], F32, kind="ExternalOutput").ap()
        outT = nc.dram_tensor("outT", [128, 8, OWN], F32, kind="ExternalOutput").ap()

        A = Arena(nc, es, 212480, "arena")
        pos = [0]

        def alloc(dtype, shape, dma=False):
            esz = 4 if dtype == F32 else 2
            nb = int(np.prod(shape)) * esz
            al = 512 if dma else 64
            pos[0] = (pos[0] + al - 1) // al * al
            v = A.view(pos[0], dtype, shape)
            pos[0] += (nb + al - 1) // al * al
            return v

        hT = alloc(F32, [8, NCOL], dma=True)
        cTs = alloc(F32, [8, 2], dma=True)
        bm_l = [alloc(F32, [72], dma=True) for _ in range(2)]
        gn_l = [alloc(F32, [3, 8], dma=True) for _ in range(2)]
        gf = alloc(F32, [8], dma=True)
        ones = alloc(BF16, [128])
        sc = alloc(BF16, [8, 2])
        modraw_l = [alloc(F32, [72, 2]) for _ in range(2)]
        Amod_l = [alloc(F32, [3, 8, 2]) for _ in range(2)]
        gate_l = [alloc(F32, [3, 8, 2]) for _ in range(2)]
        epsb = alloc(F32, [2])
        NSQ = 4
        sqr = [alloc(BF16, [512], dma=True) for i in range(NSQ)]
        tmp_s = [alloc(F32, [512], dma=True) for i in range(2)]
        rstd = alloc(F32, [512], dma=True)
        PH2 = pos[0]
        nTb = alloc(BF16, [8, 1280], dma=True)
        gT_off = pos[0]
        gTb = alloc(BF16, [NJ, 1280], dma=True)
        wab_s = [alloc(BF16, [2, 8, 128], dma=True) for i in range(3)]
        wo_s = [alloc(BF16, [NJ, 128], dma=True) for i in range(2)]
        sa_s = [alloc(F32, [512], dma=True) for i in range(2)]
        NWM = 3
        wm_s = [alloc(BF16, [8, 128], dma=True) for i in range(NWM)]
        ofin_s = [alloc(F32, [512], dma=True) for i in range(2)]
        assert pos[0] <= A.nbytes, pos[0]
        pos[0] = PH2
        poolout = alloc(BF16, [4, 2432], dma=True)
        nTt = alloc(BF16, [8, 512], dma=True)
        ident = alloc(BF16, [128], dma=True)
        mbp = alloc(BF16, [512], dma=True)
        mbn = alloc(BF16, [512], dma=True)
        sinkl = alloc(BF16, [128], dma=True)
        esink = alloc(BF16, [2, 512], dma=True)
        MG = pos[0]
        kT = alloc(BF16, [2, NCOL], dma=True)
        Vaug = alloc(BF16, [20, 2, 128], dma=True)
        cst = alloc(F32, [2, 512], dma=True)
        rt = [alloc(F32, [512], dma=True) for i in range(2)]
        MQ = pos[0]
        wq = alloc(BF16, [8, 1024], dma=True)
        wou = alloc(BF16, [8, D], dma=True)
        qz = alloc(BF16, [8, 512], dma=True)
        aot = alloc(BF16, [4, 512], dma=True)
        NPT = 8
        PT = [alloc(BF16, [512], dma=True) for i in range(NPT)]
        rcp = rt[0]
        assert pos[0] <= A.nbytes, pos[0]
        pos[0] = MQ
        wkv = alloc(BF16, [8, 640], dma=True)
        wu = alloc(BF16, [8, 512], dma=True)
        wpl = alloc(BF16, [4, 128], dma=True)
        psc = alloc(F32, [4], dma=True)
        band = alloc(BF16, [32, 128], dma=True)
        utok = alloc(BF16, [20, 512], dma=True)
        pooled = alloc(BF16, [4, 512], dma=True)
        assert pos[0] <= A.nbytes, pos[0]

        ps = [es.enter_context(nc.psum_tensor("ps%d" % i, [128, 512], F32)) for i in range(8)]
        pa, pb, py, pss, pmisc = ps[0:2], ps[2:4], ps[4:6], ps[6], ps[7]

        S.op("dve", lambda e: e.memset(ones, 1.0), writes=[("ones",)])
        S.op("dve", lambda e: e.memset(epsb, EPS), writes=[("epsb",)])
        for c0 in range(0, NCOL, 512):
            S.dma("sp", None, hT[:, :, c0:c0 + 512], xT[:, :, c0:c0 + 512],
                  writes=[("h", k, b) for b in range(c0 // 128, c0 // 128 + 4) for k in range(8)])
        S.dma("sp", "d_c", cTs, cT, writes=[("cT",)])
        S.dma("sp", "d_c", gf, gfin, writes=[("gf",)])
        S.op("act", lambda e: e.activation(sc, cTs, AF.Silu), reads=[("cT",)], writes=[("sc",)])

        wm_cnt = [0]
        curp = {"par": 0}

        def mod_load(l):
            par = l % 2
            S.dma("sp", None, bm_l[par], bmod[l], writes=[("bm", par)])
            S.dma("sp", None, gn_l[par], gains[l], writes=[("gn", par)])

        def mod_chunk(l, idx):
            slot = wm_cnt[0] % NWM
            wm_cnt[0] += 1
            S.dma("pool", None, wm_s[slot], wmod[l, idx].rearrange("p (k c) -> p k c", k=8), writes=[("wm", slot)])
            S.mm(pmisc[:, 2 * idx:2 * idx + 2], [(wm_s[slot][:, k, :], sc[:, k, :]) for k in range(8)],
                 reads=[("wm", slot), ("sc",)], writes=[("pm",)])

        def mod_finish(l, m):
            par = l % 2
            pm = pmisc[:, 16 * m:16 * m + 16].rearrange("p (m t) -> p m t", t=2)
            for t in range(2):
                S.op("dve", lambda e, t=t, pm=pm: e.tensor_tensor(
                    modraw_l[par][:, 8 * m:8 * m + 8, t], pm[:, :, t], bm_l[par][:, 8 * m:8 * m + 8], ALU.add),
                    reads=[("pm",), ("bm", par)], writes=[("modraw", par, m)])

        def mod_derive(l, i, what="both"):
            par = l % 2
            modraw, Amod, gate, gn = modraw_l[par], Amod_l[par], gate_l[par], gn_l[par]
            scale = modraw[:, (3 * i + 1) * 8:(3 * i + 2) * 8, :]
            gt = modraw[:, (3 * i + 2) * 8:(3 * i + 3) * 8, :]
            if what in ("both", "A"):
                S.op("dve", lambda e: e.tensor_scalar(Amod[:, i], scale, 1.0, None, ALU.add),
                     reads=[("modraw", par, 3 * i + 1)], writes=[("Amod", par, i)])
                for t in range(2):
                    S.op("dve", lambda e, t=t: e.tensor_tensor(Amod[:, i, :, t], Amod[:, i, :, t], gn[:, i, :], ALU.mult),
                         reads=[("Amod", par, i), ("gn", par)], writes=[("Amod", par, i)])
            if what in ("both", "gate"):
                S.op("dve", lambda e: e.tensor_scalar(gate[:, i], gt, 0.5 if i != 1 else 1.0, None, ALU.mult),
                     reads=[("modraw", par, 3 * i + 2)], writes=[("gate", par, i)])

        def mod_tasks(l, ms):
            tasks = []
            for m in ms:
                for c in range(8):
                    tasks.append(lambda m=m, c=c: mod_chunk(l, 8 * m + c))
                tasks.append(lambda m=m: mod_finish(l, m))
            return tasks

        def hkeys(k, c0, n):
            return [("h", k, b) for b in range(c0 // 128, (c0 + n + 127) // 128)]

        cnt = {"tmp": 0, "sq": 0, "ab": 0, "o": 0, "p": 0, "pt": 0, "pv": 0, "st": 0}

        def rstd_steps(c0, n):
            steps = []

            def sq_step(k):
                si = cnt["sq"] % NSQ
                cnt["sq"] += 1
                S.op("act", lambda e: e.activation(sqr[si][:, 0:n], hT[:, k, c0:c0 + n], AF.Square),
                     reads=hkeys(k, c0, n), writes=[("sq", si)])
                S.mm(pss[:, 0:n], [(ones, sqr[si][:, 0:n])], reads=[("ones",), ("sq", si)],
                     writes=[("pss",)] if k in (0, 7) else [], start=(k == 0), stop=(k == 7))

            def fin():
                S.op("act", lambda e: e.activation(rstd[:, 0:n], pss[:, 0:n], AF.Ln, bias=epsb[:, 0:1], scale=1.0 / D),
                     reads=[("pss",), ("epsb",)], writes=[("rstd",)])
                S.op("act", lambda e: e.activation(rstd[:, 0:n], rstd[:, 0:n], AF.Exp, scale=-0.5),
                     reads=[("rstd",)], writes=[("rstd",)])

            for k in range(8):
                steps.append(lambda k=k: sq_step(k))
            steps.append(fin)
            return steps

        def rstd_of(c0, n):
            for st in rstd_steps(c0, n):
                st()

        def norm_steps(c0, n, i_norm, dst, dst_keys, eng="dve", split=False):
            rsteps = rstd_steps(c0, n)
            steps = []
            shift_i = 3 * i_norm
            par = curp["par"]
            modraw, Amod = modraw_l[par], Amod_l[par]

            def mod_step(k, a, m, t):
                off = a - c0
                ti = cnt["tmp"] % 2
                cnt["tmp"] += 1
                tmp = tmp_s[ti]
                S.op(eng, lambda e: e.tensor_tensor(tmp[:, 0:m], hT[:, k, a:a + m], rstd[:, off:off + m], ALU.mult),
                     reads=hkeys(k, a, m) + [("rstd",)], writes=[("tmp", ti)])
                S.op("act", lambda e: e.activation(
                    dst(k, off, m), tmp[:, 0:m], AF.Identity, bias=modraw[:, shift_i * 8 + k, t:t + 1],
                    scale=Amod[:, i_norm, k, t:t + 1]),
                    reads=[("tmp", ti), ("modraw", par, shift_i), ("Amod", par, i_norm)], writes=dst_keys(k))

            for k in range(8):
                for (a, m, t) in colset_ranges(c0, n):
                    steps.append(lambda k=k, a=a, m=m, t=t: mod_step(k, a, m, t))
            if split:
                return rsteps, steps
            return rsteps + steps

        def norm_mod(c0, n, i_norm, dst, dst_keys, eng="dve"):
            for st in norm_steps(c0, n, i_norm, dst, dst_keys, eng):
                st()

        def interleave(main, side):
            side = list(side)
            nm = max(len(main), 1)
            per = (len(side) + nm - 1) // nm
            for st in main:
                st()
                for _ in range(per):
                    if side:
                        side.pop(0)()
            while side:
                side.pop(0)()

        def resid(pst, c0, n, i_norm, d):
            par = curp["par"]
            gate = gate_l[par]
            for (a, m, t) in colset_ranges(c0, n):
                o2 = a - c0
                S.op("dve", lambda e, a=a, m=m, t=t, o2=o2: e.scalar_tensor_tensor(
                    hT[:, d, a:a + m], pst[1][:, o2:o2 + m], gate[:, i_norm, d, t:t + 1], hT[:, d, a:a + m],
                    ALU.mult, ALU.add),
                    reads=[pst[0], ("gate", par, i_norm)], writes=hkeys(d, a, m))

        def ffn_seg_norm(c0, sn, i_norm):
            steps = []
            for sti, (off, n) in enumerate(subtiles(sn)):
                steps += norm_steps(
                    c0 + off, n, i_norm,
                    lambda k, o2, m, off=off: nTb[:, k, off + o2: off + o2 + m],
                    lambda k, sti=sti: [("nT", k, sti)])
            return steps

        def ffn(l, f, i_norm, segs, extra=(), pre=(), last_seg_extra=(), skip_first_norm=False, tail_side=()):
            curp["par"] = l % 2
            for t in pre:
                t()
            extra = list(extra)
            nslots = NJ * len(segs)
            per = (len(extra) + nslots - 1) // nslots if extra else 0
            lse = list(last_seg_extra)
            per_last = (len(lse) + NJ - 1) // NJ if lse else 0

            def seg_norm(c0, sn):
                return ffn_seg_norm(c0, sn, i_norm)

            if not skip_first_norm:
                for t in seg_norm(*segs[0]):
                    t()
            for si, (c0, sn) in enumerate(segs):
                sts = subtiles(sn)
                for j in range(NJ):
                    slot = cnt["ab"] % 3
                    cnt["ab"] += 1
                    S.dma("pool", None, wab_s[slot], wab[l, f, j].rearrange("p (a k c) -> p a k c", a=2, k=8),
                          writes=[("wab", slot)])
                    for sti, (off, n) in enumerate(sts):
                        pi = cnt["p"] % 2
                        cnt["p"] += 1
                        nkeys = [("nT", k, sti) for k in range(8)]
                        S.mm(pa[pi][:, 0:n], [(wab_s[slot][:, 0, k, :], nTb[:, k, off:off + n]) for k in range(8)],
                             reads=[("wab", slot)] + nkeys, writes=[("pa", pi)])
                        S.mm(pb[pi][:, 0:n], [(wab_s[slot][:, 1, k, :], nTb[:, k, off:off + n]) for k in range(8)],
                             reads=[("wab", slot)] + nkeys, writes=[("pb", pi)])
                        S.op("act", lambda e, pi=pi, n=n: e.activation(sa_s[pi][:, 0:n], pa[pi][:, 0:n], AF.Silu),
                             reads=[("pa", pi)], writes=[("sa", pi)])
                        S.op("dve", lambda e, pi=pi, n=n, j=j, off=off: e.tensor_tensor(
                            gTb[:, j, off:off + n], sa_s[pi][:, 0:n], pb[pi][:, 0:n], ALU.mult),
                            reads=[("sa", pi), ("pb", pi)], writes=[("g", j, sti)])
                    for _ in range(per):
                        if extra:
                            extra.pop(0)()
                    if si == len(segs) - 1:
                        for _ in range(per_last):
                            if lse:
                                lse.pop(0)()
                nxt = seg_norm(*segs[si + 1]) if si + 1 < len(segs) else list(tail_side)
                main = []

                def out_step(d, sti, off, n, slot):
                    pi = cnt["p"] % 2
                    cnt["p"] += 1
                    S.mm(py[pi][:, 0:n], [(wo_s[slot][:, j, :], gTb[:, j, off:off + n]) for j in range(NJ)],
                         reads=[("wo", slot)] + [("g", j, sti) for j in range(NJ)], writes=[("py", pi)])
                    resid((("py", pi), py[pi]), c0 + off, n, i_norm, d)

                def wo_load(d, slot):
                    S.dma("pool", None, wo_s[slot], wo[l, f, d].rearrange("p (j c) -> p j c", j=NJ),
                          writes=[("wo", slot)])

                for d in range(8):
                    slot = cnt["o"] % 2
                    cnt["o"] += 1
                    for sti, (off, n) in enumerate(sts):
                        if sti == 0:
                            main.append(lambda d=d, sti=sti, off=off, n=n, slot=slot: (wo_load(d, slot), out_step(d, sti, off, n, slot)))
                        else:
                            main.append(lambda d=d, sti=sti, off=off, n=n, slot=slot: out_step(d, sti, off, n, slot))
                interleave(main, nxt)
            while extra:
                extra.pop(0)()
            while lse:
                lse.pop(0)()

        def col2blk(c):
            return c // 128

        def mix(l):
            last = (l == DEPTH - 1)
            curp["par"] = l % 2
            S.dma("pool", "d_cst", ident, cmask[:, 0:128], writes=[("ident",)])
            S.dma("pool", "d_cst", mbp, cmask[:, 128:640], writes=[("mbp",)])
            S.dma("pool", "d_cst", mbn, cmask[:, 640:1152], writes=[("mbn",)])
            S.dma("pool", "d_cst", sinkl[0:1, :], cmask[0:1, 1152:1280], writes=[("sinkl",)])
            S.dma("pool", "d_cst", wpl, wpool[l], writes=[("wpl",)])
            S.dma("sp", "d_c", psc, pscale[l], writes=[("psc",)])
            for g in range(2):
                S.dma("sp", "d_c", tmp_s[g][0:1, :], sinkrow[l, :, g, :], writes=[("tmp", g)])
                S.op("act", lambda e, g=g: e.activation(esink[0:1, g, :], tmp_s[g][0:1, :], AF.Exp),
                     reads=[("tmp", g)], writes=[("esink",)])

            S.dma("pool", None, wkv, win[l, :, :, 512:WCA], writes=[("wkv",)])
            S.dma("pool", None, wu, win[l, :, :, 0:512], writes=[("wu",)])
            S.op("dve", lambda e: e.memset(Vaug[:, :, :, 64:128], 1.0), writes=[("Vaug",)])
            acols = [(0, 512), (512, 512), (1024, 512), (1536, 512), (2048, 384) if last else (2048, 512)]
            ev_i = [0]

            def evac(dst, src, reads, writes):
                ev_i[0] += 1
                if ev_i[0] % 2:
                    S.op("act", lambda e: e.copy(dst, src), reads=reads, writes=writes)
                else:
                    S.op("dve", lambda e: e.tensor_copy(dst, src), reads=reads, writes=writes)

            nTt2 = band.rearrange("p a b -> p (a b)").rearrange("p (k c) -> p k c", k=8)
            nbufs = [nTt, nTt2]

            def nkeys(ti):
                return [("nTt", k) for k in range(8)] if ti % 2 == 0 else [("band",)]

            def a_norm(ti, c0, n):
                nb = nbufs[ti % 2]
                return norm_steps(c0, n, 1, lambda k, o2, m: nb[:, k, o2:o2 + m],
                                  (lambda k: [("nTt", k)]) if ti % 2 == 0 else (lambda k: [("band",)]), eng="pool")

            for st in a_norm(0, *acols[0]):
                st()
            for ti, (c0, n) in enumerate(acols):
                nTt_ = nbufs[ti % 2]
                nk = nkeys(ti)
                def cs_load(n=n, c0=c0):
                    S.dma("sp", None, cst[:, 0, 0:n], cosT[:, c0:c0 + n], writes=[("cs", 0)])
                    S.dma("sp", None, cst[:, 1, 0:n], sinT[:, c0:c0 + n], writes=[("cs", 1)])

                def k_step(g, n=n, c0=c0, nTt=nTt_, nk=nk):
                    S.mm(pa[g][:, 0:n], [(wkv[:, k, g * 128:(g + 1) * 128], nTt[:, k, 0:n]) for k in range(8)],
                         reads=[("wkv",)] + nk, writes=[("pa", g)])
                    S.mm(pb[g][:, 0:n], [(wkv[:, k, 256 + g * 128:256 + (g + 1) * 128], nTt[:, k, 0:n]) for k in range(8)],
                         reads=[("wkv",)] + nk, writes=[("pb", g)])
                    S.op("dve", lambda e: e.tensor_tensor(rt[0][:, 0:n], pa[g][:, 0:n], cst[:, 0, 0:n], ALU.mult),
                         reads=[("pa", g), ("cs", 0)], writes=[("rt", 0)])
                    S.op("dve", lambda e: e.tensor_tensor(rt[1][:, 0:n], pb[g][:, 0:n], cst[:, 1, 0:n], ALU.mult),
                         reads=[("pb", g), ("cs", 1)], writes=[("rt", 1)])
                    S.op("dve", lambda e: e.tensor_tensor(kT[:, g, c0:c0 + n], rt[0][:, 0:n], rt[1][:, 0:n], ALU.add),
                         reads=[("rt", 0), ("rt", 1)], writes=[("kT", g, c0)])

                def v_step(tb, c0=c0, nTt=nTt_, nk=nk):
                    blk = (c0 + tb * 128) // 128
                    pi = cnt["p"] % 2
                    cnt["p"] += 1
                    S.mm(py[pi][:, 0:128], [(nTt[:, k, tb * 128:(tb + 1) * 128], wkv[:, k, 512:640]) for k in range(8)],
                         reads=[("wkv",)] + nk, writes=[("py", pi)])
                    S.op("act", lambda e: e.copy(
                        Vaug[:, blk, :, 0:64], py[pi][:, 0:128].rearrange("p (g d) -> p g d", g=2)),
                        reads=[("py", pi)], writes=[("Vaug",)])

                def utok_step(tb, c0=c0, nTt=nTt_, nk=nk):
                    blk = c0 // 128 + tb
                    pi = cnt["p"] % 2
                    cnt["p"] += 1
                    S.mm(py[pi][:, :], [(nTt[:, k, tb * 128:(tb + 1) * 128], wu[:, k, :]) for k in range(8)],
                         reads=[("wu",)] + nk, writes=[("py", pi)])
                    S.op("dve", lambda e: e.tensor_copy(utok[:, blk, :], py[pi][:, :]), reads=[("py", pi)], writes=[("utok", blk)])

                cs_load()
                nb_ = n // 128
                main = [lambda tb=tb: utok_step(tb) for tb in range(nb_)]
                main += [lambda g=g: k_step(g) for g in range(2)]
                main += [lambda tb=tb: v_step(tb) for tb in range(nb_)]
                side = a_norm(ti + 1, *acols[ti + 1]) if ti + 1 < len(acols) else []
                interleave(main, side)
            S.dma("pool", None, band, bandT, writes=[("band",)])
            S.dma("pool", None, wq, win[l, :, :, WCA:WCOLS], writes=[("wq",), ("wkv",), ("wu",)])
            oblocks = list(range(16)) if last else list(range(19))
            for t0 in range(0, len(oblocks), 4):
                obs = oblocks[t0:t0 + 4]
                n = 128 * len(obs)
                for g in range(4):
                    pbank, pkey = [(pa[0], ("pa", 0)), (pa[1], ("pa", 1)), (pb[0], ("pb", 0)), (pb[1], ("pb", 1))][g]
                    pairs, rd = [], [("band",)]
                    for oi, ob in enumerate(obs):
                        if ob <= 16:
                            srcs = ([(ob - 1, 0)] if ob >= 1 else []) + [(ob, 3 if ob == 0 else 1), (ob + 1 if ob < 16 else 19, 2)]
                        elif ob == 17:
                            srcs = [(17, 6), (18, 5)]
                        else:
                            srcs = [(17, 4), (18, 7)]
                        for si, (sb, kind) in enumerate(srcs):
                            pairs.append((utok[:, sb, g * 128:(g + 1) * 128], band[:, g * 8 + kind, :],
                                          pbank[:, oi * 128:(oi + 1) * 128], si == 0, si == len(srcs) - 1))
                            rd.append(("utok", sb))
                    S.mm(pbank[:, 0:n], pairs, reads=rd, writes=[pkey])
                    S.op("dve", lambda e, g=g, n=n, pbank=pbank: e.tensor_copy(pooled[:, g, 0:n], pbank[:, 0:n]),
                         reads=[pkey], writes=[("pooled", g)])
                for g in range(4):
                    pj = cnt["p"] % 2
                    cnt["p"] += 1
                    S.mm(py[pj][:, 0:n], [(wpl[:, g, :], pooled[:, g, 0:n])], reads=[("wpl",), ("pooled", g)], writes=[("py", pj)])
                    oc = obs[0] * 128
                    S.op("act", lambda e, pj=pj, g=g, n=n, oc=oc: e.activation(
                        poolout[:, g, oc:oc + n], py[pj][:, 0:n], AF.Identity, scale=psc[:, g:g + 1]),
                        reads=[("py", pj), ("psc",)], writes=[("poolout", g, oc)])
            qtiles = [(0, 512), (512, 512), (1024, 512), (1536, 512)]
            if not last:
                qtiles.append((2048, 384))
            for st in norm_steps(qtiles[0][0], qtiles[0][1], 1, lambda k, o2, m: nTt[:, k, o2:o2 + m], lambda k: [("nTt", k)], eng="pool"):
                st()
            S.barrier()

            if DEBUG_MIX_STOP[0] in ("P", "KV"):
                return
            S.dma("pool", None, wou, wout[l], writes=[("wou",)])
            S.op("dve", lambda e: e.memset(qz, 0.0), writes=[("qz", h) for h in range(8)])
            pst = [pb[0], pb[1], py[0]]
            pstk = [("pb", 0), ("pb", 1), ("py", 0)]
            pvb = [py[1], pmisc]
            pvk = [("py", 1), ("pmisc",)]

            def q_norm(c0, n):
                return norm_steps(c0, n, 1, lambda k, o2, m: nTt[:, k, o2:o2 + m], lambda k: [("nTt", k)], eng="pool")

            def emit_scores(item, sel):
                (qs, g, kbs, pts) = item
                for (kb, mb, mbkey) in kbs[sel]:
                    si = cnt["st"] % 3
                    cnt["st"] += 1
                    stp, stkey = pst[si], pstk[si]
                    pairs = []
                    rd = [("qz", 4 * g + hh) for hh in range(4)]
                    for hh in range(4):
                        pairs.append((kT[:, g, kb * 128:(kb + 1) * 128], qz[:, 4 * g + hh, qs],
                                      stp[:, hh * 128:(hh + 1) * 128], True, True))
                    S.mm(stp[:, :], pairs, reads=rd, writes=[stkey])
                    pti = cnt["pt"] % NPT
                    cnt["pt"] += 1
                    S.op("act", lambda e, stp=stp, pti=pti: e.activation(PT[pti], stp[:, :], AF.Exp, scale=0.125),
                         reads=[stkey], writes=[("PT", pti)])
                    if mb is not None:
                        S.op("dve", lambda e, pti=pti, mb=mb: e.tensor_tensor(PT[pti], PT[pti], mb, ALU.mult),
                             reads=[("PT", pti), mbkey], writes=[("PT", pti)])
                    pts.append((kb, pti))

            def emit_pv(item):
                (qs, g, kbs, pts) = item
                pvi = cnt["pv"] % 2
                cnt["pv"] += 1
                pvp, pvkey = pvb[pvi], pvk[pvi]
                pairs = [(Vaug[:, kb, g, :], PT[pti]) for (kb, pti) in pts]
                pairs.append((sinkl[0:1, :], esink[0:1, g, :]))
                S.mm(pvp[:, :], pairs, reads=[("Vaug",), ("sinkl",), ("esink",)] + [("PT", pti) for (_, pti) in pts],
                     writes=[pvkey])
                return (pvp, pvkey, qs, g)

            def emit_normalise(pvinfo):
                (pvp, pvkey, qs, g) = pvinfo
                S.op("act", lambda e: e.activation(rcp[64:128, :], pvp[64:128, :], AF.Ln), reads=[pvkey], writes=[("rt", 0)])
                S.op("act", lambda e: e.activation(rcp[64:128, :], rcp[64:128, :], AF.Exp, scale=-1.0),
                     reads=[("rt", 0)], writes=[("rt", 0)])
                for hf in range(2):
                    i0v = pvp[0:64, :].rearrange("p (c t q) -> p c t q", c=2, t=2)[:, :, hf, :]
                    i1v = rcp[64:128, :].rearrange("p (c t q) -> p c t q", c=2, t=2)[:, :, hf, :]
                    ov = aot[64 * hf:64 * hf + 64, 2 * g:2 * g + 2, qs]
                    S.op("dve", lambda e, i0v=i0v, i1v=i1v, ov=ov: e.tensor_tensor(ov, i0v, i1v, ALU.mult),
                         reads=[pvkey, ("rt", 0)], writes=[("aot", 2 * g), ("aot", 2 * g + 1)])

            def wout_step(c0, n, d):
                pairs = [(wou[:, c, d * 128:(d + 1) * 128], poolout[:, c, c0:c0 + n]) for c in range(4)]
                pairs += [(wou[:, 4 + c, d * 128:(d + 1) * 128], aot[:, c, 0:n]) for c in range(4)]
                pi = d % 2
                rd = [("wou",)] + [("aot", c) for c in range(4)]
                S.mm(pa[pi][:, 0:n], pairs, reads=rd, writes=[("pa", pi)])
                resid((("pa", pi), pa[pi]), c0, n, 1, d)

            for ti, (c0, n) in enumerate(qtiles):
                S.dma("sp", None, cst[:, 0, 0:n], cosT[:, c0:c0 + n], writes=[("cs", 0)])
                S.dma("sp", None, cst[:, 1, 0:n], sinT[:, c0:c0 + n], writes=[("cs", 1)])
                nk = [("nTt", k) for k in range(8)]
                for c in range(4):
                    S.mm(pa[0][:, 0:n], [(wq[:, k, c * 128:(c + 1) * 128], nTt[:, k, 0:n]) for k in range(8)],
                         reads=[("wq",)] + nk, writes=[("pa", 0)])
                    S.mm(pa[1][:, 0:n], [(wq[:, k, 512 + c * 128:512 + (c + 1) * 128], nTt[:, k, 0:n]) for k in range(8)],
                         reads=[("wq",)] + nk, writes=[("pa", 1)])
                    S.op("dve", lambda e, n=n: e.tensor_tensor(rt[0][:, 0:n], pa[0][:, 0:n], cst[:, 0, 0:n], ALU.mult),
                         reads=[("pa", 0), ("cs", 0)], writes=[("rt", 0)])
                    S.op("dve", lambda e, n=n: e.tensor_tensor(rt[1][:, 0:n], pa[1][:, 0:n], cst[:, 1, 0:n], ALU.mult),
                         reads=[("pa", 1), ("cs", 1)], writes=[("rt", 1)])
                    for hf in range(2):
                        rows = slice(64 * hf, 64 * hf + 64)
                        S.op("dve", lambda e, c=c, n=n, hf=hf, rows=rows: e.tensor_tensor(
                            qz[rows, 2 * c + hf, 0:n], rt[0][rows, 0:n], rt[1][rows, 0:n], ALU.add),
                            reads=[("rt", 0), ("rt", 1)], writes=[("qz", 2 * c + hf)])
                items = []
                for qb in range(n // 128):
                    qblk = (c0 + qb * 128) // 128
                    qs = slice(qb * 128, (qb + 1) * 128)
                    if qblk <= 16:
                        kbs = []
                        if qblk >= 1:
                            kbs.append((qblk - 1, mbp, ("mbp",)))
                        kbs.append((qblk, None, None))
                        nxt = qblk + 1 if qblk < 16 else 19
                        kbs.append((nxt, mbn, ("mbn",)))
                        kbs += [(17, None, None), (18, None, None)]
                    else:
                        kbs = [(17, None, None), (18, None, None)]
                    for g in range(2):
                        items.append((qs, g, kbs, []))
                emit_scores(items[0], slice(0, None))
                for ii, item in enumerate(items):
                    if ii + 1 < len(items):
                        emit_scores(items[ii + 1], slice(0, 3))
                    pvinfo = emit_pv(item)
                    if ii + 1 < len(items):
                        emit_scores(items[ii + 1], slice(3, None))
                    emit_normalise(pvinfo)
                side = q_norm(*qtiles[ti + 1]) if ti + 1 < len(qtiles) else []
                for st in side[:9]:
                    st()
                interleave([lambda d=d: wout_step(c0, n, d) for d in range(8)], side[9:])
            S.barrier()

        fin_cnt = [0]

        def final_tasks(tiles):
            tasks = []
            for (c0, n) in tiles:
                tasks += rstd_steps(c0, n)

                def out_step(k, c0=c0, n=n):
                    slot = fin_cnt[0] % 2
                    fin_cnt[0] += 1
                    S.op("dve", lambda e: e.scalar_tensor_tensor(
                        ofin_s[slot][:, 0:n], hT[:, k, c0:c0 + n], gf[:, k:k + 1], rstd[:, 0:n], ALU.mult, ALU.mult),
                        reads=hkeys(k, c0, n) + [("gf",), ("rstd",)], writes=[("ofin", slot)])
                    evs.append(S.dma("sp", "d_ofin%d" % slot, outT[:, k, c0:c0 + n], ofin_s[slot][:, 0:n], reads=[("ofin", slot)]))

                tasks += [lambda k=k, f=out_step: f(k) for k in range(8)]
            return tasks

        def final():
            for t in final_tasks([(1024, 512), (1536, 512)]):
                t()

        evs = []
        def l0_first():
            mod_load(0)
            pre = mod_tasks(0, [1, 0]) + [lambda: mod_derive(0, 0, "A")]
            extra = (mod_tasks(0, [2]) + [lambda: mod_derive(0, 0, "gate")]
                     + mod_tasks(0, [4, 3, 5, 7, 6, 8]) + [lambda: mod_derive(0, 1), lambda: mod_derive(0, 2)])
            ffn(0, 0, 0, [(0, 1280), (1280, 1280)], extra=extra, pre=pre)
            S.barrier()

        def l0_ffn2():
            mod_load(1)
            extra = mod_tasks(1, list(range(9))) + [lambda i=i: mod_derive(1, i) for i in range(3)]
            curp["par"] = 1
            nxt_norm = ffn_seg_norm(0, 1280, 0)
            curp["par"] = 0
            ffn(0, 1, 2, [(0, 1280), (1280, 1152)], extra=extra, tail_side=nxt_norm)

        prog = [
            l0_first,
            lambda: mix(0),
            l0_ffn2,
            lambda: (ffn(1, 0, 0, [(0, 1280), (1280, 1152)], skip_first_norm=True), S.barrier()),
            lambda: mix(1),
            lambda: ffn(1, 1, 2, [(0, 1024), (1024, 1024)], last_seg_extra=final_tasks([(0, 512), (512, 512)])),
            lambda: final(),
        ]
        for i in range(nstage + 1):
            prog[i]()

        if dump:
            for k in range(8):
                evs.append(S.dma("sp", "d_out", hdump[:, k, :], hT[:, k, :], reads=[("h", k, b) for b in range(20)]))
        if upto != "final":
            for k in range(8):
                evs.append(S.dma("sp", "d_out", outT[:, k, :], hT[:, k, 0:OWN], reads=[("h", k, b) for b in range(16)]))
        S.wait_all("sp", evs)

        with nc.Block() as block:
            @block.tensor
            def _(e):
                for f in S.ops["pe"]:
                    f(e)

            @block.scalar
            def _(e):
                for f in S.ops["act"]:
                    f(e)

            @block.vector
            def _(e):
                for f in S.ops["dve"]:
                    f(e)

            @block.gpsimd
            def _(e):
                for f in S.ops["pool"]:
                    f(e)

            @block.sync
            def _(e):
                for f in S.ops["sp"]:
                    f(e)
        print("instr counts", S.nins, {k: v for k, v in S.cnt.items()})
    return nc


def core_columns(half):
    p = np.arange(NL)
    return p if half == 0 else (SEQ - 1 - p)


def _kmajor(w):
    return w.reshape(8, 128, w.shape[-1]).transpose(1, 0, 2)


def prep_shared(inp):
    sh = {}
    sh["wmod"] = np.ascontiguousarray(
        inp["w_mod"].reshape(DEPTH, 8, 128, 72, 128).transpose(0, 3, 2, 1, 4).reshape(DEPTH, 72, 128, 8 * 128))
    sh["bmod"] = np.ascontiguousarray(inp["b_mod"].reshape(DEPTH, 72, 128).transpose(0, 2, 1))
    g = np.stack([inp["norm_ffn1"], inp["norm_mix"], inp["norm_ffn2"]], axis=1)
    sh["gains"] = np.ascontiguousarray(g.reshape(DEPTH, 3, 8, 128).transpose(0, 3, 1, 2))
    sh["gfin"] = np.ascontiguousarray(inp["norm_final"].reshape(8, 128).T)
    wab, wo = [], []
    for name_in, name_out in (("w_ffn1_in", "w_ffn1_out"), ("w_ffn2_in", "w_ffn2_out")):
        w = inp[name_in].reshape(DEPTH, 8, 128, 2, NJ, 128)
        wab.append(w.transpose(0, 4, 2, 3, 1, 5).reshape(DEPTH, NJ, 128, 2 * 8 * 128))
        w2 = inp[name_out].reshape(DEPTH, NJ, 128, 8, 128)
        wo.append(w2.transpose(0, 3, 2, 1, 4).reshape(DEPTH, 8, 128, NJ * 128))
    sh["wab"] = np.ascontiguousarray(np.stack(wab, axis=1))
    sh["wo"] = np.ascontiguousarray(np.stack(wo, axis=1))
    wins, wouts = [], []
    for l in range(DEPTH):
        w = inp["w_in"][l]
        u, q, k, v = w[:, 0:512], w[:, 512:1024], w[:, 1024:1152], w[:, 1152:1280]
        qsw = q.reshape(D, 8, 2, 32)[:, :, ::-1, :].reshape(D, 512)
        k3 = k.reshape(D, 2, 64)
        ksw3 = k.reshape(D, 2, 2, 32)[:, :, ::-1, :].reshape(D, 2, 64)
        kdup = np.concatenate([k3, k3], axis=-1).reshape(D, 256)
        kswdup = np.concatenate([ksw3, ksw3], axis=-1).reshape(D, 256)
        cols = np.concatenate([u, kdup, kswdup, v, q, qsw], axis=1)
        wins.append(_kmajor(cols))
        wouts.append(_kmajor(inp["w_out"][l]))
    sh["win"] = np.ascontiguousarray(np.stack(wins))
    sh["wout"] = np.ascontiguousarray(np.stack(wouts))
    sh["wpool"] = np.ascontiguousarray(inp["w_pool"].transpose(0, 2, 1, 3))
    sh["pscale"] = np.ascontiguousarray(inp["pool_scale"].reshape(DEPTH, 4, 128).transpose(0, 2, 1))
    sh["sinkrow"] = np.ascontiguousarray(
        np.repeat(inp["sink"].reshape(DEPTH, 2, 4), 128, axis=2).reshape(DEPTH, 1, 2, 512))
    cm = np.zeros((128, 1280), np.float32)
    cm[:, 0:128] = np.eye(128, dtype=np.float32)
    ki = np.arange(128)[:, None]
    qi = np.arange(128)[None, :]
    cm[:, 128:640] = np.tile(np.where(qi <= ki, 1.0, 0.0), (1, 4))
    cm[:, 640:1152] = np.tile(np.where(ki <= qi, 1.0, 0.0), (1, 4))
    cm[0, 1152 + 64:1280] = 1.0
    sh["cmask"] = cm
    return sh


def _band(out_pos, src_pos, offs, lo, hi):
    P = np.asarray(out_pos)[None, :]
    Q = np.asarray(src_pos)[:, None]
    offs = np.asarray(offs)
    win = P[None, :, :] + offs[:, None, None]
    ok = (win >= lo) & (win < hi)
    cntv = ok.sum(axis=0)
    hit = ((win == Q[None, :, :]) & ok).any(axis=0)
    return hit / cntv - (Q == P)


_BAND_CACHE = {}


def _band_tables(half):
    if half in _BAND_CACHE:
        return _BAND_CACHE[half]
    _BAND_CACHE[half] = _band_tables_build(half)
    return _BAND_CACHE[half]


def _band_tables_build(half):
    T = np.zeros((128, 32, 128), np.float32)
    blk = lambda b: list(range(128 * b, 128 * b + 128))
    for g, w in enumerate((2, 4, 8, 16)):
        fwd = list(range(-(w // 2), w - w // 2))
        offs = fwd if half == 0 else [-o for o in fwd]
        big = 1 << 30
        T[:, g * 8 + 0] = _band(blk(2), blk(1), offs, 0, big)
        T[:, g * 8 + 1] = _band(blk(2), blk(2), offs, 0, big)
        T[:, g * 8 + 2] = _band(blk(2), blk(3), offs, 0, big)
        T[:, g * 8 + 3] = _band(blk(0), blk(0), offs, 0, big)
        T[:, g * 8 + 4] = _band(blk(1), blk(0), fwd, 0, 256)
        T[:, g * 8 + 5] = _band(blk(0), blk(1), fwd, 0, 256)
        T[:, g * 8 + 6] = _band(blk(0), blk(0), fwd, 0, 256)
        T[:, g * 8 + 7] = _band(blk(1), blk(1), fwd, 0, 256)
    return T


def prep_core(inp, core):
    b, half = core // 2, core % 2
    tok = core_columns(half)
    xl = inp["x"][b][tok]
    cols = np.concatenate([xl[:CTX0], inp["ctx"][b], xl[CTX0:]], axis=0)
    m = {}
    m["xT"] = np.ascontiguousarray(cols.T.reshape(8, 128, NCOL).transpose(1, 0, 2))
    cc = np.stack([inp["c"][b], inp["c_ctx"]], axis=-1)
    m["cT"] = np.ascontiguousarray(cc.reshape(8, 128, 2).transpose(1, 0, 2))
    inv = 10000.0 ** (-np.arange(0, 32, 2, dtype=np.float64) / 32.0)
    ang = np.concatenate([(tok // 64)[:, None] * inv, (tok % 64)[:, None] * inv], axis=-1)
    cos_l, sin_l = np.cos(ang).T, np.sin(ang).T
    cosc = np.ones((32, NCOL)); sinc = np.zeros((32, NCOL))
    cosc[:, :CTX0] = cos_l[:, :CTX0]; cosc[:, H2C0:] = cos_l[:, CTX0:]
    sinc[:, :CTX0] = sin_l[:, :CTX0]; sinc[:, H2C0:] = sin_l[:, CTX0:]
    m["cosT"] = np.ascontiguousarray(np.tile(cosc, (4, 1)).astype(np.float32))
    m["sinT"] = np.ascontiguousarray(np.concatenate([-sinc, sinc, -sinc, sinc], axis=0).astype(np.float32))
    m["bandT"] = _band_tables(half)
    return m


_NC_CACHE = {}


def kernel(**inputs):
    inp = {k: np.asarray(v) for k, v in inputs.items()}
    if "nc" not in _NC_CACHE:
        _NC_CACHE["nc"] = build()
    nc = _NC_CACHE["nc"]
    sh = prep_shared(inp)
    in_maps = []
    for core in range(8):
        m = dict(sh)
        m.update(prep_core(inp, core))
        in_maps.append(m)
    res = run_bass_kernel_spmd(nc, in_maps, core_ids=list(range(8)))
    out = np.empty((4, SEQ, D), np.float32)
    for core in range(8):
        b, half = core // 2, core % 2
        o = res.results[core]["outT"]
        rows = o.transpose(2, 1, 0).reshape(OWN, D)
        out[b, core_columns(half)[:OWN]] = rows
    return out
```
